# Optimizing a Trainium2 kernel written in Bass

```python
import jax, jax.numpy as jnp
from jax import lax
import numpy as np

D_MODEL = 1024
BATCH = 4
SEQ = 4096
DEPTH = 4

EPS = 1e-6
N_BRANCH = 4
BRANCH_WIDTH = D_MODEL // 2
GM_HEADS = 4
GM_CHUNK = 128
POOL_WINDOWS = (2, 4, 8, 16)
POOL_GROUP = BRANCH_WIDTH // len(POOL_WINDOWS)
ATT_HEADS = 4
ATT_HEAD_DIM = BRANCH_WIDTH // ATT_HEADS
DIL_PATTERNS = ((128, 1), (512, 4), (2048, 16))
N_DIL = len(DIL_PATTERNS)
ATT_BLOCK = 128
MEM_LEN = 256
MEM_HEADS = 4
MEM_HEAD_DIM = BRANCH_WIDTH // MEM_HEADS
NEG = -1e30

IN_SIZES = (2 * BRANCH_WIDTH, BRANCH_WIDTH,
            BRANCH_WIDTH, BRANCH_WIDTH,
            N_DIL * BRANCH_WIDTH, BRANCH_WIDTH, BRANCH_WIDTH, BRANCH_WIDTH,
            BRANCH_WIDTH, BRANCH_WIDTH,
            N_BRANCH * D_MODEL)
D_IN = sum(IN_SIZES)

kernel_name = "hybrid_gmlp_pool_dilated_attn_mem"


def rms_norm(x, g):
    xf = x.astype(jnp.float32)
    y = xf * lax.rsqrt(jnp.mean(xf * xf, axis=-1, keepdims=True) + EPS)
    return (y * g.astype(jnp.float32)).astype(x.dtype)


def layer_norm(x, g, b):
    xf = x.astype(jnp.float32)
    mu = jnp.mean(xf, axis=-1, keepdims=True)
    var = jnp.mean(jnp.square(xf - mu), axis=-1, keepdims=True)
    y = (xf - mu) * lax.rsqrt(var + EPS)
    return (y * g.astype(jnp.float32) + b.astype(jnp.float32)).astype(x.dtype)


def split_cols(h):
    idx = np.cumsum(np.array(IN_SIZES))[:-1].tolist()
    return jnp.split(h, idx, axis=-1)


def gmlp_spatial_gating(uv, ln_g, ln_b, w_s, b_s):
    u, v = jnp.split(jax.nn.gelu(uv, approximate=False), 2, axis=-1)
    v = layer_norm(v, ln_g, ln_b)
    B_, S_, W = v.shape
    nc = S_ // GM_CHUNK
    vh = v.reshape(B_, nc, GM_CHUNK, GM_HEADS, W // GM_HEADS)
    causal = jnp.tril(jnp.ones((GM_CHUNK, GM_CHUNK), dtype=bool))
    w = jnp.where(causal[None], w_s, jnp.zeros_like(w_s)).astype(v.dtype)
    mixed = jnp.einsum('hts,bcshe->bcthe', w, vh) + b_s.T.astype(v.dtype)[None, None, :, :, None]
    return u * mixed.reshape(B_, S_, W)


def multiscale_pool(p, pool_w, pool_scale):
    B_, S_, W = p.shape
    pf = p.astype(jnp.float32)
    cs = jnp.cumsum(pf, axis=1)
    count = jnp.arange(1, S_ + 1, dtype=jnp.float32)
    outs = []
    for g, win in enumerate(POOL_WINDOWS):
        c = cs[..., g * POOL_GROUP:(g + 1) * POOL_GROUP]
        prev = jnp.pad(c, ((0, 0), (win, 0), (0, 0)))[:, :S_]
        mean = (c - prev) / jnp.minimum(count, float(win))[None, :, None]
        outs.append(mean - pf[..., g * POOL_GROUP:(g + 1) * POOL_GROUP])
    d = jnp.stack(outs, axis=2)
    y = jnp.einsum('bsgi,gio->bsgo', d, pool_w.astype(jnp.float32)).reshape(B_, S_, W)
    return (y * pool_scale.astype(jnp.float32)).astype(p.dtype)


def dilated_window_attention(q, k, v, window, dilation):
    B_, S_, H, E = q.shape
    n_back = window // dilation
    L = S_ // dilation
    nb = -(-L // ATT_BLOCK)
    Lp = nb * ATT_BLOCK

    def to_blocks(t):
        t = t.reshape(B_, L, dilation, H, E).transpose(0, 2, 1, 3, 4)
        t = jnp.pad(t, ((0, 0), (0, 0), (0, Lp - L), (0, 0), (0, 0)))
        return t.reshape(B_, dilation, nb, ATT_BLOCK, H, E)

    def with_prev(t):
        prev = jnp.pad(t, ((0, 0), (0, 0), (1, 0), (0, 0), (0, 0), (0, 0)))[:, :, :nb]
        return jnp.concatenate([prev, t], axis=3)

    qb = to_blocks(q).astype(jnp.float32)
    kk = with_prev(to_blocks(k)).astype(jnp.float32)
    vv = with_prev(to_blocks(v)).astype(jnp.float32)
    s = jnp.einsum('brnqhe,brnkhe->brnhqk', qb, kk) * (E ** -0.5)
    i = jnp.arange(ATT_BLOCK)[:, None]
    j = jnp.arange(2 * ATT_BLOCK)[None, :]
    dist = ATT_BLOCK + i - j
    band = (dist >= 0) & (dist <= n_back)
    key_exists = (jnp.arange(nb)[:, None] > 0) | (jnp.arange(2 * ATT_BLOCK)[None, :] >= ATT_BLOCK)
    valid = band[None] & key_exists[:, None, :]
    s = jnp.where(valid[None, None, :, None], s, NEG)
    m = jnp.max(s, axis=-1, keepdims=True)
    e = jnp.exp(s - m)
    den = jnp.sum(e, axis=-1, keepdims=True)
    o = jnp.einsum('brnhqk,brnkhe->brnqhe', e / den, vv)
    lse = (m + jnp.log(den))[..., 0]
    o = o.reshape(B_, dilation, Lp, H, E)[:, :, :L].transpose(0, 2, 1, 3, 4).reshape(B_, S_, H, E)
    lse = lse.transpose(0, 1, 2, 4, 3).reshape(B_, dilation, Lp, H)[:, :, :L]
    lse = lse.transpose(0, 2, 1, 3).reshape(B_, S_, H)
    return o, lse


def dilated_mixture(c_q, c_k, c_v):
    B_, S_, _ = c_q.shape
    qg = c_q.reshape(B_, S_, N_DIL, ATT_HEADS, ATT_HEAD_DIM)
    k = c_k.reshape(B_, S_, ATT_HEADS, ATT_HEAD_DIM)
    v = c_v.reshape(B_, S_, ATT_HEADS, ATT_HEAD_DIM)
    outs, lses = [], []
    for g, (win, dil) in enumerate(DIL_PATTERNS):
        o, l = dilated_window_attention(qg[:, :, g], k, v, win, dil)
        outs.append(o)
        lses.append(l)
    alpha = jax.nn.softmax(jnp.stack(lses, axis=0), axis=0)
    o = jnp.sum(alpha[..., None] * jnp.stack(outs, axis=0), axis=0)
    return o.reshape(B_, S_, BRANCH_WIDTH).astype(c_q.dtype)


def memory_attention(m_q, mem_n, w_kv):
    B_, S_, _ = m_q.shape
    q = m_q.reshape(B_, S_, MEM_HEADS, MEM_HEAD_DIM).astype(jnp.float32)
    k, v = jnp.split(mem_n @ w_kv, 2, axis=-1)
    k = k.reshape(B_, -1, MEM_HEADS, MEM_HEAD_DIM).astype(jnp.float32)
    v = v.reshape(B_, -1, MEM_HEADS, MEM_HEAD_DIM).astype(jnp.float32)
    s = jnp.einsum('bshe,bmhe->bhsm', q, k) * (MEM_HEAD_DIM ** -0.5)
    p = jax.nn.softmax(s, axis=-1)
    o = jnp.einsum('bhsm,bmhe->bshe', p, v)
    return o.reshape(B_, S_, BRANCH_WIDTH).astype(m_q.dtype)


def setup_inputs(seed: int = 0) -> dict:
    key = jax.random.key(seed)
    ks = jax.random.split(key, 20)
    f32 = jnp.float32
    nrm = lambda k, shape, s: jax.random.normal(k, shape, f32) * s
    return {
        "x": nrm(ks[0], (BATCH, SEQ, D_MODEL), 1.0),
        "mem": nrm(ks[1], (BATCH, MEM_LEN, D_MODEL), 1.0),
        "norm_g": 1.0 + nrm(ks[2], (DEPTH, D_MODEL), 0.02),
        "w_in": nrm(ks[3], (DEPTH, D_MODEL, D_IN), D_MODEL ** -0.5),
        "gm_ln_g": 1.0 + nrm(ks[4], (DEPTH, BRANCH_WIDTH), 0.02),
        "gm_ln_b": nrm(ks[5], (DEPTH, BRANCH_WIDTH), 0.02),
        "gm_ws": nrm(ks[6], (DEPTH, GM_HEADS, GM_CHUNK, GM_CHUNK), GM_CHUNK ** -0.5),
        "gm_bs": 1.0 + nrm(ks[7], (DEPTH, GM_HEADS, GM_CHUNK), 0.1),
        "pool_w": nrm(ks[8], (DEPTH, len(POOL_WINDOWS), POOL_GROUP, POOL_GROUP), POOL_GROUP ** -0.5),
        "pool_scale": 1.0 + nrm(ks[9], (DEPTH, BRANCH_WIDTH), 0.1),
        "mem_norm_g": 1.0 + nrm(ks[10], (DEPTH, D_MODEL), 0.02),
        "w_mem_kv": nrm(ks[11], (DEPTH, D_MODEL, 2 * BRANCH_WIDTH), D_MODEL ** -0.5),
        "w_branch": nrm(ks[12], (DEPTH, N_BRANCH, BRANCH_WIDTH, D_MODEL), BRANCH_WIDTH ** -0.5),
        "w_out": nrm(ks[13], (DEPTH, D_MODEL, D_MODEL), 0.5 * D_MODEL ** -0.5),
        "final_norm_g": 1.0 + nrm(ks[14], (D_MODEL,), 0.02),
    }


def reference(x, mem, norm_g, w_in, gm_ln_g, gm_ln_b, gm_ws, gm_bs, pool_w, pool_scale,
              mem_norm_g, w_mem_kv, w_branch, w_out, final_norm_g):
    B_, S_, D = x.shape
    for l in range(DEPTH):
        h = rms_norm(x, norm_g[l])
        proj = h @ w_in[l]
        (a_uv, a_gate, p_in, p_gate, c_q, c_k, c_v, c_gate,
         m_q, m_gate, g_merge) = split_cols(proj)
        y_a = gmlp_spatial_gating(a_uv, gm_ln_g[l], gm_ln_b[l], gm_ws[l], gm_bs[l]) * jax.nn.silu(a_gate)
        y_p = multiscale_pool(p_in, pool_w[l], pool_scale[l]) * jax.nn.silu(p_gate)
        y_c = dilated_mixture(c_q, c_k, c_v) * jax.nn.silu(c_gate)
        mem_n = rms_norm(mem, mem_norm_g[l])
        y_m = memory_attention(m_q, mem_n, w_mem_kv[l]) * jax.nn.silu(m_gate)
        gates = jax.nn.sigmoid(g_merge.reshape(B_, S_, N_BRANCH, D))
        z = (gates[:, :, 0] * (y_a @ w_branch[l, 0])
             + gates[:, :, 1] * (y_p @ w_branch[l, 1])
             + gates[:, :, 2] * (y_c @ w_branch[l, 2])
             + gates[:, :, 3] * (y_m @ w_branch[l, 3]))
        x = x + z @ w_out[l]
    return rms_norm(x, final_norm_g)
```

```python
import numpy as np
from contextlib import ExitStack
import concourse.bass as bass
import concourse.mybir as mybir
from concourse.bass_utils import run_bass_kernel_spmd

MODE = "V4"

F32 = mybir.dt.float32
BF16 = mybir.dt.bfloat16
AF = mybir.ActivationFunctionType
ALU = mybir.AluOpType

DEPTH = 4
D = 1024
DIN = 10752
SGT = 2048
TG = 512
NTG = 4
EPS = 1e-6
OFF = dict(a_u=0, a_v=512, a_gate=1024, p_in=1536, p_gate=2048, c_q=2560, c_k=4096, c_v=4608,
           c_gate=5120, m_q=5632, m_gate=6144, g_merge=6656)
QSCALE = 128 ** -0.5
NB = 4
NDS = 40
YIDX = {0: 1, 1: 2, 2: 0, 3: 3}


class Buf:
    __slots__ = ("name", "lw", "rd")

    def __init__(self, name):
        self.name = name
        self.lw = []
        self.rd = {}


class Prog:
    ENG = ("pe", "act", "dve", "pool", "sp")

    def __init__(self, nc, ES):
        self.nc = nc
        self.ES = ES
        self.q = {n: [] for n in self.ENG}
        self.sem = {}
        self.cnt = {}
        self.key = {}
        self.waited = {n: {} for n in self.ENG}
        self.last_tok = {n: None for n in self.ENG}
        self.nkeys = 0
        self.dma_sems = []
        for i in range(NDS):
            self.dma_sems.append((self._newkey(), ES.enter_context(nc.semaphore(f"dq{i}"))))
        self.dma_val = [0] * NDS
        self.dma_rr = 0
        self.dma_recent = []
        self.n_ep = 0
        self.new_epoch()

    def _newkey(self):
        self.nkeys += 1
        return self.nkeys

    def new_epoch(self):
        for n in ("pe", "act", "dve"):
            self.sem[n] = self.ES.enter_context(self.nc.semaphore(f"e{self.n_ep}_{n}"))
            self.cnt[n] = 0
            self.key[n] = self._newkey()
        self.n_ep += 1

    def _waits(self, qn, toks, skip_self=False):
        best = {}
        for t in toks:
            if t is None:
                continue
            k, s, v = t
            if skip_self and qn in self.key and k == self.key[qn]:
                continue
            if k not in best or best[k][1] < v:
                best[k] = (s, v)
        out = []
        wd = self.waited[qn]
        for k, (s, v) in best.items():
            if wd.get(k, 0) >= v:
                continue
            wd[k] = v
            out.append((s, v))
        return out

    def _deps(self, reads, writes, after, append):
        toks = list(after)
        for b in reads:
            toks.extend(b.lw)
        for b in writes:
            if not append:
                toks.extend(b.lw)
            toks.extend(b.rd.values())
        return toks

    def _update(self, tok, reads, writes, append):
        for b in writes:
            if append:
                b.lw.append(tok)
            else:
                b.lw = [tok]
                b.rd = {}
        for b in reads:
            k = tok[0]
            if k not in b.rd or b.rd[k][2] < tok[2]:
                b.rd[k] = tok

    def op(self, qn, fn, reads=(), writes=(), after=(), append=False):
        toks = self._deps(reads, writes, after, append)
        ws = self._waits(qn, toks, skip_self=(qn == "pe"))
        self.cnt[qn] += 1
        sem = self.sem[qn]
        tok = (self.key[qn], sem, self.cnt[qn])

        def run(eng, ws=ws, fn=fn, sem=sem):
            for (s, v) in ws:
                eng.wait_ge(s, v)
            fn(eng).then_inc(sem, 1)
        self.q[qn].append(run)
        self._update(tok, reads, writes, append)
        self.last_tok[qn] = tok
        return tok

    def dma(self, qn, out_ap, in_ap, reads=(), writes=(), after=(), append=False, track=False):
        toks = self._deps(reads, writes, after, append)
        i = self.dma_rr
        self.dma_rr = (i + 1) % NDS
        k, sem = self.dma_sems[i]
        prev = self.dma_val[i]
        self.dma_val[i] += 16
        val = self.dma_val[i]
        if prev > 0:
            toks.append((k, sem, prev))
        ws = self._waits(qn, toks)
        tok = (k, sem, val)

        def run(eng, ws=ws, sem=sem, out_ap=out_ap, in_ap=in_ap):
            for (s, v) in ws:
                eng.wait_ge(s, v)
            eng.dma_start(out=out_ap, in_=in_ap).then_inc(sem, 16)
        self.q[qn].append(run)
        self._update(tok, reads, writes, append)
        if qn != "pool" or track:
            self.dma_recent.append(tok)
        return tok

    def barrier(self, full=False):
        toks = [self.last_tok[n] for n in ("pe", "act", "dve")] + self.dma_recent
        for qn in (("pe", "act", "dve", "sp") if full else ("act", "dve", "sp")):
            ws = self._waits(qn, toks, skip_self=True)
            if not ws:
                continue

            def run(eng, ws=ws):
                for (s, v) in ws:
                    eng.wait_ge(s, v)
            self.q[qn].append(run)
        self.dma_recent = []


def build_program(L, NSG, recompute_prev):
    nc = bass.Bass("TRN2", target_bir_lowering=False)
    NT = SGT * NSG

    def dram(name, shape, dt=F32, kind="ExternalInput"):
        return nc.dram_tensor(name, shape, dt, kind=kind).ap()

    NTL = NT // 256
    xT = dram("xT", [NTL, 128, 2048])
    xpT = dram("xpT", [8, 128, 2048]) if recompute_prev else None
    memT = dram("memT", [D, 256])
    w_in = dram("w_in", [L, D, DIN])
    w_kv = dram("w_kv", [L, D, 1024])
    w_br = dram("w_br", [L, 4, 512, D])
    w_out = dram("w_out", [L, D, D])
    pool_w = dram("pool_w", [L, 4, 128, 128])
    wsT = dram("wsT", [L, 4, 128, 128])
    norm_g = dram("norm_g", [128, L, 8])
    mem_g = dram("mem_g", [128, L, 8])
    ln_g = dram("ln_g", [L, 512])
    ln_b = dram("ln_b", [L, 512])
    gm_bs = dram("gm_bs", [L, 512])
    pscale = dram("pscale", [128, L, 4])
    fin_g = dram("fin_g", [128, 8])
    masks = dram("masks", [4, 128, 512])
    rcin = dram("rc", [NSG, 128, 64])
    oT = dram("oT", [NTL, 128, 2048], kind="ExternalOutput")
    if recompute_prev:
        xdst_t = dram("xo", [NTL, 128, 2048], kind="ExternalOutput")
    else:
        xdst_t = dram("xs", [NTL, 128, 2048], kind="Internal")
    kvprev = dram("kvprev", [4, 2, 128, SGT], BF16, kind="Internal")

    ES = ExitStack()
    with ES:
        def sb(name, shape, dt):
            return ES.enter_context(nc.sbuf_tensor(name, shape, dt))

        hT = sb("hT", [128, 8, SGT], BF16)
        yT = sb("yT", [128, 16, SGT], BF16)
        R2 = sb("R2", [128, 16384], BF16)
        slots = [sb(f"ws{i}", [128, 8, 512], BF16) for i in range(NB)]
        xt0 = sb("xt0", [128, 2048], F32)
        xt1 = sb("xt1", [128, 2048], F32)
        rstd = sb("rstd", [128, 512], F32)
        ones_bf = sb("ones_bf", [128, 128], BF16)
        ident_bf = sb("ident_bf", [128, 128], BF16)
        mk_b = sb("mk_b", [128, 3, 512], BF16)
        epst = sb("epst", [128, 1], F32)
        g_col = sb("g_col", [128, L, 8], F32)
        gm_col = sb("gm_col", [128, L, 8], F32)
        gf_col = sb("gf_col", [128, 8], F32)
        psc_col = sb("psc_col", [128, L, 4], F32)
        lng_b = sb("lng_b", [128, 512], F32)
        lnb_b = sb("lnb_b", [128, 512], F32)
        bs_b = sb("bs_b", [128, 512], F32)
        ws_f = sb("ws_f", [128, 4, 128], F32)
        ws_bf = sb("ws_bf", [128, 4, 128], BF16)
        poolw = sb("poolw", [128, 4, 128], BF16)
        rc_t = sb("rc_t", [128, NSG, 64], F32)
        ptail = sb("ptail", [128, 4, 16], F32)
        rstd_m = sb("rstd_m", [128, 256], F32)
        mn = sb("mn", [128, 8, 256], BF16)
        kmT = sb("kmT", [128, 4, 256], BF16)
        vm = sb("vm", [128, 2, 512], BF16)
        bnst = sb("bnst", [128, 3, 6], F32)
        bnmv = sb("bnmv", [128, 3, 2], F32)
        bnrs = sb("bnrs", [128, 3, 1], F32)
        t16 = sb("t16", [128, 16], F32)
        banks = [ES.enter_context(nc.psum_tensor(f"pb{i}", [128, 512], F32)) for i in range(8)]

        P = Prog(nc, ES)
        block = ES.enter_context(nc.Block())

        bankB = [Buf(f"bank{i}") for i in range(8)]
        slotB = [Buf(f"slot{i}") for i in range(NB)]
        hTB = [Buf(f"hT{i}") for i in range(NTG)]
        xtB = Buf("xt")
        sqB = Buf("sq")
        rstdB = Buf("rstd")
        constB = Buf("const")
        layerB = Buf("layerconst")
        memB = Buf("memkv")
        ptailB = Buf("ptail")
        kvprevB = [Buf(f"kvprev{h}") for h in range(4)]
        yB = {}

        def ybuf(b, ch, tg):
            key = (b, ch, tg)
            if key not in yB:
                yB[key] = Buf(f"y{key}")
            return yB[key]

        R1 = yT[:, 4:16, :].rearrange("p a t -> p (a t)")

        def r1(off, n):
            return R1[:, off:off + n]

        def r2(off, n):
            return R2[:, off:off + n]

        def f32v(ap):
            return ap.bitcast(F32)

        def ss(s0, d):
            return slice(s0, s0 + 127 * d + 1, d)

        wstate = {"next": 0}

        def wload(pieces):
            s = wstate["next"]
            wstate["next"] = (s + 1) % NB
            first = True
            for outf, in_ap in pieces:
                P.dma("pool", outf(slots[s]), in_ap, writes=[slotB[s]], append=not first)
                first = False
            return slots[s], slotB[s]

        def win_piece(l, c0, n, dst0):
            return (lambda sl, dst0=dst0, n=n: sl[:, :, dst0:dst0 + n],
                    w_in[l, :, c0:c0 + n].rearrange("(k p) n -> p k n", p=128))

        def mm_group(bank_i, mms, reads, append=False, extra_writes=()):
            def fn(pe, mms=mms):
                inst = None
                for (o, a, b, st, sp) in mms:
                    inst = pe.matmul(o, lhsT=a, rhs=b, start=st, stop=sp)
                return inst
            return P.op("pe", fn, reads=reads, writes=[bankB[bank_i]] + list(extra_writes), append=append)

        def proj_group(bank_i, slot_ap, slot_buf, c0, n, tg, ncols=TG, tcol0=None):
            t0 = tg * TG if tcol0 is None else tcol0
            mms = [(banks[bank_i][0:n, 0:ncols], slot_ap[:, k, c0:c0 + n], hT[:, k, t0:t0 + ncols],
                    k == 0, k == 7) for k in range(8)]
            return mm_group(bank_i, mms, reads=[slot_buf, hTB[tg]])

        P.dma("pool", mk_b[:], masks[0:3].rearrange("m p n -> p m n"), writes=[constB], track=True)
        P.dma("pool", ident_bf[:], masks[3, :, 0:128], writes=[constB], append=True, track=True)
        P.dma("sp", g_col[:], norm_g, writes=[constB], append=True)
        P.dma("sp", gm_col[:], mem_g, writes=[constB], append=True)
        P.dma("sp", gf_col[:], fin_g, writes=[constB], append=True)
        P.dma("sp", psc_col[:], pscale, writes=[constB], append=True)
        P.dma("sp", rc_t[:], rcin.rearrange("s p n -> p s n"), writes=[constB], append=True)
        P.op("dve", lambda v: v.memset(epst[:], EPS), writes=[Buf("eps")])
        P.op("dve", lambda v: v.memset(ones_bf[:], 1.0), writes=[Buf("ones")])
        P.barrier()

        memf = f32v(r2(0, 4096)).rearrange("p (k m) -> p k m", k=8)
        memsq = r2(4096, 2048).rearrange("p (k m) -> p k m", k=8)
        memfB = Buf("memf")
        P.dma("sp", memf, memT.rearrange("(k p) m -> p k m", p=128), writes=[memfB])
        P.op("act", lambda a: a.activation(out=memsq, in_=memf, func=AF.Square), reads=[memfB], writes=[sqB])
        mm_group(0, [(banks[0][:, 0:256], ones_bf[:], memsq[:, k, :], k == 0, k == 7) for k in range(8)], reads=[sqB])
        P.op("act", lambda a: a.activation(out=rstd_m[:], in_=banks[0][:, 0:256], func=AF.Sqrt, bias=epst[:], scale=1.0 / D),
             reads=[bankB[0]], writes=[memB])
        P.op("dve", lambda v: v.reciprocal(out=rstd_m[:], in_=rstd_m[:]), reads=[memB], writes=[memB])
        P.barrier()

        xtf = [xt0[:, :], xt1[:, :]]
        xth = [xt0[:, :].rearrange("p (k t) -> p k t", k=8), xt1[:, :].rearrange("p (k t) -> p k t", k=8)]
        xthB = [Buf("xth0"), Buf("xth1")]
        rsv = [rstd[:, 0:256], rstd[:, 256:512]]
        rsB = [Buf("rs0"), Buf("rs1")]
        sqBs = [Buf("sq0"), Buf("sq1")]

        def x_load256(xsrc, c0, p):
            tok512 = c0 // 512
            sub = (c0 // 256) % 2
            for half in range(2):
                src = xsrc[tok512 * 2 + half].rearrange("p (k t) -> p k t", k=4)[:, :, sub * 256:(sub + 1) * 256]
                P.dma("sp", xth[p][:, half * 4:(half + 1) * 4, :], src, writes=[xthB[p]], append=(half == 1))

        def phase_P0(l, xsrc, t0, after_tg=None):
            sqv = [yT[:, 0, :].rearrange("p (k t) -> p k t", k=8), yT[:, 1, :].rearrange("p (k t) -> p k t", k=8)]
            for s_ in range(8):
                p = s_ % 2
                c0 = t0 + s_ * 256
                cl = s_ * 256
                tg = s_ // 2
                x_load256(xsrc, c0, p)
                P.op("act", lambda a, p=p: a.activation(out=sqv[p], in_=xth[p], func=AF.Square), reads=[xthB[p]], writes=[sqBs[p]])
                mm_group(p, [(banks[p][:, 0:256], ones_bf[:], sqv[p][:, k, :], k == 0, k == 7) for k in range(8)], reads=[sqBs[p]])
                P.op("act", lambda a, p=p: a.activation(out=rsv[p], in_=banks[p][:, 0:256], func=AF.Ln, bias=epst[:], scale=1.0 / D),
                     reads=[bankB[p]], writes=[rsB[p]])
                P.op("act", lambda a, p=p: a.activation(out=rsv[p], in_=rsv[p], func=AF.Exp, scale=-0.5), reads=[rsB[p]], writes=[rsB[p]])
                for k in range(8):
                    P.op("dve", lambda v, k=k, p=p, cl=cl: v.scalar_tensor_tensor(
                        out=hT[:, k, cl:cl + 256], in0=xth[p][:, k, :], scalar=g_col[:, l, k:k + 1], in1=rsv[p],
                        op0=ALU.mult, op1=ALU.mult), reads=[xthB[p], rsB[p]], writes=[hTB[tg]], append=not (s_ % 2 == 0 and k == 0))
                if after_tg is not None and s_ % 2 == 1:
                    after_tg(tg)

        KT = r1(0, 4096)
        VT = r1(4096, 4096)
        VA = r1(8192, 8832).rearrange("p (b e) -> p b e", e=128)
        QT = r1(17024, 6144).rearrange("p (g t) -> p g t", g=3)
        accv = f32v(r2(0, 4096))
        denv = f32v(r2(4096, 4096))
        Et = [r2(8192 + 512 * i, 512) for i in range(3)]
        Pm = [r2(9728 + 512 * i, 512) for i in range(3)]
        recv = f32v(r2(11264, 1024))
        sgv = [f32v(r2(12288, 1024)), f32v(r2(13312, 1024))]
        tv = f32v(r2(14336, 1024))

        KTB = Buf("KT")
        VTB = Buf("VT")
        VAB = Buf("VA")
        QTB = [Buf(f"QT{g}") for g in range(3)]
        accB = Buf("acc")
        denB = Buf("den")
        EB = [Buf("E0"), Buf("E1"), Buf("E2")]
        PB_ = [Buf("P0"), Buf("P1"), Buf("P2")]
        recB = Buf("rec")
        tB = Buf("t")
        sgCB = [Buf("sg0"), Buf("sg1")]

        def kv_begin(l, h):
            return wload([win_piece(l, OFF["c_k"] + 128 * h, 128, 0), win_piece(l, OFF["c_v"] + 128 * h, 128, 128)])

        def kv_tg(sl, sB, tg):
            bk = (0, 1)[tg % 2]
            bv = (2, 7)[tg % 2]
            proj_group(bk, sl, sB, 0, 128, tg)
            P.op("act", lambda a: a.activation(out=KT[:, SGT + tg * TG:SGT + (tg + 1) * TG], in_=banks[bk][:, :], func=AF.Copy),
                 reads=[bankB[bk]], writes=[KTB], append=(tg > 0))
            proj_group(bv, sl, sB, 128, 128, tg)
            P.op("dve", lambda v: v.tensor_copy(out=VT[:, SGT + tg * TG:SGT + (tg + 1) * TG], in_=banks[bv][:, :]),
                 reads=[bankB[bv]], writes=[VTB], append=(tg > 0))

        def kv_project(l, h):
            sl, sB = wload([win_piece(l, OFF["c_k"] + 128 * h, 128, 0), win_piece(l, OFF["c_v"] + 128 * h, 128, 128)])
            for tg in range(NTG):
                bk = (0, 1)[tg % 2]
                bv = (2, 7)[tg % 2]
                proj_group(bk, sl, sB, 0, 128, tg)
                P.op("act", lambda a, tg=tg, bk=bk: a.activation(out=KT[:, SGT + tg * TG:SGT + (tg + 1) * TG], in_=banks[bk][:, :], func=AF.Copy),
                     reads=[bankB[bk]], writes=[KTB], append=(tg > 0))
                proj_group(bv, sl, sB, 128, 128, tg)
                P.op("dve", lambda v, tg=tg, bv=bv: v.tensor_copy(out=VT[:, SGT + tg * TG:SGT + (tg + 1) * TG], in_=banks[bv][:, :]),
                     reads=[bankB[bv]], writes=[VTB], append=(tg > 0))
            return KTB, VTB

        def kv_save(h, KTB, VTB):
            P.dma("sp", kvprev[h, 0], KT[:, SGT:2 * SGT], reads=[KTB], writes=[kvprevB[h]])
            P.dma("sp", kvprev[h, 1], VT[:, SGT:2 * SGT], reads=[VTB], writes=[kvprevB[h]], append=True)

        def kv_loadprev(h, KTB, VTB):
            P.dma("sp", KT[:, 0:SGT], kvprev[h, 0], reads=[kvprevB[h]], writes=[KTB], append=True, after=list(KTB.lw))
            P.dma("sp", VT[:, 0:SGT], kvprev[h, 1], reads=[kvprevB[h]], writes=[VTB], append=True, after=list(VTB.lw))

        DIL = (1, 4, 16)

        def blocks_list(has_prev):
            lst = []
            for d in DIL:
                nb = 16 // d
                for r in range(d):
                    for n in range(-1 if has_prev else 0, nb):
                        lst.append((d, r, n))
            return lst

        def phase_C(l, has_prev, save_kv, kv0_done=False):
            for h in range(4):
                if not (h == 0 and kv0_done):
                    kv_project(l, h)
                if save_kv:
                    kv_save(h, KTB, VTB)
                if has_prev:
                    kv_loadprev(h, KTB, VTB)
                blist = blocks_list(has_prev)
                bidx = {b: i for i, b in enumerate(blist)}
                for g0 in range(0, len(blist), 8):
                    grp = blist[g0:g0 + 8]
                    bi = (7, 0, 1, 2)[(g0 // 8) % 4]
                    pbf = banks[bi][:, :].bitcast(BF16)

                    def fn(pe, grp=grp, pbf=pbf):
                        inst = None
                        for i, (d, r, n) in enumerate(grp):
                            st = SGT + r + d * 128 * n
                            inst = pe.transpose(out=pbf[:, i * 128:(i + 1) * 128], in_=VT[:, ss(st, d)], identity=ident_bf[:])
                        return inst
                    P.op("pe", fn, reads=[VTB], writes=[bankB[bi]])
                    ng = len(grp)
                    P.op("dve", lambda v, g0=g0, ng=ng, pbf=pbf: v.tensor_copy(
                        out=VA[:, g0:g0 + ng, :], in_=pbf[:, 0:ng * 128].rearrange("p (b e) -> p b e", e=128)),
                        reads=[bankB[bi]], writes=[VAB], append=(g0 > 0))
                slA, sBA = wload([win_piece(l, OFF["c_q"] + g * 512 + 128 * h, 128, g * 128) for g in range(3)]
                                 + [win_piece(l, OFF["c_gate"] + 128 * h, 128, 384)])
                for g in range(3):
                    for tg in range(NTG):
                        bi = (7, 0, 1, 2)[tg % 4]
                        proj_group(bi, slA, sBA, g * 128, 128, tg)
                        P.op("act", lambda a, g=g, tg=tg, bi=bi: a.activation(
                            out=QT[:, g, tg * TG:(tg + 1) * TG], in_=banks[bi][:, :], func=AF.Copy, scale=QSCALE),
                            reads=[bankB[bi]], writes=[QTB[g]], append=(tg > 0))
                LAG = 2
                pending = []
                step = 0

                def rec_pv(g, d, quad, pr, info, par, obank, dbank):
                    omms = []
                    dmms = []
                    for ii, (r, n, qs) in enumerate(info):
                        oc = (pr * 2 + ii) * 128
                        hasp = not (n == 0 and not has_prev)
                        pcol = 384 if ii == 0 else 256
                        omms.append((banks[obank][:, oc:oc + 128], VA[:, bidx[(d, r, n)], :], Pm[par][:, ii * 128:(ii + 1) * 128], True, not hasp))
                        dmms.append((banks[dbank][:, oc:oc + 128], ones_bf[:], Pm[par][:, ii * 128:(ii + 1) * 128], True, not hasp))
                        if hasp:
                            omms.append((banks[obank][:, oc:oc + 128], VA[:, bidx[(d, r, n - 1)], :], Pm[par][:, pcol:pcol + 128], False, True))
                            dmms.append((banks[dbank][:, oc:oc + 128], ones_bf[:], Pm[par][:, pcol:pcol + 128], False, True))
                    mm_group(obank, omms, reads=[PB_[par], VAB], append=(pr > 0))
                    mm_group(dbank, dmms, reads=[PB_[par]], append=(pr > 0))
                    if pr == 0:
                        return
                    j0 = quad * 4
                    if d == 1:
                        av = accv[:, j0 * 128:j0 * 128 + 512]
                        dv = denv[:, j0 * 128:j0 * 128 + 512]
                        ob = banks[obank][:, :]
                        db = banks[dbank][:, :]
                    elif d == 4:
                        av = accv[:, quad:SGT:4]
                        dv = denv[:, quad:SGT:4]
                        ob = banks[obank][:, :]
                        db = banks[dbank][:, :]
                    else:
                        av = accv.rearrange("p (i r) -> p r i", r=16)[:, j0:j0 + 4, :]
                        dv = denv.rearrange("p (i r) -> p r i", r=16)[:, j0:j0 + 4, :]
                        ob = banks[obank][:, :].rearrange("p (a b) -> p a b", a=4)
                        db = banks[dbank][:, :].rearrange("p (a b) -> p a b", a=4)
                    if g == 0:
                        P.op("act", lambda a, av=av, ob=ob: a.activation(out=av, in_=ob, func=AF.Copy),
                             reads=[bankB[obank]], writes=[accB], append=(quad > 0))
                        P.op("dve", lambda v, dv=dv, db=db: v.tensor_copy(out=dv, in_=db),
                             reads=[bankB[dbank]], writes=[denB], append=(quad > 0))
                    else:
                        P.op("dve", lambda v, av=av, ob=ob: v.tensor_tensor(out=av, in0=ob, in1=av, op=ALU.add),
                             reads=[bankB[obank]], writes=[accB], append=(quad > 0))
                        P.op("dve", lambda v, dv=dv, db=db: v.tensor_tensor(out=dv, in0=db, in1=dv, op=ALU.add),
                             reads=[bankB[dbank]], writes=[denB], append=(quad > 0))

                for g, d in enumerate(DIL):
                    nb = 16 // d
                    for quad in range(4):
                        obank = 3 + (quad % 2)
                        dbank = 5 + (quad % 2)
                        for pr in range(2):
                            j = quad * 4 + pr * 2
                            info = []
                            for jj in (j, j + 1):
                                r, n = jj // nb, jj % nb
                                info.append((r, n, r + d * 128 * n))
                            cross = [n == 0 for (_, n, _) in info]
                            if cross[0] and cross[1]:
                                mki, ncols = 2, (512 if has_prev else 256)
                            elif cross[0]:
                                mki, ncols = 1, (512 if has_prev else 384)
                            else:
                                mki, ncols = 0, 512
                            sbank = step % 3
                            par = step % 3
                            step += 1
                            mms = []
                            for ii, (r, n, qs) in enumerate(info):
                                qap = QT[:, g, ss(qs, d)]
                                mms.append((banks[sbank][:, ii * 128:(ii + 1) * 128], KT[:, ss(SGT + qs, d)], qap, True, True))
                            for ii, col in ((1, 256), (0, 384)):
                                r, n, qs = info[ii]
                                if n == 0 and not has_prev:
                                    continue
                                qap = QT[:, g, ss(qs, d)]
                                ks = SGT + qs - 128 * d
                                mms.append((banks[sbank][:, col:col + 128], KT[:, ss(ks, d)], qap, True, True))
                            mm_group(sbank, mms, reads=[KTB, QTB[g]])
                            P.op("act", lambda a, sbank=sbank, par=par, ncols=ncols: a.activation(
                                out=Et[par][:, 0:ncols], in_=banks[sbank][:, 0:ncols], func=AF.Exp),
                                reads=[bankB[sbank]], writes=[EB[par]])
                            P.op("dve", lambda v, par=par, ncols=ncols, mki=mki: v.tensor_tensor(
                                out=Pm[par][:, 0:ncols], in0=Et[par][:, 0:ncols], in1=mk_b[:, mki, 0:ncols], op=ALU.mult),
                                reads=[EB[par]], writes=[PB_[par]])
                            pending.append((g, d, quad, pr, info, par, obank, dbank))
                            if len(pending) > LAG:
                                rec_pv(*pending.pop(0))
                while pending:
                    rec_pv(*pending.pop(0))
                sgB = sgCB
                for tg in range(NTG):
                    bi = (7, 0)[tg % 2]
                    proj_group(bi, slA, sBA, 384, 128, tg)
                    P.op("act", lambda a, tg=tg, bi=bi: a.activation(out=sgv[tg % 2], in_=banks[bi][:, :], func=AF.Silu),
                         reads=[bankB[bi]], writes=[sgB[tg % 2]])
                    P.op("act", lambda a, tg=tg: a.activation(out=recv, in_=denv[:, tg * TG:(tg + 1) * TG], func=AF.Ln), reads=[denB], writes=[recB])
                    P.op("act", lambda a: a.activation(out=recv, in_=recv, func=AF.Exp, scale=-1.0), reads=[recB], writes=[recB])
                    P.op("dve", lambda v, tg=tg: v.tensor_tensor(out=tv, in0=accv[:, tg * TG:(tg + 1) * TG], in1=recv, op=ALU.mult),
                         reads=[accB, recB], writes=[tB])
                    P.op("dve", lambda v, tg=tg, h=h: v.tensor_tensor(out=yT[:, 0 + h, tg * TG:(tg + 1) * TG], in0=tv, in1=sgv[tg % 2], op=ALU.mult),
                         reads=[tB, sgB[tg % 2]], writes=[ybuf(2, h, tg)] + ([sqBs[h]] if h < 2 else []))
            P.barrier()

        def phase_prev_recompute(l):
            for h in range(4):
                KTB, VTB = kv_project(l, h)
                kv_save(h, KTB, VTB)
                P.barrier()
            sl, sB = wload([win_piece(l, OFF["p_in"], 512, 0)])
            for g in range(4):
                bi = 6 + (g % 2)
                proj_group(bi, sl, sB, g * 128, 128, 3)
                P.op("act", lambda a, g=g, bi=bi: a.activation(out=ptail[:, g, :], in_=banks[bi][:, 496:512], func=AF.Copy),
                     reads=[bankB[bi]], writes=[ptailB], append=(g > 0))

        def phase_B(l, sgi, save_tail):
            pbuf = [f32v(r2(0, 4128)), f32v(r1(16384, 4128))]
            s_a = f32v(r2(4128, 4128))
            s_b = f32v(r1(0, 4128))
            dg = [r2(8256, 2048), r1(16384 + 4128, 2048)]
            sgl = [f32v(r2(10304, 1024)), f32v(r2(11328, 1024))]
            W = SGT + 16
            pbB = [Buf("pbuf0"), Buf("pbuf1")]
            saB = Buf("s_a")
            sbB = Buf("s_b")
            dgB = [Buf("dg0"), Buf("dg1")]
            sgB = [Buf("sg0"), Buf("sg1")]
            t16B = Buf("t16")
            slots_g = {}

            def front(g):
                q = g % 2
                slots_g[g] = wload([win_piece(l, OFF["p_in"] + 128 * g, 128, 0), win_piece(l, OFF["p_gate"] + 128 * g, 128, 128)])
                sl, sB = slots_g[g]
                P.op("dve", lambda v: v.tensor_copy(out=pbuf[q][:, 0:16], in_=ptail[:, g, :]), reads=[ptailB], writes=[pbB[q]])
                for tg in range(NTG):
                    bi = 6 + (tg % 2)
                    proj_group(bi, sl, sB, 0, 128, tg)
                    P.op("act", lambda a, tg=tg, bi=bi: a.activation(out=pbuf[q][:, 16 + tg * TG:16 + (tg + 1) * TG], in_=banks[bi][:, :], func=AF.Copy),
                         reads=[bankB[bi]], writes=[pbB[q]], append=True)

            def back(g):
                q = g % 2
                w = 2 ** (g + 1)
                sl, sB = slots_g[g]
                pb = pbuf[q]
                P.op("dve", lambda v: v.tensor_tensor(out=s_a[:, 1:W], in0=pb[:, 1:W], in1=pb[:, 0:W - 1], op=ALU.add),
                     reads=[pbB[q]], writes=[saB])
                S, SB_ = s_a, saB
                if g >= 1:
                    P.op("dve", lambda v: v.tensor_tensor(out=s_b[:, 3:W], in0=s_a[:, 3:W], in1=s_a[:, 1:W - 2], op=ALU.add),
                         reads=[saB], writes=[sbB])
                    S, SB_ = s_b, sbB
                if g >= 2:
                    P.op("dve", lambda v: v.tensor_tensor(out=s_a[:, 7:W], in0=s_b[:, 7:W], in1=s_b[:, 3:W - 4], op=ALU.add),
                         reads=[sbB], writes=[saB])
                    S, SB_ = s_a, saB
                if g >= 3:
                    P.op("dve", lambda v: v.tensor_tensor(out=s_b[:, 15:W], in0=s_a[:, 15:W], in1=s_a[:, 7:W - 8], op=ALU.add),
                         reads=[saB], writes=[sbB])
                    S, SB_ = s_b, sbB
                P.op("dve", lambda v, S=S: v.scalar_tensor_tensor(out=dg[q][:, :], in0=S[:, 16:W], scalar=1.0 / w, in1=pb[:, 16:W],
                                                                  op0=ALU.mult, op1=ALU.subtract), reads=[SB_, pbB[q]], writes=[dgB[q]])
                P.op("dve", lambda v, S=S: v.tensor_tensor(out=t16[:], in0=S[:, 16:32], in1=rc_t[:, sgi, g * 16:(g + 1) * 16], op=ALU.mult),
                     reads=[SB_], writes=[t16B])
                P.op("dve", lambda v: v.tensor_tensor(out=dg[q][:, 0:16], in0=t16[:], in1=pb[:, 16:32], op=ALU.subtract),
                     reads=[t16B, pbB[q]], writes=[dgB[q]], append=True)
                if save_tail:
                    P.op("dve", lambda v: v.tensor_copy(out=ptail[:, g, :], in_=pb[:, SGT:SGT + 16]), reads=[pbB[q]], writes=[ptailB])
                for tg in range(NTG):
                    bg = 2 + (tg % 2)
                    by = 4 + (tg % 2)
                    proj_group(bg, sl, sB, 128, 128, tg)
                    P.op("act", lambda a, tg=tg, bg=bg: a.activation(out=sgl[tg % 2], in_=banks[bg][:, :], func=AF.Silu),
                         reads=[bankB[bg]], writes=[sgB[tg % 2]])
                    mm_group(by, [(banks[by][:, :], poolw[:, g, :], dg[q][:, tg * TG:(tg + 1) * TG], True, True)], reads=[dgB[q], layerB])
                    P.op("dve", lambda v, tg=tg, by=by: v.scalar_tensor_tensor(
                        out=yT[:, 8 + g, tg * TG:(tg + 1) * TG], in0=banks[by][:, :], scalar=psc_col[:, l, g:g + 1], in1=sgl[tg % 2],
                        op0=ALU.mult, op1=ALU.mult), reads=[bankB[by], sgB[tg % 2]], writes=[ybuf(1, g, tg)])

            front(0)
            front(1)
            back(0)
            front(2)
            back(1)
            front(3)
            back(2)
            back(3)
            P.barrier()

        def phase_A(l):
            vg = [f32v(r2(1024 * i, 1024)) for i in range(3)]
            vt = [f32v(r2(3072 + 1024 * i, 1024)) for i in range(3)]
            vln = [r2(6144 + 512 * i, 512) for i in range(3)]
            gu = [f32v(r2(7680, 1024)), f32v(r2(8704, 1024))]
            sgl = [f32v(r2(9728, 1024)), f32v(r2(10752, 1024))]
            t1 = [f32v(r2(11776, 1024)), f32v(r2(12800, 1024))]
            slV, sBV = wload([win_piece(l, OFF["a_v"], 512, 0)])
            slU, sBU = wload([win_piece(l, OFF["a_u"], 512, 0)])
            slG, sBG = wload([win_piece(l, OFF["a_gate"], 512, 0)])
            vgB = [Buf("vg%d" % i) for i in range(3)]
            vtB = [Buf("vt%d" % i) for i in range(3)]
            vlnB = [Buf("vln%d" % i) for i in range(3)]
            stB = [Buf("st%d" % i) for i in range(3)]
            guB = [Buf("gu0"), Buf("gu1")]
            sgB = [Buf("sg0"), Buf("sg1")]
            t1B = [Buf("t10"), Buf("t11")]

            def rec_front(idx, tg, tt, p):
                c0 = tg * TG + tt * 128
                bv = 6 + (idx % 2)
                mm_group(bv, [(banks[bv][:, :], hT[:, k, c0:c0 + 128], slV[:, k, :], k == 0, k == 7) for k in range(8)],
                         reads=[sBV, hTB[tg]])
                P.op("act", lambda a: a.activation(out=vg[p], in_=banks[bv][:, :], func=AF.Gelu),
                     reads=[bankB[bv]], writes=[vgB[p]])
                P.op("dve", lambda v: v.bn_stats(out=bnst[:, p, :], in_=vg[p]), reads=[vgB[p]], writes=[stB[p]])
                P.op("dve", lambda v: v.bn_aggr(out=bnmv[:, p, :], in_=bnst[:, p, :]), reads=[stB[p]], writes=[stB[p]])
                P.op("act", lambda a: a.activation(out=bnrs[:, p, :], in_=bnmv[:, p, 1:2], func=AF.Sqrt, bias=epst[:], scale=1.0),
                     reads=[stB[p]], writes=[stB[p]])
                P.op("dve", lambda v: v.reciprocal(out=bnrs[:, p, :], in_=bnrs[:, p, :]), reads=[stB[p]], writes=[stB[p]])
                P.op("dve", lambda v: v.tensor_scalar(out=vt[p], in0=vg[p], scalar1=bnmv[:, p, 0:1], scalar2=bnrs[:, p, 0:1],
                                                      op0=ALU.subtract, op1=ALU.mult), reads=[vgB[p], stB[p]], writes=[vtB[p]])
                P.op("dve", lambda v: v.tensor_tensor(out=vt[p], in0=vt[p], in1=lng_b[:], op=ALU.mult),
                     reads=[vtB[p], layerB], writes=[vtB[p]])
                P.op("dve", lambda v: v.tensor_tensor(out=vln[p], in0=vt[p], in1=lnb_b[:], op=ALU.add),
                     reads=[vtB[p], layerB], writes=[vlnB[p]])

            def rec_back(idx, tg, tt, p):
                for h in range(4):
                    mm_group(h, [(banks[h][:, tt * 128:(tt + 1) * 128], vln[p][:, h * 128:(h + 1) * 128], ws_bf[:, h, :], True, True)],
                             reads=[vlnB[p], layerB], append=(tt > 0))
                if tt < 3:
                    return
                for h in range(4):
                    q = h % 2
                    proj_group(4, slU, sBU, h * 128, 128, tg)
                    P.op("act", lambda a, q=q: a.activation(out=gu[q], in_=banks[4][:, :], func=AF.Gelu), reads=[bankB[4]], writes=[guB[q]])
                    proj_group(5, slG, sBG, h * 128, 128, tg)
                    P.op("act", lambda a, q=q: a.activation(out=sgl[q], in_=banks[5][:, :], func=AF.Silu), reads=[bankB[5]], writes=[sgB[q]])
                    bsv = bs_b[:, h * 128:(h + 1) * 128].unsqueeze(1).to_broadcast([128, 4, 128])
                    P.op("dve", lambda v, q=q, h=h, bsv=bsv: v.tensor_tensor(
                        out=t1[q].rearrange("p (a b) -> p a b", a=4), in0=banks[h][:, :].rearrange("p (a b) -> p a b", a=4), in1=bsv, op=ALU.add),
                        reads=[bankB[h], layerB], writes=[t1B[q]])
                    P.op("dve", lambda v, q=q: v.tensor_tensor(out=t1[q], in0=t1[q], in1=gu[q], op=ALU.mult), reads=[t1B[q], guB[q]], writes=[t1B[q]])
                    P.op("dve", lambda v, q=q, h=h: v.tensor_tensor(out=yT[:, 4 + h, tg * TG:(tg + 1) * TG], in0=t1[q], in1=sgl[q], op=ALU.mult),
                         reads=[t1B[q], sgB[q]], writes=[ybuf(0, h, tg)])

            pend = []
            idx = 0
            for tg in range(NTG):
                for tt in range(4):
                    p = idx % 3
                    rec_front(idx, tg, tt, p)
                    pend.append((idx, tg, tt, p))
                    idx += 1
                    if len(pend) > 2:
                        rec_back(*pend.pop(0))
            while pend:
                rec_back(*pend.pop(0))
            P.barrier()

        def layer_setup(l):
            P.dma("sp", lng_b[:], ln_g[l:l + 1, :].partition_broadcast(128), writes=[layerB])
            P.dma("sp", lnb_b[:], ln_b[l:l + 1, :].partition_broadcast(128), writes=[layerB], append=True)
            P.dma("sp", bs_b[:], gm_bs[l:l + 1, :].partition_broadcast(128), writes=[layerB], append=True)
            P.dma("sp", ws_f[:], wsT[l].rearrange("h s t -> s h t"), writes=[layerB], append=True)
            P.dma("pool", poolw[:], pool_w[l].rearrange("g i o -> i g o"), writes=[layerB], append=True)
            for h in range(4):
                P.op("dve", lambda v, h=h: v.tensor_tensor(out=ws_bf[:, h, :], in0=ws_f[:, h, :], in1=mk_b[:, 0, 0:128], op=ALU.mult),
                     reads=[layerB], writes=[layerB], append=True)
            if not recompute_prev:
                P.op("dve", lambda v: v.memset(ptail[:], 0.0), writes=[ptailB])
            memfB2 = Buf("memf2")
            P.dma("sp", memf, memT.rearrange("(k p) m -> p k m", p=128), writes=[memfB2])
            for k in range(8):
                P.op("dve", lambda v, k=k: v.scalar_tensor_tensor(out=mn[:, k, :], in0=memf[:, k, :], scalar=gm_col[:, l, k:k + 1], in1=rstd_m[:],
                                                                   op0=ALU.mult, op1=ALU.mult), reads=[memfB2, memB], writes=[memB], append=True)
            slK, sBK = wload([(lambda sl: sl[:, :, :], w_kv[l, :, 0:512].rearrange("(k p) n -> p k n", p=128))])
            slVv, sBVv = wload([(lambda sl: sl[:, :, :], w_kv[l, :, 512:1024].rearrange("(k p) n -> p k n", p=128))])
            for h in range(4):
                bi = 6 + (h % 2)
                mm_group(bi, [(banks[bi][:, 0:256], slK[:, k, h * 128:(h + 1) * 128], mn[:, k, :], k == 0, k == 7) for k in range(8)],
                         reads=[sBK, memB])
                P.op("act", lambda a, h=h, bi=bi: a.activation(out=kmT[:, h, :], in_=banks[bi][:, 0:256], func=AF.Copy),
                     reads=[bankB[bi]], writes=[memB], append=True)
            for mb in range(2):
                bi = 4 + mb
                mm_group(bi, [(banks[bi][:, :], mn[:, k, mb * 128:(mb + 1) * 128], slVv[:, k, :], k == 0, k == 7) for k in range(8)],
                         reads=[sBVv, memB])
                P.op("act", lambda a, mb=mb, bi=bi: a.activation(out=vm[:, mb, :], in_=banks[bi][:, :], func=AF.Copy),
                     reads=[bankB[bi]], writes=[memB], append=True)
            P.barrier()

        def phase_M(l):
            qs = [r2(512 * i, 512) for i in range(3)]
            Ev = [r2(1536 + 1024 * i, 1024).rearrange("p (m t) -> p m t", m=2) for i in range(3)]
            rec = [f32v(r2(4608 + 1024 * i, 1024)) for i in range(2)]
            sgl = [f32v(r2(6656 + 1024 * i, 1024)) for i in range(3)]
            tt_ = [f32v(r2(9728 + 1024 * i, 1024)) for i in range(2)]
            qsB = [Buf("qs%d" % i) for i in range(3)]
            EvB = [Buf("E%d" % i) for i in range(3)]
            recB = [Buf("r0"), Buf("r1")]
            sgB = [Buf("s%d" % i) for i in range(3)]
            ttB = [Buf("t0"), Buf("t1")]
            slots_h = {}

            def stA(i, h, tg):
                p3 = i % 3
                if tg == 0:
                    slots_h[h] = wload([win_piece(l, OFF["m_q"] + 128 * h, 128, 0), win_piece(l, OFF["m_gate"] + 128 * h, 128, 128)])
                sl, sB = slots_h[h]
                bq = 4 + (i % 2)
                bg = 6 + (i % 2)
                proj_group(bq, sl, sB, 0, 128, tg)
                P.op("dve", lambda v: v.tensor_scalar_mul(out=qs[p3], in0=banks[bq][:, :], scalar1=QSCALE),
                     reads=[bankB[bq]], writes=[qsB[p3]])
                proj_group(bg, sl, sB, 128, 128, tg)
                P.op("act", lambda a: a.activation(out=sgl[p3], in_=banks[bg][:, :], func=AF.Silu), reads=[bankB[bg]], writes=[sgB[p3]])

            def stB_(i, h, tg):
                p3 = i % 3
                for mb in range(2):
                    mm_group(mb, [(banks[mb][:, :], kmT[:, h, mb * 128:(mb + 1) * 128], qs[p3], True, True)], reads=[qsB[p3], memB])
                    P.op("act", lambda a, mb=mb: a.activation(out=Ev[p3][:, mb, :], in_=banks[mb][:, :], func=AF.Exp),
                         reads=[bankB[mb]], writes=[EvB[p3]], append=(mb > 0))

            def stC(i, h, tg):
                p3 = i % 3
                p = i % 2
                mm_group(2, [(banks[2][:, :], vm[:, mb, h * 128:(h + 1) * 128], Ev[p3][:, mb, :], mb == 0, mb == 1) for mb in range(2)],
                         reads=[EvB[p3], memB])
                mm_group(3, [(banks[3][:, :], ones_bf[:], Ev[p3][:, mb, :], mb == 0, mb == 1) for mb in range(2)], reads=[EvB[p3]])
                P.op("act", lambda a: a.activation(out=rec[p], in_=banks[3][:, :], func=AF.Ln), reads=[bankB[3]], writes=[recB[p]])
                P.op("act", lambda a: a.activation(out=rec[p], in_=rec[p], func=AF.Exp, scale=-1.0), reads=[recB[p]], writes=[recB[p]])
                P.op("dve", lambda v: v.tensor_tensor(out=tt_[p], in0=banks[2][:, :], in1=rec[p], op=ALU.mult),
                     reads=[bankB[2], recB[p]], writes=[ttB[p]])
                P.op("dve", lambda v: v.tensor_tensor(out=yT[:, 12 + h, tg * TG:(tg + 1) * TG], in0=tt_[p], in1=sgl[p3], op=ALU.mult),
                     reads=[ttB[p], sgB[p3]], writes=[ybuf(3, h, tg)])

            its = [(i, i // 4, i % 4) for i in range(16)]
            for i in range(16 + 2):
                if i < 16:
                    stA(*its[i])
                if 0 <= i - 1 < 16:
                    stB_(*its[i - 1])
                if 0 <= i - 2 < 16:
                    stC(*its[i - 2])
            P.barrier()

        zT = R2[:, :].rearrange("p (c t) -> p c t", c=8)
        zB = {}

        def phase_G(l):
            sgt = [xt0[:, 0:512], xt0[:, 512:1024]]
            zacc = [xt0[:, 1024:1536], xt0[:, 1536:2048]]
            tmp = [xt1[:, 0:512], xt1[:, 512:1024]]
            sgtB = [Buf("sgt0"), Buf("sgt1")]
            zaB = [Buf("za0"), Buf("za1")]
            tmB = [Buf("tm0"), Buf("tm1")]
            it = 0
            for cc in range(8):
                slG, sBG = wload([win_piece(l, OFF["g_merge"] + b * 1024 + cc * 128, 128, b * 128) for b in range(4)])
                slW, sBW = wload([(lambda sl, b=b: sl[:, 0:4, b * 128:(b + 1) * 128],
                                   w_br[l, b, :, cc * 128:(cc + 1) * 128].rearrange("(k p) n -> p k n", p=128)) for b in range(4)])
                for tg in range(NTG):
                    zp = it % 2
                    it += 1
                    for b in range(4):
                        p = b % 2
                        gb = 0 + p
                        yb = 2 + p
                        proj_group(gb, slG, sBG, b * 128, 128, tg)
                        yi = YIDX[b]
                        mm_group(yb, [(banks[yb][:, :], slW[:, kc, b * 128:(b + 1) * 128], yT[:, yi * 4 + kc, tg * TG:(tg + 1) * TG], kc == 0, kc == 3)
                                      for kc in range(4)], reads=[sBW] + [ybuf(b, kc, tg) for kc in range(4)])
                        P.op("act", lambda a, p=p, gb=gb: a.activation(out=sgt[p], in_=banks[gb][:, :], func=AF.Sigmoid),
                             reads=[bankB[gb]], writes=[sgtB[p]])
                        if b == 0:
                            P.op("dve", lambda v, p=p, yb=yb, zp=zp: v.tensor_tensor(out=zacc[zp], in0=banks[yb][:, :], in1=sgt[p], op=ALU.mult),
                                 reads=[bankB[yb], sgtB[p]], writes=[zaB[zp]])
                        else:
                            P.op("dve", lambda v, p=p, yb=yb: v.tensor_tensor(out=tmp[p], in0=banks[yb][:, :], in1=sgt[p], op=ALU.mult),
                                 reads=[bankB[yb], sgtB[p]], writes=[tmB[p]])
                            if b < 3:
                                P.op("dve", lambda v, p=p, zp=zp: v.tensor_tensor(out=zacc[zp], in0=zacc[zp], in1=tmp[p], op=ALU.add),
                                     reads=[tmB[p], zaB[zp]], writes=[zaB[zp]])
                            else:
                                zB[(cc, tg)] = Buf(f"z{cc}_{tg}")
                                P.op("dve", lambda v, p=p, zp=zp, cc=cc, tg=tg: v.tensor_tensor(
                                    out=zT[:, cc, tg * TG:(tg + 1) * TG], in0=zacc[zp], in1=tmp[p], op=ALU.add),
                                    reads=[tmB[p], zaB[zp]], writes=[zB[(cc, tg)]])
            P.barrier()

        def phase_O(l, xsrc, xdst, t0):
            xq = [xt0[:, :].rearrange("p (k t) -> p k t", k=4), xt1[:, :].rearrange("p (k t) -> p k t", k=4)]
            slO = []
            for s_ in range(2):
                slO.append(wload([(lambda sl: sl[:, :, :], w_out[l, :, s_ * 512:(s_ + 1) * 512].rearrange("(k p) n -> p k n", p=128))]))
            steps = [(tg, half) for tg in range(NTG) for half in range(2)]

            def tile_of(i):
                tg, half = steps[i]
                return (t0 // 512 + tg) * 2 + half
            for i, (tg, half) in enumerate(steps):
                p = i % 2
                if i == 0:
                    P.dma("sp", xtf[0], xsrc[tile_of(0)], writes=[xthB[0]])
                if i + 1 < 8:
                    P.dma("sp", xtf[1 - p], xsrc[tile_of(i + 1)], writes=[xthB[1 - p]])
                sl, sB = slO[half]
                for kk in range(4):
                    bi = 2 + kk
                    mm_group(bi, [(banks[bi][:, :], sl[:, cc, kk * 128:(kk + 1) * 128], zT[:, cc, tg * TG:(tg + 1) * TG], cc == 0, cc == 7)
                                  for cc in range(8)], reads=[sB] + [zB[(cc, tg)] for cc in range(8)])
                    P.op("dve", lambda v, kk=kk, bi=bi, p=p: v.tensor_tensor(out=xq[p][:, kk, :], in0=banks[bi][:, :], in1=xq[p][:, kk, :], op=ALU.add),
                         reads=[bankB[bi], xthB[p]], writes=[xthB[p]], append=True)
                P.dma("sp", xdst[tile_of(i)], xtf[p], reads=[xthB[p]])
            P.barrier()

        def phase_F(xsrc):
            sqo = [hT[:, 0, :].rearrange("p (k t) -> p k t", k=8), hT[:, 1, :].rearrange("p (k t) -> p k t", k=8)]
            nst = NT // 256
            for s_ in range(nst):
                p = s_ % 2
                c0 = s_ * 256
                if s_ == 0:
                    x_load256(xsrc, 0, 0)
                if s_ + 1 < nst:
                    x_load256(xsrc, c0 + 256, 1 - p)
                P.op("act", lambda a, p=p: a.activation(out=sqo[p], in_=xth[p], func=AF.Square), reads=[xthB[p]], writes=[sqBs[p]])
                mm_group(p, [(banks[p][:, 0:256], ones_bf[:], sqo[p][:, k, :], k == 0, k == 7) for k in range(8)], reads=[sqBs[p]])
                P.op("act", lambda a, p=p: a.activation(out=rsv[p], in_=banks[p][:, 0:256], func=AF.Ln, bias=epst[:], scale=1.0 / D),
                     reads=[bankB[p]], writes=[rsB[p]])
                P.op("act", lambda a, p=p: a.activation(out=rsv[p], in_=rsv[p], func=AF.Exp, scale=-0.5), reads=[rsB[p]], writes=[rsB[p]])
                for k in range(8):
                    P.op("dve", lambda v, k=k, p=p: v.scalar_tensor_tensor(out=xth[p][:, k, :], in0=xth[p][:, k, :], scalar=gf_col[:, k:k + 1], in1=rsv[p],
                                                                            op0=ALU.mult, op1=ALU.mult), reads=[rsB[p], xthB[p]], writes=[xthB[p]], append=(k > 0))
                P.dma("sp", oT[s_], xtf[p], reads=[xthB[p]])
            P.barrier()

        for l in range(L):
            layer_setup(l)
            if recompute_prev:
                xsrc, xdst = xT, xdst_t
            else:
                xsrc = xT if l == 0 else xdst_t
                xdst = xdst_t
            for sgi in range(NSG):
                P.new_epoch()
                t0 = sgi * SGT
                if recompute_prev:
                    phase_P0(l, xpT, 0)
                    P.barrier()
                    phase_prev_recompute(l)
                    P.barrier()
                kv0 = kv_begin(l, 0)
                phase_P0(l, xsrc, t0, after_tg=lambda tg, kv0=kv0: kv_tg(kv0[0], kv0[1], tg))
                has_prev = recompute_prev or sgi > 0
                phase_C(l, has_prev, save_kv=(not recompute_prev and sgi < NSG - 1), kv0_done=True)
                phase_B(l, sgi, save_tail=(not recompute_prev and sgi < NSG - 1))
                phase_A(l)
                phase_M(l)
                phase_G(l)
                phase_O(l, xsrc, xdst, t0)
        phase_F(xdst_t)
        P.barrier(full=True)

        @block.tensor
        def _(pe):
            for f in P.q["pe"]:
                f(pe)

        @block.scalar
        def _(act):
            for f in P.q["act"]:
                f(act)

        @block.vector
        def _(dve):
            for f in P.q["dve"]:
                f(dve)

        @block.gpsimd
        def _(pool):
            for f in P.q["pool"]:
                f(pool)

        @block.sync
        def _(sp):
            for f in P.q["sp"]:
                f(sp)
    return nc


_PROG_CACHE = {}


def _get_prog(L, NSG, recompute_prev):
    key = (L, NSG, recompute_prev)
    if key not in _PROG_CACHE:
        _PROG_CACHE[key] = build_program(L, NSG, recompute_prev)
    return _PROG_CACHE[key]


def _masks(pv):
    k = np.arange(128)[:, None]
    q = np.arange(128)[None, :]
    cur = (k <= q).astype(np.float32)
    A = (k >= q).astype(np.float32)
    X = A * np.float32(pv)
    m = np.zeros((4, 128, 512), np.float32)
    m[0] = np.concatenate([cur, cur, A, A], axis=1)
    m[1] = np.concatenate([cur, cur, A, X], axis=1)
    m[2] = np.concatenate([cur, cur, X, X], axis=1)
    m[3, :, 0:128] = np.eye(128, dtype=np.float32)
    return m


def _rc(start):
    rc = np.zeros((128, 64), np.float32)
    for g, w in enumerate((2, 4, 8, 16)):
        for t in range(16):
            cnt = min(t + 1, w) if start else w
            rc[:, g * 16 + t] = 1.0 / cnt
    return rc


def _tile_x(a):
    T = a.shape[0]
    return np.ascontiguousarray(a.reshape(T // 256, 256, 8, 128).transpose(0, 3, 2, 1)).reshape(T // 256, 128, 2048)


def _tile_x_in(a):
    T = a.shape[0]
    return np.ascontiguousarray(a.reshape(T // 512, 512, 2, 4, 128).transpose(0, 2, 4, 3, 1)).reshape(T // 256, 128, 2048)


def _untile_x_in(a):
    n = a.shape[0] // 2
    return np.ascontiguousarray(a.reshape(n, 2, 128, 4, 512).transpose(0, 4, 1, 3, 2)).reshape(n * 512, 1024)


def _untile_x(a):
    n = a.shape[0]
    return np.ascontiguousarray(a.reshape(n, 128, 8, 256).transpose(0, 3, 2, 1)).reshape(n * 256, 1024)


def kernel(x, mem, norm_g, w_in, gm_ln_g, gm_ln_b, gm_ws, gm_bs, pool_w, pool_scale,
           mem_norm_g, w_mem_kv, w_branch, w_out, final_norm_g):
    f = lambda a: np.ascontiguousarray(np.asarray(a, dtype=np.float32))
    x = f(x); mem = f(mem)
    B, S, _ = x.shape
    wsT_all = np.ascontiguousarray(np.transpose(f(gm_ws), (0, 1, 3, 2)))
    gm_bs_f = f(gm_bs).reshape(DEPTH, 512)
    def colv(a, nk):
        a = f(a)
        return np.ascontiguousarray(a.reshape(a.shape[0], nk, 128).transpose(2, 0, 1))
    common = dict(w_in=f(w_in), w_kv=f(w_mem_kv), w_br=f(w_branch), w_out=f(w_out), pool_w=f(pool_w), wsT=wsT_all,
                  ln_g=f(gm_ln_g), ln_b=f(gm_ln_b), gm_bs=gm_bs_f)
    colc = dict(norm_g=colv(norm_g, 8), mem_g=colv(mem_norm_g, 8), pscale=colv(pool_scale, 4))
    fin = np.ascontiguousarray(f(final_norm_g).reshape(8, 128).T)
    out = np.empty((B, S, D), np.float32)
    if MODE == "V4":
        nc = _get_prog(DEPTH, 2, False)
        in_maps = []
        for b in range(B):
            m = dict(common)
            m.update(colc)
            m.update(xT=_tile_x_in(x[b]), memT=np.ascontiguousarray(mem[b].T), fin_g=fin,
                     masks=_masks(1.0), rc=np.stack([_rc(True), _rc(False)]))
            in_maps.append(m)
        res = run_bass_kernel_spmd(nc, in_maps, core_ids=list(range(B)))
        for b in range(B):
            out[b] = _untile_x(np.asarray(res.results[b]["oT"]))
        return out
    nc = _get_prog(1, 1, True)
    xcur = [_tile_x_in(x[b]) for b in range(B)]
    zeros_prev = np.zeros((8, 128, 2048), np.float32)
    for l in range(DEPTH):
        in_maps = []
        for c in range(8):
            b, hf = c // 2, c % 2
            m = {k: np.ascontiguousarray(v[l:l + 1]) for k, v in common.items()}
            m.update({k: np.ascontiguousarray(v[:, l:l + 1]) for k, v in colc.items()})
            m.update(xT=np.ascontiguousarray(xcur[b][hf * 8:(hf + 1) * 8]),
                     xpT=(np.ascontiguousarray(xcur[b][0:8]) if hf == 1 else zeros_prev),
                     memT=np.ascontiguousarray(mem[b].T), fin_g=fin,
                     masks=_masks(float(hf)), rc=_rc(hf == 0)[None])
            in_maps.append(m)
        res = run_bass_kernel_spmd(nc, in_maps, core_ids=list(range(8)))
        if l < DEPTH - 1:
            xcur = [np.concatenate([np.asarray(res.results[2 * b]["xo"]), np.asarray(res.results[2 * b + 1]["xo"])], axis=0) for b in range(B)]
        else:
            for b in range(B):
                out[b] = _untile_x(np.concatenate([np.asarray(res.results[2 * b]["oT"]), np.asarray(res.results[2 * b + 1]["oT"])], axis=0))
    return out
```

```python
import numpy as np
from contextlib import ExitStack
import concourse.bass as bass
import concourse.mybir as mybir
from concourse.bass_utils import run_bass_kernel_spmd

MODE = "V4"

F32 = mybir.dt.float32
BF16 = mybir.dt.bfloat16
AF = mybir.ActivationFunctionType
ALU = mybir.AluOpType

DEPTH = 4
D = 1024
DIN = 10752
SGT = 2048
TG = 512
NTG = 4
EPS = 1e-6
OFF = dict(a_u=0, a_v=512, a_gate=1024, p_in=1536, p_gate=2048, c_q=2560, c_k=4096, c_v=4608,
           c_gate=5120, m_q=5632, m_gate=6144, g_merge=6656)
QSCALE = 128 ** -0.5
NB = 4
NDS = 40
YIDX = {0: 1, 1: 2, 2: 0, 3: 3}


class Buf:
    __slots__ = ("name", "lw", "rd")

    def __init__(self, name):
        self.name = name
        self.lw = []
        self.rd = {}


class Prog:
    ENG = ("pe", "act", "dve", "pool", "sp")

    def __init__(self, nc, ES):
        self.nc = nc
        self.ES = ES
        self.q = {n: [] for n in self.ENG}
        self.sem = {}
        self.cnt = {}
        self.key = {}
        self.waited = {n: {} for n in self.ENG}
        self.last_tok = {n: None for n in self.ENG}
        self.nkeys = 0
        self.dma_sems = []
        for i in range(NDS):
            self.dma_sems.append((self._newkey(), ES.enter_context(nc.semaphore(f"dq{i}"))))
        self.dma_val = [0] * NDS
        self.dma_rr = 0
        self.dma_recent = []
        self.n_ep = 0
        self.new_epoch()

    def _newkey(self):
        self.nkeys += 1
        return self.nkeys

    def new_epoch(self):
        for n in ("pe", "act", "dve"):
            self.sem[n] = self.ES.enter_context(self.nc.semaphore(f"e{self.n_ep}_{n}"))
            self.cnt[n] = 0
            self.key[n] = self._newkey()
        self.n_ep += 1

    def _waits(self, qn, toks, skip_self=False):
        best = {}
        for t in toks:
            if t is None:
                continue
            k, s, v = t
            if skip_self and qn in self.key and k == self.key[qn]:
                continue
            if k not in best or best[k][1] < v:
                best[k] = (s, v)
        out = []
        wd = self.waited[qn]
        for k, (s, v) in best.items():
            if wd.get(k, 0) >= v:
                continue
            wd[k] = v
            out.append((s, v))
        return out

    def _deps(self, reads, writes, after, append):
        toks = list(after)
        for b in reads:
            toks.extend(b.lw)
        for b in writes:
            if not append:
                toks.extend(b.lw)
            toks.extend(b.rd.values())
        return toks

    def _update(self, tok, reads, writes, append):
        for b in writes:
            if append:
                b.lw.append(tok)
            else:
                b.lw = [tok]
                b.rd = {}
        for b in reads:
            k = tok[0]
            if k not in b.rd or b.rd[k][2] < tok[2]:
                b.rd[k] = tok

    def op(self, qn, fn, reads=(), writes=(), after=(), append=False):
        toks = self._deps(reads, writes, after, append)
        ws = self._waits(qn, toks, skip_self=(qn == "pe"))
        self.cnt[qn] += 1
        sem = self.sem[qn]
        tok = (self.key[qn], sem, self.cnt[qn])

        def run(eng, ws=ws, fn=fn, sem=sem):
            for (s, v) in ws:
                eng.wait_ge(s, v)
            fn(eng).then_inc(sem, 1)
        self.q[qn].append(run)
        self._update(tok, reads, writes, append)
        self.last_tok[qn] = tok
        return tok

    def dma(self, qn, out_ap, in_ap, reads=(), writes=(), after=(), append=False, track=False):
        toks = self._deps(reads, writes, after, append)
        i = self.dma_rr
        self.dma_rr = (i + 1) % NDS
        k, sem = self.dma_sems[i]
        prev = self.dma_val[i]
        self.dma_val[i] += 16
        val = self.dma_val[i]
        if prev > 0:
            toks.append((k, sem, prev))
        ws = self._waits(qn, toks)
        tok = (k, sem, val)

        def run(eng, ws=ws, sem=sem, out_ap=out_ap, in_ap=in_ap):
            for (s, v) in ws:
                eng.wait_ge(s, v)
            eng.dma_start(out=out_ap, in_=in_ap).then_inc(sem, 16)
        self.q[qn].append(run)
        self._update(tok, reads, writes, append)
        if qn != "pool" or track:
            self.dma_recent.append(tok)
        return tok

    def barrier(self, full=False):
        toks = [self.last_tok[n] for n in ("pe", "act", "dve")] + self.dma_recent
        for qn in (("pe", "act", "dve", "sp") if full else ("act", "dve", "sp")):
            ws = self._waits(qn, toks, skip_self=True)
            if not ws:
                continue

            def run(eng, ws=ws):
                for (s, v) in ws:
                    eng.wait_ge(s, v)
            self.q[qn].append(run)
        self.dma_recent = []


def build_program(L, NSG, recompute_prev):
    nc = bass.Bass("TRN2", target_bir_lowering=False)
    NT = SGT * NSG

    def dram(name, shape, dt=F32, kind="ExternalInput"):
        return nc.dram_tensor(name, shape, dt, kind=kind).ap()

    NTL = NT // 256
    xT = dram("xT", [NTL, 128, 2048])
    xpT = dram("xpT", [8, 128, 2048]) if recompute_prev else None
    memT = dram("memT", [D, 256])
    w_in = dram("w_in", [L, D, DIN])
    w_kv = dram("w_kv", [L, D, 1024])
    w_br = dram("w_br", [L, 4, 512, D])
    w_out = dram("w_out", [L, D, D])
    pool_w = dram("pool_w", [L, 4, 128, 128])
    wsT = dram("wsT", [L, 4, 128, 128])
    norm_g = dram("norm_g", [128, L, 8])
    mem_g = dram("mem_g", [128, L, 8])
    ln_g = dram("ln_g", [L, 512])
    ln_b = dram("ln_b", [L, 512])
    gm_bs = dram("gm_bs", [L, 512])
    pscale = dram("pscale", [128, L, 4])
    fin_g = dram("fin_g", [128, 8])
    masks = dram("masks", [4, 128, 512])
    rcin = dram("rc", [NSG, 128, 64])
    oT = dram("oT", [NTL, 128, 2048], kind="ExternalOutput")
    if recompute_prev:
        xdst_t = dram("xo", [NTL, 128, 2048], kind="ExternalOutput")
    else:
        xdst_t = dram("xs", [NTL, 128, 2048], kind="Internal")
    kvprev = dram("kvprev", [4, 2, 128, SGT], BF16, kind="Internal")

    ES = ExitStack()
    with ES:
        def sb(name, shape, dt):
            return ES.enter_context(nc.sbuf_tensor(name, shape, dt))

        hT = sb("hT", [128, 8, SGT], BF16)
        yT = sb("yT", [128, 16, SGT], BF16)
        R2 = sb("R2", [128, 16384], BF16)
        slots = [sb(f"ws{i}", [128, 8, 512], BF16) for i in range(NB)]
        xt0 = sb("xt0", [128, 2048], F32)
        xt1 = sb("xt1", [128, 2048], F32)
        rstd = sb("rstd", [128, 512], F32)
        ones_bf = sb("ones_bf", [128, 128], BF16)
        ident_bf = sb("ident_bf", [128, 128], BF16)
        mk_b = sb("mk_b", [128, 3, 512], BF16)
        epst = sb("epst", [128, 1], F32)
        g_col = sb("g_col", [128, L, 8], F32)
        gm_col = sb("gm_col", [128, L, 8], F32)
        gf_col = sb("gf_col", [128, 8], F32)
        psc_col = sb("psc_col", [128, L, 4], F32)
        lng_b = sb("lng_b", [128, 512], F32)
        lnb_b = sb("lnb_b", [128, 512], F32)
        bs_b = sb("bs_b", [128, 512], F32)
        ws_f = sb("ws_f", [128, 4, 128], F32)
        ws_bf = sb("ws_bf", [128, 4, 128], BF16)
        poolw = sb("poolw", [128, 4, 128], BF16)
        rc_t = sb("rc_t", [128, NSG, 64], F32)
        ptail = sb("ptail", [128, 4, 16], F32)
        rstd_m = sb("rstd_m", [128, 256], F32)
        mn = sb("mn", [128, 8, 256], BF16)
        kmT = sb("kmT", [128, 4, 256], BF16)
        vm = sb("vm", [128, 2, 512], BF16)
        bnst = sb("bnst", [128, 4, 6], F32)
        bnmv = sb("bnmv", [128, 4, 2], F32)
        bnrs = sb("bnrs", [128, 4, 1], F32)
        t16 = sb("t16", [128, 16], F32)
        banks = [ES.enter_context(nc.psum_tensor(f"pb{i}", [128, 512], F32)) for i in range(8)]

        P = Prog(nc, ES)
        block = ES.enter_context(nc.Block())

        bankB = [Buf(f"bank{i}") for i in range(8)]
        slotB = [Buf(f"slot{i}") for i in range(NB)]
        hTB = [Buf(f"hT{i}") for i in range(NTG)]
        xtB = Buf("xt")
        sqB = Buf("sq")
        rstdB = Buf("rstd")
        constB = Buf("const")
        layerB = Buf("layerconst")
        memB = Buf("memkv")
        ptailB = Buf("ptail")
        kvprevB = [Buf(f"kvprev{h}") for h in range(4)]
        yB = {}

        def ybuf(b, ch, tg):
            key = (b, ch, tg)
            if key not in yB:
                yB[key] = Buf(f"y{key}")
            return yB[key]

        R1 = yT[:, 4:16, :].rearrange("p a t -> p (a t)")

        def r1(off, n):
            return R1[:, off:off + n]

        def r2(off, n):
            return R2[:, off:off + n]

        def f32v(ap):
            return ap.bitcast(F32)

        def ss(s0, d):
            return slice(s0, s0 + 127 * d + 1, d)

        wstate = {"next": 0}

        def wload(pieces):
            s = wstate["next"]
            wstate["next"] = (s + 1) % NB
            first = True
            for outf, in_ap in pieces:
                P.dma("pool", outf(slots[s]), in_ap, writes=[slotB[s]], append=not first)
                first = False
            return slots[s], slotB[s]

        def win_piece(l, c0, n, dst0):
            return (lambda sl, dst0=dst0, n=n: sl[:, :, dst0:dst0 + n],
                    w_in[l, :, c0:c0 + n].rearrange("(k p) n -> p k n", p=128))

        def mm_group(bank_i, mms, reads, append=False, extra_writes=()):
            def fn(pe, mms=mms):
                inst = None
                for (o, a, b, st, sp) in mms:
                    inst = pe.matmul(o, lhsT=a, rhs=b, start=st, stop=sp)
                return inst
            return P.op("pe", fn, reads=reads, writes=[bankB[bank_i]] + list(extra_writes), append=append)

        def proj_group(bank_i, slot_ap, slot_buf, c0, n, tg, ncols=TG, tcol0=None):
            t0 = tg * TG if tcol0 is None else tcol0
            mms = [(banks[bank_i][0:n, 0:ncols], slot_ap[:, k, c0:c0 + n], hT[:, k, t0:t0 + ncols],
                    k == 0, k == 7) for k in range(8)]
            return mm_group(bank_i, mms, reads=[slot_buf, hTB[tg]])

        P.dma("pool", mk_b[:], masks[0:3].rearrange("m p n -> p m n"), writes=[constB], track=True)
        P.dma("pool", ident_bf[:], masks[3, :, 0:128], writes=[constB], append=True, track=True)
        P.dma("sp", g_col[:], norm_g, writes=[constB], append=True)
        P.dma("sp", gm_col[:], mem_g, writes=[constB], append=True)
        P.dma("sp", gf_col[:], fin_g, writes=[constB], append=True)
        P.dma("sp", psc_col[:], pscale, writes=[constB], append=True)
        P.dma("sp", rc_t[:], rcin.rearrange("s p n -> p s n"), writes=[constB], append=True)
        P.op("dve", lambda v: v.memset(epst[:], EPS), writes=[Buf("eps")])
        P.op("dve", lambda v: v.memset(ones_bf[:], 1.0), writes=[Buf("ones")])
        P.barrier()

        memf = f32v(r2(0, 4096)).rearrange("p (k m) -> p k m", k=8)
        memsq = r2(4096, 2048).rearrange("p (k m) -> p k m", k=8)
        memfB = Buf("memf")
        P.dma("sp", memf, memT.rearrange("(k p) m -> p k m", p=128), writes=[memfB])
        P.op("act", lambda a: a.activation(out=memsq, in_=memf, func=AF.Square), reads=[memfB], writes=[sqB])
        mm_group(0, [(banks[0][:, 0:256], ones_bf[:], memsq[:, k, :], k == 0, k == 7) for k in range(8)], reads=[sqB])
        P.op("act", lambda a: a.activation(out=rstd_m[:], in_=banks[0][:, 0:256], func=AF.Sqrt, bias=epst[:], scale=1.0 / D),
             reads=[bankB[0]], writes=[memB])
        P.op("dve", lambda v: v.reciprocal(out=rstd_m[:], in_=rstd_m[:]), reads=[memB], writes=[memB])
        P.barrier()

        xtf = [xt0[:, :], xt1[:, :]]
        xth = [xt0[:, :].rearrange("p (k t) -> p k t", k=8), xt1[:, :].rearrange("p (k t) -> p k t", k=8)]
        xthB = [Buf("xth0"), Buf("xth1")]
        rsv = [rstd[:, 0:256], rstd[:, 256:512]]
        rsB = [Buf("rs0"), Buf("rs1")]
        sqBs = [Buf("sq0"), Buf("sq1")]

        def x_load256(xsrc, c0, p):
            tok512 = c0 // 512
            sub = (c0 // 256) % 2
            for half in range(2):
                src = xsrc[tok512 * 2 + half].rearrange("p (k t) -> p k t", k=4)[:, :, sub * 256:(sub + 1) * 256]
                P.dma("sp", xth[p][:, half * 4:(half + 1) * 4, :], src, writes=[xthB[p]], append=(half == 1))

        def phase_P0(l, xsrc, t0, after_tg=None):
            sqv = [yT[:, 0, :].rearrange("p (k t) -> p k t", k=8), yT[:, 1, :].rearrange("p (k t) -> p k t", k=8)]
            for s_ in range(8):
                p = s_ % 2
                c0 = t0 + s_ * 256
                cl = s_ * 256
                tg = s_ // 2
                x_load256(xsrc, c0, p)
                P.op("act", lambda a, p=p: a.activation(out=sqv[p], in_=xth[p], func=AF.Square), reads=[xthB[p]], writes=[sqBs[p]])
                mm_group(p, [(banks[p][:, 0:256], ones_bf[:], sqv[p][:, k, :], k == 0, k == 7) for k in range(8)], reads=[sqBs[p]])
                P.op("act", lambda a, p=p: a.activation(out=rsv[p], in_=banks[p][:, 0:256], func=AF.Ln, bias=epst[:], scale=1.0 / D),
                     reads=[bankB[p]], writes=[rsB[p]])
                P.op("act", lambda a, p=p: a.activation(out=rsv[p], in_=rsv[p], func=AF.Exp, scale=-0.5), reads=[rsB[p]], writes=[rsB[p]])
                for k in range(8):
                    P.op("dve", lambda v, k=k, p=p, cl=cl: v.scalar_tensor_tensor(
                        out=hT[:, k, cl:cl + 256], in0=xth[p][:, k, :], scalar=g_col[:, l, k:k + 1], in1=rsv[p],
                        op0=ALU.mult, op1=ALU.mult), reads=[xthB[p], rsB[p]], writes=[hTB[tg]], append=not (s_ % 2 == 0 and k == 0))
                if after_tg is not None and s_ % 2 == 1:
                    after_tg(tg)

        KT = r1(0, 4096)
        VT = r1(4096, 4096)
        VA = r1(8192, 8832).rearrange("p (b e) -> p b e", e=128)
        QT = r1(17024, 6144).rearrange("p (g t) -> p g t", g=3)
        accv = f32v(r2(0, 4096))
        denv = f32v(r2(4096, 4096))
        Et = [r2(8192 + 512 * i, 512) for i in range(4)]
        Pm = [r2(10240 + 512 * i, 512) for i in range(4)]
        recv = f32v(r2(12288, 1024))
        sgv = [f32v(r2(13312, 1024)), f32v(r2(14336, 1024))]
        tv = f32v(r2(15360, 1024))

        KTB = Buf("KT")
        VTB = Buf("VT")
        VAB = Buf("VA")
        QTB = [Buf(f"QT{g}") for g in range(3)]
        accB = Buf("acc")
        denB = Buf("den")
        EB = [Buf("E0"), Buf("E1"), Buf("E2"), Buf("E3")]
        PB_ = [Buf("P0"), Buf("P1"), Buf("P2"), Buf("P3")]
        recB = Buf("rec")
        tB = Buf("t")
        sgCB = [Buf("sg0"), Buf("sg1")]

        def kv_begin(l, h):
            return wload([win_piece(l, OFF["c_k"] + 128 * h, 128, 0), win_piece(l, OFF["c_v"] + 128 * h, 128, 128)])

        def kv_tg(sl, sB, tg):
            bk = (0, 1)[tg % 2]
            bv = (2, 7)[tg % 2]
            proj_group(bk, sl, sB, 0, 128, tg)
            P.op("act", lambda a: a.activation(out=KT[:, SGT + tg * TG:SGT + (tg + 1) * TG], in_=banks[bk][:, :], func=AF.Copy),
                 reads=[bankB[bk]], writes=[KTB], append=(tg > 0))
            proj_group(bv, sl, sB, 128, 128, tg)
            P.op("dve", lambda v: v.tensor_copy(out=VT[:, SGT + tg * TG:SGT + (tg + 1) * TG], in_=banks[bv][:, :]),
                 reads=[bankB[bv]], writes=[VTB], append=(tg > 0))

        def kv_project(l, h):
            sl, sB = wload([win_piece(l, OFF["c_k"] + 128 * h, 128, 0), win_piece(l, OFF["c_v"] + 128 * h, 128, 128)])
            for tg in range(NTG):
                bk = (0, 1)[tg % 2]
                bv = (2, 7)[tg % 2]
                proj_group(bk, sl, sB, 0, 128, tg)
                P.op("act", lambda a, tg=tg, bk=bk: a.activation(out=KT[:, SGT + tg * TG:SGT + (tg + 1) * TG], in_=banks[bk][:, :], func=AF.Copy),
                     reads=[bankB[bk]], writes=[KTB], append=(tg > 0))
                proj_group(bv, sl, sB, 128, 128, tg)
                P.op("dve", lambda v, tg=tg, bv=bv: v.tensor_copy(out=VT[:, SGT + tg * TG:SGT + (tg + 1) * TG], in_=banks[bv][:, :]),
                     reads=[bankB[bv]], writes=[VTB], append=(tg > 0))
            return KTB, VTB

        def kv_save(h, KTB, VTB):
            P.dma("sp", kvprev[h, 0], KT[:, SGT:2 * SGT], reads=[KTB], writes=[kvprevB[h]])
            P.dma("sp", kvprev[h, 1], VT[:, SGT:2 * SGT], reads=[VTB], writes=[kvprevB[h]], append=True)

        def kv_loadprev(h, KTB, VTB):
            P.dma("sp", KT[:, 0:SGT], kvprev[h, 0], reads=[kvprevB[h]], writes=[KTB], append=True, after=list(KTB.lw))
            P.dma("sp", VT[:, 0:SGT], kvprev[h, 1], reads=[kvprevB[h]], writes=[VTB], append=True, after=list(VTB.lw))

        DIL = (1, 4, 16)

        def blocks_list(has_prev):
            lst = []
            for d in DIL:
                nb = 16 // d
                for r in range(d):
                    for n in range(-1 if has_prev else 0, nb):
                        lst.append((d, r, n))
            return lst

        def phase_C(l, has_prev, save_kv, kv0_done=False):
            for h in range(4):
                if not (h == 0 and kv0_done):
                    kv_project(l, h)
                if save_kv:
                    kv_save(h, KTB, VTB)
                if has_prev:
                    kv_loadprev(h, KTB, VTB)
                blist = blocks_list(has_prev)
                bidx = {b: i for i, b in enumerate(blist)}
                for g0 in range(0, len(blist), 8):
                    grp = blist[g0:g0 + 8]
                    bi = (7, 0, 1, 2)[(g0 // 8) % 4]
                    pbf = banks[bi][:, :].bitcast(BF16)

                    def fn(pe, grp=grp, pbf=pbf):
                        inst = None
                        for i, (d, r, n) in enumerate(grp):
                            st = SGT + r + d * 128 * n
                            inst = pe.transpose(out=pbf[:, i * 128:(i + 1) * 128], in_=VT[:, ss(st, d)], identity=ident_bf[:])
                        return inst
                    P.op("pe", fn, reads=[VTB], writes=[bankB[bi]])
                    ng = len(grp)
                    P.op("dve", lambda v, g0=g0, ng=ng, pbf=pbf: v.tensor_copy(
                        out=VA[:, g0:g0 + ng, :], in_=pbf[:, 0:ng * 128].rearrange("p (b e) -> p b e", e=128)),
                        reads=[bankB[bi]], writes=[VAB], append=(g0 > 0))
                slA, sBA = wload([win_piece(l, OFF["c_q"] + g * 512 + 128 * h, 128, g * 128) for g in range(3)]
                                 + [win_piece(l, OFF["c_gate"] + 128 * h, 128, 384)])
                for g in range(3):
                    for tg in range(NTG):
                        bi = (7, 0, 1, 2)[tg % 4]
                        proj_group(bi, slA, sBA, g * 128, 128, tg)
                        P.op("act", lambda a, g=g, tg=tg, bi=bi: a.activation(
                            out=QT[:, g, tg * TG:(tg + 1) * TG], in_=banks[bi][:, :], func=AF.Copy, scale=QSCALE),
                            reads=[bankB[bi]], writes=[QTB[g]], append=(tg > 0))
                LAG = 3
                pending = []
                step = 0

                def rec_pv(g, d, quad, pr, info, par, obank, dbank):
                    omms = []
                    dmms = []
                    for ii, (r, n, qs) in enumerate(info):
                        oc = (pr * 2 + ii) * 128
                        hasp = not (n == 0 and not has_prev)
                        pcol = 384 if ii == 0 else 256
                        omms.append((banks[obank][:, oc:oc + 128], VA[:, bidx[(d, r, n)], :], Pm[par][:, ii * 128:(ii + 1) * 128], True, not hasp))
                        dmms.append((banks[dbank][:, oc:oc + 128], ones_bf[:], Pm[par][:, ii * 128:(ii + 1) * 128], True, not hasp))
                        if hasp:
                            omms.append((banks[obank][:, oc:oc + 128], VA[:, bidx[(d, r, n - 1)], :], Pm[par][:, pcol:pcol + 128], False, True))
                            dmms.append((banks[dbank][:, oc:oc + 128], ones_bf[:], Pm[par][:, pcol:pcol + 128], False, True))
                    mm_group(obank, omms, reads=[PB_[par], VAB], append=(pr > 0))
                    mm_group(dbank, dmms, reads=[PB_[par]], append=(pr > 0))
                    if pr == 0:
                        return
                    j0 = quad * 4
                    if d == 1:
                        av = accv[:, j0 * 128:j0 * 128 + 512]
                        dv = denv[:, j0 * 128:j0 * 128 + 512]
                        ob = banks[obank][:, :]
                        db = banks[dbank][:, :]
                    elif d == 4:
                        av = accv[:, quad:SGT:4]
                        dv = denv[:, quad:SGT:4]
                        ob = banks[obank][:, :]
                        db = banks[dbank][:, :]
                    else:
                        av = accv.rearrange("p (i r) -> p r i", r=16)[:, j0:j0 + 4, :]
                        dv = denv.rearrange("p (i r) -> p r i", r=16)[:, j0:j0 + 4, :]
                        ob = banks[obank][:, :].rearrange("p (a b) -> p a b", a=4)
                        db = banks[dbank][:, :].rearrange("p (a b) -> p a b", a=4)
                    if g == 0:
                        P.op("act", lambda a, av=av, ob=ob: a.activation(out=av, in_=ob, func=AF.Copy),
                             reads=[bankB[obank]], writes=[accB], append=(quad > 0))
                        P.op("dve", lambda v, dv=dv, db=db: v.tensor_copy(out=dv, in_=db),
                             reads=[bankB[dbank]], writes=[denB], append=(quad > 0))
                    else:
                        P.op("dve", lambda v, av=av, ob=ob: v.tensor_tensor(out=av, in0=ob, in1=av, op=ALU.add),
                             reads=[bankB[obank]], writes=[accB], append=(quad > 0))
                        P.op("dve", lambda v, dv=dv, db=db: v.tensor_tensor(out=dv, in0=db, in1=dv, op=ALU.add),
                             reads=[bankB[dbank]], writes=[denB], append=(quad > 0))

                for g, d in enumerate(DIL):
                    nb = 16 // d
                    for quad in range(4):
                        obank = 3 + (quad % 2)
                        dbank = 5 + (quad % 2)
                        for pr in range(2):
                            j = quad * 4 + pr * 2
                            info = []
                            for jj in (j, j + 1):
                                r, n = jj // nb, jj % nb
                                info.append((r, n, r + d * 128 * n))
                            cross = [n == 0 for (_, n, _) in info]
                            if cross[0] and cross[1]:
                                mki, ncols = 2, (512 if has_prev else 256)
                            elif cross[0]:
                                mki, ncols = 1, (512 if has_prev else 384)
                            else:
                                mki, ncols = 0, 512
                            sbank = (0, 1, 2, 7)[step % 4]
                            par = step % 4
                            step += 1
                            mms = []
                            for ii, (r, n, qs) in enumerate(info):
                                qap = QT[:, g, ss(qs, d)]
                                mms.append((banks[sbank][:, ii * 128:(ii + 1) * 128], KT[:, ss(SGT + qs, d)], qap, True, True))
                            for ii, col in ((1, 256), (0, 384)):
                                r, n, qs = info[ii]
                                if n == 0 and not has_prev:
                                    continue
                                qap = QT[:, g, ss(qs, d)]
                                ks = SGT + qs - 128 * d
                                mms.append((banks[sbank][:, col:col + 128], KT[:, ss(ks, d)], qap, True, True))
                            mm_group(sbank, mms, reads=[KTB, QTB[g]])
                            P.op("act", lambda a, sbank=sbank, par=par, ncols=ncols: a.activation(
                                out=Et[par][:, 0:ncols], in_=banks[sbank][:, 0:ncols], func=AF.Exp),
                                reads=[bankB[sbank]], writes=[EB[par]])
                            P.op("dve", lambda v, par=par, ncols=ncols, mki=mki: v.tensor_tensor(
                                out=Pm[par][:, 0:ncols], in0=Et[par][:, 0:ncols], in1=mk_b[:, mki, 0:ncols], op=ALU.mult),
                                reads=[EB[par]], writes=[PB_[par]])
                            pending.append((g, d, quad, pr, info, par, obank, dbank))
                            if len(pending) > LAG:
                                rec_pv(*pending.pop(0))
                while pending:
                    rec_pv(*pending.pop(0))
                sgB = sgCB
                for tg in range(NTG):
                    bi = (7, 0)[tg % 2]
                    proj_group(bi, slA, sBA, 384, 128, tg)
                    P.op("act", lambda a, tg=tg, bi=bi: a.activation(out=sgv[tg % 2], in_=banks[bi][:, :], func=AF.Silu),
                         reads=[bankB[bi]], writes=[sgB[tg % 2]])
                    P.op("act", lambda a, tg=tg: a.activation(out=recv, in_=denv[:, tg * TG:(tg + 1) * TG], func=AF.Ln), reads=[denB], writes=[recB])
                    P.op("act", lambda a: a.activation(out=recv, in_=recv, func=AF.Exp, scale=-1.0), reads=[recB], writes=[recB])
                    P.op("dve", lambda v, tg=tg: v.tensor_tensor(out=tv, in0=accv[:, tg * TG:(tg + 1) * TG], in1=recv, op=ALU.mult),
                         reads=[accB, recB], writes=[tB])
                    P.op("dve", lambda v, tg=tg, h=h: v.tensor_tensor(out=yT[:, 0 + h, tg * TG:(tg + 1) * TG], in0=tv, in1=sgv[tg % 2], op=ALU.mult),
                         reads=[tB, sgB[tg % 2]], writes=[ybuf(2, h, tg)] + ([sqBs[h]] if h < 2 else []))
            P.barrier()

        def phase_prev_recompute(l):
            for h in range(4):
                KTB, VTB = kv_project(l, h)
                kv_save(h, KTB, VTB)
                P.barrier()
            sl, sB = wload([win_piece(l, OFF["p_in"], 512, 0)])
            for g in range(4):
                bi = 6 + (g % 2)
                proj_group(bi, sl, sB, g * 128, 128, 3)
                P.op("act", lambda a, g=g, bi=bi: a.activation(out=ptail[:, g, :], in_=banks[bi][:, 496:512], func=AF.Copy),
                     reads=[bankB[bi]], writes=[ptailB], append=(g > 0))

        def phase_B(l, sgi, save_tail):
            pbuf = [f32v(r2(0, 4128)), f32v(r1(16384, 4128))]
            s_a = f32v(r2(4128, 4128))
            s_b = f32v(r1(0, 4128))
            dg = [r2(8256, 2048), r1(16384 + 4128, 2048)]
            sgl = [f32v(r2(10304, 1024)), f32v(r2(11328, 1024))]
            W = SGT + 16
            pbB = [Buf("pbuf0"), Buf("pbuf1")]
            saB = Buf("s_a")
            sbB = Buf("s_b")
            dgB = [Buf("dg0"), Buf("dg1")]
            sgB = [Buf("sg0"), Buf("sg1")]
            t16B = Buf("t16")
            slots_g = {}

            def front(g):
                q = g % 2
                slots_g[g] = wload([win_piece(l, OFF["p_in"] + 128 * g, 128, 0), win_piece(l, OFF["p_gate"] + 128 * g, 128, 128)])
                sl, sB = slots_g[g]
                P.op("dve", lambda v: v.tensor_copy(out=pbuf[q][:, 0:16], in_=ptail[:, g, :]), reads=[ptailB], writes=[pbB[q]])
                for tg in range(NTG):
                    bi = 6 + (tg % 2)
                    proj_group(bi, sl, sB, 0, 128, tg)
                    P.op("act", lambda a, tg=tg, bi=bi: a.activation(out=pbuf[q][:, 16 + tg * TG:16 + (tg + 1) * TG], in_=banks[bi][:, :], func=AF.Copy),
                         reads=[bankB[bi]], writes=[pbB[q]], append=True)

            def back(g):
                q = g % 2
                w = 2 ** (g + 1)
                sl, sB = slots_g[g]
                pb = pbuf[q]
                P.op("dve", lambda v: v.tensor_tensor(out=s_a[:, 1:W], in0=pb[:, 1:W], in1=pb[:, 0:W - 1], op=ALU.add),
                     reads=[pbB[q]], writes=[saB])
                S, SB_ = s_a, saB
                if g >= 1:
                    P.op("dve", lambda v: v.tensor_tensor(out=s_b[:, 3:W], in0=s_a[:, 3:W], in1=s_a[:, 1:W - 2], op=ALU.add),
                         reads=[saB], writes=[sbB])
                    S, SB_ = s_b, sbB
                if g >= 2:
                    P.op("dve", lambda v: v.tensor_tensor(out=s_a[:, 7:W], in0=s_b[:, 7:W], in1=s_b[:, 3:W - 4], op=ALU.add),
                         reads=[sbB], writes=[saB])
                    S, SB_ = s_a, saB
                if g >= 3:
                    P.op("dve", lambda v: v.tensor_tensor(out=s_b[:, 15:W], in0=s_a[:, 15:W], in1=s_a[:, 7:W - 8], op=ALU.add),
                         reads=[saB], writes=[sbB])
                    S, SB_ = s_b, sbB
                P.op("dve", lambda v, S=S: v.scalar_tensor_tensor(out=dg[q][:, :], in0=S[:, 16:W], scalar=1.0 / w, in1=pb[:, 16:W],
                                                                  op0=ALU.mult, op1=ALU.subtract), reads=[SB_, pbB[q]], writes=[dgB[q]])
                P.op("dve", lambda v, S=S: v.tensor_tensor(out=t16[:], in0=S[:, 16:32], in1=rc_t[:, sgi, g * 16:(g + 1) * 16], op=ALU.mult),
                     reads=[SB_], writes=[t16B])
                P.op("dve", lambda v: v.tensor_tensor(out=dg[q][:, 0:16], in0=t16[:], in1=pb[:, 16:32], op=ALU.subtract),
                     reads=[t16B, pbB[q]], writes=[dgB[q]], append=True)
                if save_tail:
                    P.op("dve", lambda v: v.tensor_copy(out=ptail[:, g, :], in_=pb[:, SGT:SGT + 16]), reads=[pbB[q]], writes=[ptailB])
                for tg in range(NTG):
                    bg = 2 + (tg % 2)
                    by = 4 + (tg % 2)
                    proj_group(bg, sl, sB, 128, 128, tg)
                    P.op("act", lambda a, tg=tg, bg=bg: a.activation(out=sgl[tg % 2], in_=banks[bg][:, :], func=AF.Silu),
                         reads=[bankB[bg]], writes=[sgB[tg % 2]])
                    mm_group(by, [(banks[by][:, :], poolw[:, g, :], dg[q][:, tg * TG:(tg + 1) * TG], True, True)], reads=[dgB[q], layerB])
                    P.op("dve", lambda v, tg=tg, by=by: v.scalar_tensor_tensor(
                        out=yT[:, 8 + g, tg * TG:(tg + 1) * TG], in0=banks[by][:, :], scalar=psc_col[:, l, g:g + 1], in1=sgl[tg % 2],
                        op0=ALU.mult, op1=ALU.mult), reads=[bankB[by], sgB[tg % 2]], writes=[ybuf(1, g, tg)])

            front(0)
            front(1)
            back(0)
            front(2)
            back(1)
            front(3)
            back(2)
            back(3)
            P.barrier()

        def phase_A(l):
            vg = [f32v(r2(1024 * i, 1024)) for i in range(4)]
            vt = vg
            vln = [r2(4096 + 512 * i, 512) for i in range(4)]
            gu = [f32v(r2(6144, 1024)), f32v(r2(7168, 1024))]
            sgl = [f32v(r2(8192, 1024)), f32v(r2(9216, 1024))]
            t1 = [f32v(r2(10240, 1024)), f32v(r2(11264, 1024))]
            slV, sBV = wload([win_piece(l, OFF["a_v"], 512, 0)])
            slU, sBU = wload([win_piece(l, OFF["a_u"], 512, 0)])
            slG, sBG = wload([win_piece(l, OFF["a_gate"], 512, 0)])
            vgB = [Buf("vg%d" % i) for i in range(4)]
            vtB = vgB
            vlnB = [Buf("vln%d" % i) for i in range(4)]
            stB = [Buf("st%d" % i) for i in range(4)]
            guB = [Buf("gu0"), Buf("gu1")]
            sgB = [Buf("sg0"), Buf("sg1")]
            t1B = [Buf("t10"), Buf("t11")]

            def rec_front(idx, tg, tt, p):
                c0 = tg * TG + tt * 128
                bv = 6 + (idx % 2)
                mm_group(bv, [(banks[bv][:, :], hT[:, k, c0:c0 + 128], slV[:, k, :], k == 0, k == 7) for k in range(8)],
                         reads=[sBV, hTB[tg]])
                P.op("act", lambda a: a.activation(out=vg[p], in_=banks[bv][:, :], func=AF.Gelu),
                     reads=[bankB[bv]], writes=[vgB[p]])
                P.op("dve", lambda v: v.bn_stats(out=bnst[:, p, :], in_=vg[p]), reads=[vgB[p]], writes=[stB[p]])
                P.op("dve", lambda v: v.bn_aggr(out=bnmv[:, p, :], in_=bnst[:, p, :]), reads=[stB[p]], writes=[stB[p]])
                P.op("act", lambda a: a.activation(out=bnrs[:, p, :], in_=bnmv[:, p, 1:2], func=AF.Sqrt, bias=epst[:], scale=1.0),
                     reads=[stB[p]], writes=[stB[p]])
                P.op("dve", lambda v: v.reciprocal(out=bnrs[:, p, :], in_=bnrs[:, p, :]), reads=[stB[p]], writes=[stB[p]])
                P.op("dve", lambda v: v.tensor_scalar(out=vt[p], in0=vg[p], scalar1=bnmv[:, p, 0:1], scalar2=bnrs[:, p, 0:1],
                                                      op0=ALU.subtract, op1=ALU.mult), reads=[vgB[p], stB[p]], writes=[vtB[p]])
                P.op("dve", lambda v: v.tensor_tensor(out=vt[p], in0=vt[p], in1=lng_b[:], op=ALU.mult),
                     reads=[vtB[p], layerB], writes=[vtB[p]])
                P.op("dve", lambda v: v.tensor_tensor(out=vln[p], in0=vt[p], in1=lnb_b[:], op=ALU.add),
                     reads=[vtB[p], layerB], writes=[vlnB[p]])

            def rec_back(idx, tg, tt, p):
                for h in range(4):
                    mm_group(h, [(banks[h][:, tt * 128:(tt + 1) * 128], vln[p][:, h * 128:(h + 1) * 128], ws_bf[:, h, :], True, True)],
                             reads=[vlnB[p], layerB], append=(tt > 0))
                if tt < 3:
                    return
                for h in range(4):
                    q = h % 2
                    proj_group(4, slU, sBU, h * 128, 128, tg)
                    P.op("act", lambda a, q=q: a.activation(out=gu[q], in_=banks[4][:, :], func=AF.Gelu), reads=[bankB[4]], writes=[guB[q]])
                    proj_group(5, slG, sBG, h * 128, 128, tg)
                    P.op("act", lambda a, q=q: a.activation(out=sgl[q], in_=banks[5][:, :], func=AF.Silu), reads=[bankB[5]], writes=[sgB[q]])
                    bsv = bs_b[:, h * 128:(h + 1) * 128].unsqueeze(1).to_broadcast([128, 4, 128])
                    P.op("dve", lambda v, q=q, h=h, bsv=bsv: v.tensor_tensor(
                        out=t1[q].rearrange("p (a b) -> p a b", a=4), in0=banks[h][:, :].rearrange("p (a b) -> p a b", a=4), in1=bsv, op=ALU.add),
                        reads=[bankB[h], layerB], writes=[t1B[q]])
                    P.op("dve", lambda v, q=q: v.tensor_tensor(out=t1[q], in0=t1[q], in1=gu[q], op=ALU.mult), reads=[t1B[q], guB[q]], writes=[t1B[q]])
                    P.op("dve", lambda v, q=q, h=h: v.tensor_tensor(out=yT[:, 4 + h, tg * TG:(tg + 1) * TG], in0=t1[q], in1=sgl[q], op=ALU.mult),
                         reads=[t1B[q], sgB[q]], writes=[ybuf(0, h, tg)])

            pend = []
            idx = 0
            for tg in range(NTG):
                for tt in range(4):
                    p = idx % 4
                    rec_front(idx, tg, tt, p)
                    pend.append((idx, tg, tt, p))
                    idx += 1
                    if len(pend) > 3:
                        rec_back(*pend.pop(0))
            while pend:
                rec_back(*pend.pop(0))
            P.barrier()

        def layer_setup(l):
            P.dma("sp", lng_b[:], ln_g[l:l + 1, :].partition_broadcast(128), writes=[layerB])
            P.dma("sp", lnb_b[:], ln_b[l:l + 1, :].partition_broadcast(128), writes=[layerB], append=True)
            P.dma("sp", bs_b[:], gm_bs[l:l + 1, :].partition_broadcast(128), writes=[layerB], append=True)
            P.dma("sp", ws_f[:], wsT[l].rearrange("h s t -> s h t"), writes=[layerB], append=True)
            P.dma("pool", poolw[:], pool_w[l].rearrange("g i o -> i g o"), writes=[layerB], append=True)
            for h in range(4):
                P.op("dve", lambda v, h=h: v.tensor_tensor(out=ws_bf[:, h, :], in0=ws_f[:, h, :], in1=mk_b[:, 0, 0:128], op=ALU.mult),
                     reads=[layerB], writes=[layerB], append=True)
            if not recompute_prev:
                P.op("dve", lambda v: v.memset(ptail[:], 0.0), writes=[ptailB])
            memfB2 = Buf("memf2")
            P.dma("sp", memf, memT.rearrange("(k p) m -> p k m", p=128), writes=[memfB2])
            for k in range(8):
                P.op("dve", lambda v, k=k: v.scalar_tensor_tensor(out=mn[:, k, :], in0=memf[:, k, :], scalar=gm_col[:, l, k:k + 1], in1=rstd_m[:],
                                                                   op0=ALU.mult, op1=ALU.mult), reads=[memfB2, memB], writes=[memB], append=True)
            slK, sBK = wload([(lambda sl: sl[:, :, :], w_kv[l, :, 0:512].rearrange("(k p) n -> p k n", p=128))])
            slVv, sBVv = wload([(lambda sl: sl[:, :, :], w_kv[l, :, 512:1024].rearrange("(k p) n -> p k n", p=128))])
            for h in range(4):
                bi = 6 + (h % 2)
                mm_group(bi, [(banks[bi][:, 0:256], slK[:, k, h * 128:(h + 1) * 128], mn[:, k, :], k == 0, k == 7) for k in range(8)],
                         reads=[sBK, memB])
                P.op("act", lambda a, h=h, bi=bi: a.activation(out=kmT[:, h, :], in_=banks[bi][:, 0:256], func=AF.Copy),
                     reads=[bankB[bi]], writes=[memB], append=True)
            for mb in range(2):
                bi = 4 + mb
                mm_group(bi, [(banks[bi][:, :], mn[:, k, mb * 128:(mb + 1) * 128], slVv[:, k, :], k == 0, k == 7) for k in range(8)],
                         reads=[sBVv, memB])
                P.op("act", lambda a, mb=mb, bi=bi: a.activation(out=vm[:, mb, :], in_=banks[bi][:, :], func=AF.Copy),
                     reads=[bankB[bi]], writes=[memB], append=True)
            P.barrier()

        def phase_M(l):
            qs = [r2(512 * i, 512) for i in range(3)]
            Ev = [r2(1536 + 1024 * i, 1024).rearrange("p (m t) -> p m t", m=2) for i in range(3)]
            rec = [f32v(r2(4608 + 1024 * i, 1024)) for i in range(2)]
            sgl = [f32v(r2(6656 + 1024 * i, 1024)) for i in range(3)]
            tt_ = [f32v(r2(9728 + 1024 * i, 1024)) for i in range(2)]
            qsB = [Buf("qs%d" % i) for i in range(3)]
            EvB = [Buf("E%d" % i) for i in range(3)]
            recB = [Buf("r0"), Buf("r1")]
            sgB = [Buf("s%d" % i) for i in range(3)]
            ttB = [Buf("t0"), Buf("t1")]
            slots_h = {}

            def stA(i, h, tg):
                p3 = i % 3
                if tg == 0:
                    slots_h[h] = wload([win_piece(l, OFF["m_q"] + 128 * h, 128, 0), win_piece(l, OFF["m_gate"] + 128 * h, 128, 128)])
                sl, sB = slots_h[h]
                bq = 4 + (i % 2)
                bg = 6 + (i % 2)
                proj_group(bq, sl, sB, 0, 128, tg)
                P.op("dve", lambda v: v.tensor_scalar_mul(out=qs[p3], in0=banks[bq][:, :], scalar1=QSCALE),
                     reads=[bankB[bq]], writes=[qsB[p3]])
                proj_group(bg, sl, sB, 128, 128, tg)
                P.op("act", lambda a: a.activation(out=sgl[p3], in_=banks[bg][:, :], func=AF.Silu), reads=[bankB[bg]], writes=[sgB[p3]])

            def stB_(i, h, tg):
                p3 = i % 3
                for mb in range(2):
                    mm_group(mb, [(banks[mb][:, :], kmT[:, h, mb * 128:(mb + 1) * 128], qs[p3], True, True)], reads=[qsB[p3], memB])
                    P.op("act", lambda a, mb=mb: a.activation(out=Ev[p3][:, mb, :], in_=banks[mb][:, :], func=AF.Exp),
                         reads=[bankB[mb]], writes=[EvB[p3]], append=(mb > 0))

            def stC(i, h, tg):
                p3 = i % 3
                p = i % 2
                mm_group(2, [(banks[2][:, :], vm[:, mb, h * 128:(h + 1) * 128], Ev[p3][:, mb, :], mb == 0, mb == 1) for mb in range(2)],
                         reads=[EvB[p3], memB])
                mm_group(3, [(banks[3][:, :], ones_bf[:], Ev[p3][:, mb, :], mb == 0, mb == 1) for mb in range(2)], reads=[EvB[p3]])
                P.op("act", lambda a: a.activation(out=rec[p], in_=banks[3][:, :], func=AF.Ln), reads=[bankB[3]], writes=[recB[p]])
                P.op("act", lambda a: a.activation(out=rec[p], in_=rec[p], func=AF.Exp, scale=-1.0), reads=[recB[p]], writes=[recB[p]])
                P.op("dve", lambda v: v.tensor_tensor(out=tt_[p], in0=banks[2][:, :], in1=rec[p], op=ALU.mult),
                     reads=[bankB[2], recB[p]], writes=[ttB[p]])
                P.op("dve", lambda v: v.tensor_tensor(out=yT[:, 12 + h, tg * TG:(tg + 1) * TG], in0=tt_[p], in1=sgl[p3], op=ALU.mult),
                     reads=[ttB[p], sgB[p3]], writes=[ybuf(3, h, tg)])

            its = [(i, i // 4, i % 4) for i in range(16)]
            for i in range(16 + 2):
                if i < 16:
                    stA(*its[i])
                if 0 <= i - 1 < 16:
                    stB_(*its[i - 1])
                if 0 <= i - 2 < 16:
                    stC(*its[i - 2])
            P.barrier()

        zT = R2[:, :].rearrange("p (c t) -> p c t", c=8)
        zB = {}

        def phase_G(l):
            sgt = [xt0[:, 0:512], xt0[:, 512:1024]]
            zacc = [xt0[:, 1024:1536], xt0[:, 1536:2048]]
            tmp = [xt1[:, 0:512], xt1[:, 512:1024]]
            sgtB = [Buf("sgt0"), Buf("sgt1")]
            zaB = [Buf("za0"), Buf("za1")]
            tmB = [Buf("tm0"), Buf("tm1")]
            it = 0
            for cc in range(8):
                slG, sBG = wload([win_piece(l, OFF["g_merge"] + b * 1024 + cc * 128, 128, b * 128) for b in range(4)])
                slW, sBW = wload([(lambda sl, b=b: sl[:, 0:4, b * 128:(b + 1) * 128],
                                   w_br[l, b, :, cc * 128:(cc + 1) * 128].rearrange("(k p) n -> p k n", p=128)) for b in range(4)])
                for tg in range(NTG):
                    zp = it % 2
                    it += 1
                    for b in range(4):
                        p = b % 2
                        gb = 0 + p
                        yb = 2 + p
                        proj_group(gb, slG, sBG, b * 128, 128, tg)
                        yi = YIDX[b]
                        mm_group(yb, [(banks[yb][:, :], slW[:, kc, b * 128:(b + 1) * 128], yT[:, yi * 4 + kc, tg * TG:(tg + 1) * TG], kc == 0, kc == 3)
                                      for kc in range(4)], reads=[sBW] + [ybuf(b, kc, tg) for kc in range(4)])
                        P.op("act", lambda a, p=p, gb=gb: a.activation(out=sgt[p], in_=banks[gb][:, :], func=AF.Sigmoid),
                             reads=[bankB[gb]], writes=[sgtB[p]])
                        if b == 0:
                            P.op("dve", lambda v, p=p, yb=yb, zp=zp: v.tensor_tensor(out=zacc[zp], in0=banks[yb][:, :], in1=sgt[p], op=ALU.mult),
                                 reads=[bankB[yb], sgtB[p]], writes=[zaB[zp]])
                        else:
                            P.op("dve", lambda v, p=p, yb=yb: v.tensor_tensor(out=tmp[p], in0=banks[yb][:, :], in1=sgt[p], op=ALU.mult),
                                 reads=[bankB[yb], sgtB[p]], writes=[tmB[p]])
                            if b < 3:
                                P.op("dve", lambda v, p=p, zp=zp: v.tensor_tensor(out=zacc[zp], in0=zacc[zp], in1=tmp[p], op=ALU.add),
                                     reads=[tmB[p], zaB[zp]], writes=[zaB[zp]])
                            else:
                                zB[(cc, tg)] = Buf(f"z{cc}_{tg}")
                                P.op("dve", lambda v, p=p, zp=zp, cc=cc, tg=tg: v.tensor_tensor(
                                    out=zT[:, cc, tg * TG:(tg + 1) * TG], in0=zacc[zp], in1=tmp[p], op=ALU.add),
                                    reads=[tmB[p], zaB[zp]], writes=[zB[(cc, tg)]])
            P.barrier()

        def phase_O(l, xsrc, xdst, t0):
            xq = [xt0[:, :].rearrange("p (k t) -> p k t", k=4), xt1[:, :].rearrange("p (k t) -> p k t", k=4)]
            slO = []
            for s_ in range(2):
                slO.append(wload([(lambda sl: sl[:, :, :], w_out[l, :, s_ * 512:(s_ + 1) * 512].rearrange("(k p) n -> p k n", p=128))]))
            steps = [(tg, half) for tg in range(NTG) for half in range(2)]

            def tile_of(i):
                tg, half = steps[i]
                return (t0 // 512 + tg) * 2 + half
            for i, (tg, half) in enumerate(steps):
                p = i % 2
                if i == 0:
                    P.dma("sp", xtf[0], xsrc[tile_of(0)], writes=[xthB[0]])
                if i + 1 < 8:
                    P.dma("sp", xtf[1 - p], xsrc[tile_of(i + 1)], writes=[xthB[1 - p]])
                sl, sB = slO[half]
                for kk in range(4):
                    bi = 2 + kk
                    mm_group(bi, [(banks[bi][:, :], sl[:, cc, kk * 128:(kk + 1) * 128], zT[:, cc, tg * TG:(tg + 1) * TG], cc == 0, cc == 7)
                                  for cc in range(8)], reads=[sB] + [zB[(cc, tg)] for cc in range(8)])
                    P.op("dve", lambda v, kk=kk, bi=bi, p=p: v.tensor_tensor(out=xq[p][:, kk, :], in0=banks[bi][:, :], in1=xq[p][:, kk, :], op=ALU.add),
                         reads=[bankB[bi], xthB[p]], writes=[xthB[p]], append=True)
                P.dma("sp", xdst[tile_of(i)], xtf[p], reads=[xthB[p]])
            P.barrier()

        def phase_F(xsrc):
            sqo = [hT[:, 0, :].rearrange("p (k t) -> p k t", k=8), hT[:, 1, :].rearrange("p (k t) -> p k t", k=8)]
            nst = NT // 256
            for s_ in range(nst):
                p = s_ % 2
                c0 = s_ * 256
                if s_ == 0:
                    x_load256(xsrc, 0, 0)
                if s_ + 1 < nst:
                    x_load256(xsrc, c0 + 256, 1 - p)
                P.op("act", lambda a, p=p: a.activation(out=sqo[p], in_=xth[p], func=AF.Square), reads=[xthB[p]], writes=[sqBs[p]])
                mm_group(p, [(banks[p][:, 0:256], ones_bf[:], sqo[p][:, k, :], k == 0, k == 7) for k in range(8)], reads=[sqBs[p]])
                P.op("act", lambda a, p=p: a.activation(out=rsv[p], in_=banks[p][:, 0:256], func=AF.Ln, bias=epst[:], scale=1.0 / D),
                     reads=[bankB[p]], writes=[rsB[p]])
                P.op("act", lambda a, p=p: a.activation(out=rsv[p], in_=rsv[p], func=AF.Exp, scale=-0.5), reads=[rsB[p]], writes=[rsB[p]])
                for k in range(8):
                    P.op("dve", lambda v, k=k, p=p: v.scalar_tensor_tensor(out=xth[p][:, k, :], in0=xth[p][:, k, :], scalar=gf_col[:, k:k + 1], in1=rsv[p],
                                                                            op0=ALU.mult, op1=ALU.mult), reads=[rsB[p], xthB[p]], writes=[xthB[p]], append=(k > 0))
                P.dma("sp", oT[s_], xtf[p], reads=[xthB[p]])
            P.barrier()

        for l in range(L):
            layer_setup(l)
            if recompute_prev:
                xsrc, xdst = xT, xdst_t
            else:
                xsrc = xT if l == 0 else xdst_t
                xdst = xdst_t
            for sgi in range(NSG):
                P.new_epoch()
                t0 = sgi * SGT
                if recompute_prev:
                    phase_P0(l, xpT, 0)
                    P.barrier()
                    phase_prev_recompute(l)
                    P.barrier()
                kv0 = kv_begin(l, 0)
                phase_P0(l, xsrc, t0, after_tg=lambda tg, kv0=kv0: kv_tg(kv0[0], kv0[1], tg))
                has_prev = recompute_prev or sgi > 0
                phase_C(l, has_prev, save_kv=(not recompute_prev and sgi < NSG - 1), kv0_done=True)
                phase_B(l, sgi, save_tail=(not recompute_prev and sgi < NSG - 1))
                phase_A(l)
                phase_M(l)
                phase_G(l)
                phase_O(l, xsrc, xdst, t0)
        phase_F(xdst_t)
        P.barrier(full=True)

        @block.tensor
        def _(pe):
            for f in P.q["pe"]:
                f(pe)

        @block.scalar
        def _(act):
            for f in P.q["act"]:
                f(act)

        @block.vector
        def _(dve):
            for f in P.q["dve"]:
                f(dve)

        @block.gpsimd
        def _(pool):
            for f in P.q["pool"]:
                f(pool)

        @block.sync
        def _(sp):
            for f in P.q["sp"]:
                f(sp)
    return nc


_PROG_CACHE = {}


def _get_prog(L, NSG, recompute_prev):
    key = (L, NSG, recompute_prev)
    if key not in _PROG_CACHE:
        _PROG_CACHE[key] = build_program(L, NSG, recompute_prev)
    return _PROG_CACHE[key]


def _masks(pv):
    k = np.arange(128)[:, None]
    q = np.arange(128)[None, :]
    cur = (k <= q).astype(np.float32)
    A = (k >= q).astype(np.float32)
    X = A * np.float32(pv)
    m = np.zeros((4, 128, 512), np.float32)
    m[0] = np.concatenate([cur, cur, A, A], axis=1)
    m[1] = np.concatenate([cur, cur, A, X], axis=1)
    m[2] = np.concatenate([cur, cur, X, X], axis=1)
    m[3, :, 0:128] = np.eye(128, dtype=np.float32)
    return m


def _rc(start):
    rc = np.zeros((128, 64), np.float32)
    for g, w in enumerate((2, 4, 8, 16)):
        for t in range(16):
            cnt = min(t + 1, w) if start else w
            rc[:, g * 16 + t] = 1.0 / cnt
    return rc


def _tile_x(a):
    T = a.shape[0]
    return np.ascontiguousarray(a.reshape(T // 256, 256, 8, 128).transpose(0, 3, 2, 1)).reshape(T // 256, 128, 2048)


def _tile_x_in(a):
    T = a.shape[0]
    return np.ascontiguousarray(a.reshape(T // 512, 512, 2, 4, 128).transpose(0, 2, 4, 3, 1)).reshape(T // 256, 128, 2048)


def _untile_x_in(a):
    n = a.shape[0] // 2
    return np.ascontiguousarray(a.reshape(n, 2, 128, 4, 512).transpose(0, 4, 1, 3, 2)).reshape(n * 512, 1024)


def _untile_x(a):
    n = a.shape[0]
    return np.ascontiguousarray(a.reshape(n, 128, 8, 256).transpose(0, 3, 2, 1)).reshape(n * 256, 1024)


def kernel(x, mem, norm_g, w_in, gm_ln_g, gm_ln_b, gm_ws, gm_bs, pool_w, pool_scale,
           mem_norm_g, w_mem_kv, w_branch, w_out, final_norm_g):
    f = lambda a: np.ascontiguousarray(np.asarray(a, dtype=np.float32))
    x = f(x); mem = f(mem)
    B, S, _ = x.shape
    wsT_all = np.ascontiguousarray(np.transpose(f(gm_ws), (0, 1, 3, 2)))
    gm_bs_f = f(gm_bs).reshape(DEPTH, 512)
    def colv(a, nk):
        a = f(a)
        return np.ascontiguousarray(a.reshape(a.shape[0], nk, 128).transpose(2, 0, 1))
    common = dict(w_in=f(w_in), w_kv=f(w_mem_kv), w_br=f(w_branch), w_out=f(w_out), pool_w=f(pool_w), wsT=wsT_all,
                  ln_g=f(gm_ln_g), ln_b=f(gm_ln_b), gm_bs=gm_bs_f)
    colc = dict(norm_g=colv(norm_g, 8), mem_g=colv(mem_norm_g, 8), pscale=colv(pool_scale, 4))
    fin = np.ascontiguousarray(f(final_norm_g).reshape(8, 128).T)
    out = np.empty((B, S, D), np.float32)
    if MODE == "V4":
        nc = _get_prog(DEPTH, 2, False)
        in_maps = []
        for b in range(B):
            m = dict(common)
            m.update(colc)
            m.update(xT=_tile_x_in(x[b]), memT=np.ascontiguousarray(mem[b].T), fin_g=fin,
                     masks=_masks(1.0), rc=np.stack([_rc(True), _rc(False)]))
            in_maps.append(m)
        res = run_bass_kernel_spmd(nc, in_maps, core_ids=list(range(B)))
        for b in range(B):
            out[b] = _untile_x(np.asarray(res.results[b]["oT"]))
        return out
    nc = _get_prog(1, 1, True)
    xcur = [_tile_x_in(x[b]) for b in range(B)]
    zeros_prev = np.zeros((8, 128, 2048), np.float32)
    for l in range(DEPTH):
        in_maps = []
        for c in range(8):
            b, hf = c // 2, c % 2
            m = {k: np.ascontiguousarray(v[l:l + 1]) for k, v in common.items()}
            m.update({k: np.ascontiguousarray(v[:, l:l + 1]) for k, v in colc.items()})
            m.update(xT=np.ascontiguousarray(xcur[b][hf * 8:(hf + 1) * 8]),
                     xpT=(np.ascontiguousarray(xcur[b][0:8]) if hf == 1 else zeros_prev),
                     memT=np.ascontiguousarray(mem[b].T), fin_g=fin,
                     masks=_masks(float(hf)), rc=_rc(hf == 0)[None])
            in_maps.append(m)
        res = run_bass_kernel_spmd(nc, in_maps, core_ids=list(range(8)))
        if l < DEPTH - 1:
            xcur = [np.concatenate([np.asarray(res.results[2 * b]["xo"]), np.asarray(res.results[2 * b + 1]["xo"])], axis=0) for b in range(B)]
        else:
            for b in range(B):
                out[b] = _untile_x(np.concatenate([np.asarray(res.results[2 * b]["oT"]), np.asarray(res.results[2 * b + 1]["oT"])], axis=0))
    return out
```

```python
import numpy as np
from contextlib import ExitStack
import concourse.bass as bass
import concourse.mybir as mybir
from concourse.bass_utils import run_bass_kernel_spmd

MODE = "V4"

F32 = mybir.dt.float32
BF16 = mybir.dt.bfloat16
AF = mybir.ActivationFunctionType
ALU = mybir.AluOpType

DEPTH = 4
D = 1024
DIN = 10752
SGT = 2048
TG = 512
NTG = 4
EPS = 1e-6
OFF = dict(a_u=0, a_v=512, a_gate=1024, p_in=1536, p_gate=2048, c_q=2560, c_k=4096, c_v=4608,
           c_gate=5120, m_q=5632, m_gate=6144, g_merge=6656)
QSCALE = 128 ** -0.5
NB = 4
NDS = 40
YIDX = {0: 1, 1: 2, 2: 0, 3: 3}


class Buf:
    __slots__ = ("name", "lw", "rd")

    def __init__(self, name):
        self.name = name
        self.lw = []
        self.rd = {}


class Prog:
    ENG = ("pe", "act", "dve", "pool", "sp")

    def __init__(self, nc, ES):
        self.nc = nc
        self.ES = ES
        self.q = {n: [] for n in self.ENG}
        self.sem = {}
        self.cnt = {}
        self.key = {}
        self.waited = {n: {} for n in self.ENG}
        self.last_tok = {n: None for n in self.ENG}
        self.nkeys = 0
        self.dma_sems = []
        for i in range(NDS):
            self.dma_sems.append((self._newkey(), ES.enter_context(nc.semaphore(f"dq{i}"))))
        self.dma_val = [0] * NDS
        self.dma_rr = 0
        self.dma_recent = []
        self.n_ep = 0
        self.new_epoch()

    def _newkey(self):
        self.nkeys += 1
        return self.nkeys

    def new_epoch(self):
        for n in ("pe", "act", "dve"):
            self.sem[n] = self.ES.enter_context(self.nc.semaphore(f"e{self.n_ep}_{n}"))
            self.cnt[n] = 0
            self.key[n] = self._newkey()
        self.n_ep += 1

    def _waits(self, qn, toks, skip_self=False):
        best = {}
        for t in toks:
            if t is None:
                continue
            k, s, v = t
            if skip_self and qn in self.key and k == self.key[qn]:
                continue
            if k not in best or best[k][1] < v:
                best[k] = (s, v)
        out = []
        wd = self.waited[qn]
        for k, (s, v) in best.items():
            if wd.get(k, 0) >= v:
                continue
            wd[k] = v
            out.append((s, v))
        return out

    def _deps(self, reads, writes, after, append):
        toks = list(after)
        for b in reads:
            toks.extend(b.lw)
        for b in writes:
            if not append:
                toks.extend(b.lw)
            toks.extend(b.rd.values())
        return toks

    def _update(self, tok, reads, writes, append):
        for b in writes:
            if append:
                b.lw.append(tok)
            else:
                b.lw = [tok]
                b.rd = {}
        for b in reads:
            k = tok[0]
            if k not in b.rd or b.rd[k][2] < tok[2]:
                b.rd[k] = tok

    def op(self, qn, fn, reads=(), writes=(), after=(), append=False):
        toks = self._deps(reads, writes, after, append)
        ws = self._waits(qn, toks, skip_self=(qn == "pe"))
        self.cnt[qn] += 1
        sem = self.sem[qn]
        tok = (self.key[qn], sem, self.cnt[qn])

        def run(eng, ws=ws, fn=fn, sem=sem):
            for (s, v) in ws:
                eng.wait_ge(s, v)
            fn(eng).then_inc(sem, 1)
        self.q[qn].append(run)
        self._update(tok, reads, writes, append)
        self.last_tok[qn] = tok
        return tok

    def dma(self, qn, out_ap, in_ap, reads=(), writes=(), after=(), append=False, track=False):
        toks = self._deps(reads, writes, after, append)
        i = self.dma_rr
        self.dma_rr = (i + 1) % NDS
        k, sem = self.dma_sems[i]
        prev = self.dma_val[i]
        self.dma_val[i] += 16
        val = self.dma_val[i]
        if prev > 0:
            toks.append((k, sem, prev))
        ws = self._waits(qn, toks)
        tok = (k, sem, val)

        def run(eng, ws=ws, sem=sem, out_ap=out_ap, in_ap=in_ap):
            for (s, v) in ws:
                eng.wait_ge(s, v)
            eng.dma_start(out=out_ap, in_=in_ap).then_inc(sem, 16)
        self.q[qn].append(run)
        self._update(tok, reads, writes, append)
        if qn != "pool" or track:
            self.dma_recent.append(tok)
        return tok

    def barrier(self, full=False):
        toks = [self.last_tok[n] for n in ("pe", "act", "dve")] + self.dma_recent
        for qn in (("pe", "act", "dve", "sp") if full else ("act", "dve", "sp")):
            ws = self._waits(qn, toks, skip_self=True)
            if not ws:
                continue

            def run(eng, ws=ws):
                for (s, v) in ws:
                    eng.wait_ge(s, v)
            self.q[qn].append(run)
        self.dma_recent = []


def build_program(L, NSG, recompute_prev):
    nc = bass.Bass("TRN2", target_bir_lowering=False)
    NT = SGT * NSG

    def dram(name, shape, dt=F32, kind="ExternalInput"):
        return nc.dram_tensor(name, shape, dt, kind=kind).ap()

    NTL = NT // 256
    xT = dram("xT", [NTL, 128, 2048])
    xpT = dram("xpT", [8, 128, 2048]) if recompute_prev else None
    memT = dram("memT", [D, 256])
    w_in = dram("w_in", [L, D, DIN])
    w_kv = dram("w_kv", [L, D, 1024])
    w_br = dram("w_br", [L, 4, 512, D])
    w_out = dram("w_out", [L, D, D])
    pool_w = dram("pool_w", [L, 4, 128, 128])
    wsT = dram("wsT", [L, 4, 128, 128])
    norm_g = dram("norm_g", [128, L, 8])
    mem_g = dram("mem_g", [128, L, 8])
    ln_g = dram("ln_g", [L, 512])
    ln_b = dram("ln_b", [L, 512])
    gm_bs = dram("gm_bs", [L, 512])
    pscale = dram("pscale", [128, L, 4])
    fin_g = dram("fin_g", [128, 8])
    masks = dram("masks", [4, 128, 512])
    rcin = dram("rc", [NSG, 128, 64])
    oT = dram("oT", [NTL, 128, 2048], kind="ExternalOutput")
    if recompute_prev:
        xdst_t = dram("xo", [NTL, 128, 2048], kind="ExternalOutput")
    else:
        xdst_t = dram("xs", [NTL, 128, 2048], kind="Internal")
    kvprev = dram("kvprev", [4, 2, 128, SGT], BF16, kind="Internal")

    ES = ExitStack()
    with ES:
        def sb(name, shape, dt):
            return ES.enter_context(nc.sbuf_tensor(name, shape, dt))

        hT = sb("hT", [128, 8, SGT], BF16)
        yT = sb("yT", [128, 16, SGT], BF16)
        R2 = sb("R2", [128, 16384], BF16)
        slots = [sb(f"ws{i}", [128, 8, 512], BF16) for i in range(NB)]
        xt0 = sb("xt0", [128, 2048], F32)
        xt1 = sb("xt1", [128, 2048], F32)
        rstd = sb("rstd", [128, 512], F32)
        ones_bf = sb("ones_bf", [128, 128], BF16)
        ident_bf = sb("ident_bf", [128, 128], BF16)
        mk_b = sb("mk_b", [128, 3, 512], BF16)
        epst = sb("epst", [128, 1], F32)
        g_col = sb("g_col", [128, L, 8], F32)
        gm_col = sb("gm_col", [128, L, 8], F32)
        gf_col = sb("gf_col", [128, 8], F32)
        psc_col = sb("psc_col", [128, L, 4], F32)
        lng_b = sb("lng_b", [128, 512], F32)
        lnb_b = sb("lnb_b", [128, 512], F32)
        bs_b = sb("bs_b", [128, 512], F32)
        ws_f = sb("ws_f", [128, 4, 128], F32)
        ws_bf = sb("ws_bf", [128, 4, 128], BF16)
        poolw = sb("poolw", [128, 4, 128], BF16)
        rc_t = sb("rc_t", [128, NSG, 64], F32)
        ptail = sb("ptail", [128, 4, 16], F32)
        rstd_m = sb("rstd_m", [128, 256], F32)
        mn = sb("mn", [128, 8, 256], BF16)
        kmT = sb("kmT", [128, 4, 256], BF16)
        vm = sb("vm", [128, 2, 512], BF16)
        bnst = sb("bnst", [128, 4, 6], F32)
        bnmv = sb("bnmv", [128, 4, 2], F32)
        bnrs = sb("bnrs", [128, 4, 1], F32)
        t16 = sb("t16", [128, 16], F32)
        banks = [ES.enter_context(nc.psum_tensor(f"pb{i}", [128, 512], F32)) for i in range(8)]

        P = Prog(nc, ES)
        block = ES.enter_context(nc.Block())

        bankB = [Buf(f"bank{i}") for i in range(8)]
        slotB = [Buf(f"slot{i}") for i in range(NB)]
        hTB = [Buf(f"hT{i}") for i in range(NTG)]
        xtB = Buf("xt")
        sqB = Buf("sq")
        rstdB = Buf("rstd")
        constB = Buf("const")
        layerB = Buf("layerconst")
        memB = Buf("memkv")
        ptailB = Buf("ptail")
        kvprevB = [Buf(f"kvprev{h}") for h in range(4)]
        yB = {}

        def ybuf(b, ch, tg):
            key = (b, ch, tg)
            if key not in yB:
                yB[key] = Buf(f"y{key}")
            return yB[key]

        R1 = yT[:, 4:16, :].rearrange("p a t -> p (a t)")

        def r1(off, n):
            return R1[:, off:off + n]

        def r2(off, n):
            return R2[:, off:off + n]

        def f32v(ap):
            return ap.bitcast(F32)

        def ss(s0, d):
            return slice(s0, s0 + 127 * d + 1, d)

        wstate = {"next": 0}

        def wload(pieces):
            s = wstate["next"]
            wstate["next"] = (s + 1) % NB
            first = True
            for outf, in_ap in pieces:
                P.dma("pool", outf(slots[s]), in_ap, writes=[slotB[s]], append=not first)
                first = False
            return slots[s], slotB[s]

        def win_piece(l, c0, n, dst0):
            return (lambda sl, dst0=dst0, n=n: sl[:, :, dst0:dst0 + n],
                    w_in[l, :, c0:c0 + n].rearrange("(k p) n -> p k n", p=128))

        def mm_group(bank_i, mms, reads, append=False, extra_writes=()):
            def fn(pe, mms=mms):
                inst = None
                for (o, a, b, st, sp) in mms:
                    inst = pe.matmul(o, lhsT=a, rhs=b, start=st, stop=sp)
                return inst
            return P.op("pe", fn, reads=reads, writes=[bankB[bank_i]] + list(extra_writes), append=append)

        def proj_group(bank_i, slot_ap, slot_buf, c0, n, tg, ncols=TG, tcol0=None):
            t0 = tg * TG if tcol0 is None else tcol0
            mms = [(banks[bank_i][0:n, 0:ncols], slot_ap[:, k, c0:c0 + n], hT[:, k, t0:t0 + ncols],
                    k == 0, k == 7) for k in range(8)]
            return mm_group(bank_i, mms, reads=[slot_buf, hTB[tg]])

        P.dma("pool", mk_b[:], masks[0:3].rearrange("m p n -> p m n"), writes=[constB], track=True)
        P.dma("pool", ident_bf[:], masks[3, :, 0:128], writes=[constB], append=True, track=True)
        P.dma("sp", g_col[:], norm_g, writes=[constB], append=True)
        P.dma("sp", gm_col[:], mem_g, writes=[constB], append=True)
        P.dma("sp", gf_col[:], fin_g, writes=[constB], append=True)
        P.dma("sp", psc_col[:], pscale, writes=[constB], append=True)
        P.dma("sp", rc_t[:], rcin.rearrange("s p n -> p s n"), writes=[constB], append=True)
        P.op("dve", lambda v: v.memset(epst[:], EPS), writes=[Buf("eps")])
        P.op("dve", lambda v: v.memset(ones_bf[:], 1.0), writes=[Buf("ones")])
        P.barrier()

        memf = f32v(r2(0, 4096)).rearrange("p (k m) -> p k m", k=8)
        memsq = r2(4096, 2048).rearrange("p (k m) -> p k m", k=8)
        memfB = Buf("memf")
        P.dma("sp", memf, memT.rearrange("(k p) m -> p k m", p=128), writes=[memfB])
        P.op("act", lambda a: a.activation(out=memsq, in_=memf, func=AF.Square), reads=[memfB], writes=[sqB])
        mm_group(0, [(banks[0][:, 0:256], ones_bf[:], memsq[:, k, :], k == 0, k == 7) for k in range(8)], reads=[sqB])
        P.op("act", lambda a: a.activation(out=rstd_m[:], in_=banks[0][:, 0:256], func=AF.Sqrt, bias=epst[:], scale=1.0 / D),
             reads=[bankB[0]], writes=[memB])
        P.op("dve", lambda v: v.reciprocal(out=rstd_m[:], in_=rstd_m[:]), reads=[memB], writes=[memB])
        P.barrier()

        xtf = [xt0[:, :], xt1[:, :]]
        xth = [xt0[:, :].rearrange("p (k t) -> p k t", k=8), xt1[:, :].rearrange("p (k t) -> p k t", k=8)]
        xthB = [Buf("xth0"), Buf("xth1")]
        rsv = [rstd[:, 0:256], rstd[:, 256:512]]
        rsB = [Buf("rs0"), Buf("rs1")]
        sqBs = [Buf("sq0"), Buf("sq1")]

        def x_load256(xsrc, c0, p):
            tok512 = c0 // 512
            sub = (c0 // 256) % 2
            for half in range(2):
                src = xsrc[tok512 * 2 + half].rearrange("p (k t) -> p k t", k=4)[:, :, sub * 256:(sub + 1) * 256]
                P.dma("sp", xth[p][:, half * 4:(half + 1) * 4, :], src, writes=[xthB[p]], append=(half == 1))

        def phase_P0(l, xsrc, t0, after_tg=None):
            sqv = [yT[:, 0, :].rearrange("p (k t) -> p k t", k=8), yT[:, 1, :].rearrange("p (k t) -> p k t", k=8)]
            for s_ in range(8):
                p = s_ % 2
                c0 = t0 + s_ * 256
                cl = s_ * 256
                tg = s_ // 2
                x_load256(xsrc, c0, p)
                P.op("act", lambda a, p=p: a.activation(out=sqv[p], in_=xth[p], func=AF.Square), reads=[xthB[p]], writes=[sqBs[p]])
                mm_group(p, [(banks[p][:, 0:256], ones_bf[:], sqv[p][:, k, :], k == 0, k == 7) for k in range(8)], reads=[sqBs[p]])
                P.op("act", lambda a, p=p: a.activation(out=rsv[p], in_=banks[p][:, 0:256], func=AF.Ln, bias=epst[:], scale=1.0 / D),
                     reads=[bankB[p]], writes=[rsB[p]])
                P.op("act", lambda a, p=p: a.activation(out=rsv[p], in_=rsv[p], func=AF.Exp, scale=-0.5), reads=[rsB[p]], writes=[rsB[p]])
                for k in range(8):
                    P.op("dve", lambda v, k=k, p=p, cl=cl: v.scalar_tensor_tensor(
                        out=hT[:, k, cl:cl + 256], in0=xth[p][:, k, :], scalar=g_col[:, l, k:k + 1], in1=rsv[p],
                        op0=ALU.mult, op1=ALU.mult), reads=[xthB[p], rsB[p]], writes=[hTB[tg]], append=not (s_ % 2 == 0 and k == 0))
                if after_tg is not None and s_ % 2 == 1:
                    after_tg(tg)

        KT = r1(0, 4096)
        VT = r1(4096, 4096)
        VA = r1(8192, 8832).rearrange("p (b e) -> p b e", e=128)
        QT = r1(17024, 6144).rearrange("p (g t) -> p g t", g=3)
        accv = f32v(r2(0, 4096))
        denv = f32v(r2(4096, 4096))
        Et = [r2(8192 + 512 * i, 512) for i in range(4)]
        Pm = [r2(10240 + 512 * i, 512) for i in range(4)]
        recv = f32v(r2(12288, 1024))
        sgv = [f32v(r2(13312, 1024)), f32v(r2(14336, 1024))]
        tv = f32v(r2(15360, 1024))

        KTB = Buf("KT")
        VTB = Buf("VT")
        VAB = Buf("VA")
        QTB = [Buf(f"QT{g}") for g in range(3)]
        accB = Buf("acc")
        denB = Buf("den")
        EB = [Buf("E0"), Buf("E1"), Buf("E2"), Buf("E3")]
        PB_ = [Buf("P0"), Buf("P1"), Buf("P2"), Buf("P3")]
        recB = Buf("rec")
        tB = Buf("t")
        sgCB = [Buf("sg0"), Buf("sg1")]

        def kv_begin(l, h):
            return wload([win_piece(l, OFF["c_k"] + 128 * h, 128, 0), win_piece(l, OFF["c_v"] + 128 * h, 128, 128)])

        def kv_tg(sl, sB, tg):
            bk = (0, 1)[tg % 2]
            bv = (2, 7)[tg % 2]
            proj_group(bk, sl, sB, 0, 128, tg)
            P.op("act", lambda a: a.activation(out=KT[:, SGT + tg * TG:SGT + (tg + 1) * TG], in_=banks[bk][:, :], func=AF.Copy),
                 reads=[bankB[bk]], writes=[KTB], append=(tg > 0))
            proj_group(bv, sl, sB, 128, 128, tg)
            P.op("dve", lambda v: v.tensor_copy(out=VT[:, SGT + tg * TG:SGT + (tg + 1) * TG], in_=banks[bv][:, :]),
                 reads=[bankB[bv]], writes=[VTB], append=(tg > 0))

        def kv_project(l, h):
            sl, sB = wload([win_piece(l, OFF["c_k"] + 128 * h, 128, 0), win_piece(l, OFF["c_v"] + 128 * h, 128, 128)])
            for tg in range(NTG):
                bk = (0, 1)[tg % 2]
                bv = (2, 7)[tg % 2]
                proj_group(bk, sl, sB, 0, 128, tg)
                P.op("act", lambda a, tg=tg, bk=bk: a.activation(out=KT[:, SGT + tg * TG:SGT + (tg + 1) * TG], in_=banks[bk][:, :], func=AF.Copy),
                     reads=[bankB[bk]], writes=[KTB], append=(tg > 0))
                proj_group(bv, sl, sB, 128, 128, tg)
                P.op("dve", lambda v, tg=tg, bv=bv: v.tensor_copy(out=VT[:, SGT + tg * TG:SGT + (tg + 1) * TG], in_=banks[bv][:, :]),
                     reads=[bankB[bv]], writes=[VTB], append=(tg > 0))
            return KTB, VTB

        def kv_save(h, KTB, VTB):
            P.dma("sp", kvprev[h, 0], KT[:, SGT:2 * SGT], reads=[KTB], writes=[kvprevB[h]])
            P.dma("sp", kvprev[h, 1], VT[:, SGT:2 * SGT], reads=[VTB], writes=[kvprevB[h]], append=True)

        def kv_loadprev(h, KTB, VTB):
            P.dma("sp", KT[:, 0:SGT], kvprev[h, 0], reads=[kvprevB[h]], writes=[KTB], append=True, after=list(KTB.lw))
            P.dma("sp", VT[:, 0:SGT], kvprev[h, 1], reads=[kvprevB[h]], writes=[VTB], append=True, after=list(VTB.lw))

        DIL = (1, 4, 16)

        def blocks_list(has_prev):
            lst = []
            for d in DIL:
                nb = 16 // d
                for r in range(d):
                    for n in range(-1 if has_prev else 0, nb):
                        lst.append((d, r, n))
            return lst

        def phase_C(l, has_prev, save_kv, kv0_done=False):
            for h in range(4):
                if not (h == 0 and kv0_done):
                    kv_project(l, h)
                if save_kv:
                    kv_save(h, KTB, VTB)
                if has_prev:
                    kv_loadprev(h, KTB, VTB)
                blist = blocks_list(has_prev)
                bidx = {b: i for i, b in enumerate(blist)}
                for g0 in range(0, len(blist), 8):
                    grp = blist[g0:g0 + 8]
                    bi = (7, 0, 1, 2)[(g0 // 8) % 4]
                    pbf = banks[bi][:, :].bitcast(BF16)

                    def fn(pe, grp=grp, pbf=pbf):
                        inst = None
                        for i, (d, r, n) in enumerate(grp):
                            st = SGT + r + d * 128 * n
                            inst = pe.transpose(out=pbf[:, i * 128:(i + 1) * 128], in_=VT[:, ss(st, d)], identity=ident_bf[:])
                        return inst
                    P.op("pe", fn, reads=[VTB], writes=[bankB[bi]])
                    ng = len(grp)
                    P.op("dve", lambda v, g0=g0, ng=ng, pbf=pbf: v.tensor_copy(
                        out=VA[:, g0:g0 + ng, :], in_=pbf[:, 0:ng * 128].rearrange("p (b e) -> p b e", e=128)),
                        reads=[bankB[bi]], writes=[VAB], append=(g0 > 0))
                slA, sBA = wload([win_piece(l, OFF["c_q"] + g * 512 + 128 * h, 128, g * 128) for g in range(3)]
                                 + [win_piece(l, OFF["c_gate"] + 128 * h, 128, 384)])
                for g in range(3):
                    for tg in range(NTG):
                        bi = (7, 0, 1, 2)[tg % 4]
                        proj_group(bi, slA, sBA, g * 128, 128, tg)
                        P.op("act", lambda a, g=g, tg=tg, bi=bi: a.activation(
                            out=QT[:, g, tg * TG:(tg + 1) * TG], in_=banks[bi][:, :], func=AF.Copy, scale=QSCALE),
                            reads=[bankB[bi]], writes=[QTB[g]], append=(tg > 0))
                LAG = 3
                pending = []
                step = 0

                def rec_pv(g, d, quad, pr, info, par, obank, dbank):
                    omms = []
                    dmms = []
                    for ii, (r, n, qs) in enumerate(info):
                        oc = (pr * 2 + ii) * 128
                        hasp = not (n == 0 and not has_prev)
                        pcol = 384 if ii == 0 else 256
                        omms.append((banks[obank][:, oc:oc + 128], VA[:, bidx[(d, r, n)], :], Pm[par][:, ii * 128:(ii + 1) * 128], True, not hasp))
                        dmms.append((banks[dbank][:, oc:oc + 128], ones_bf[:], Pm[par][:, ii * 128:(ii + 1) * 128], True, not hasp))
                        if hasp:
                            omms.append((banks[obank][:, oc:oc + 128], VA[:, bidx[(d, r, n - 1)], :], Pm[par][:, pcol:pcol + 128], False, True))
                            dmms.append((banks[dbank][:, oc:oc + 128], ones_bf[:], Pm[par][:, pcol:pcol + 128], False, True))
                    mm_group(obank, omms, reads=[PB_[par], VAB], append=(pr > 0))
                    mm_group(dbank, dmms, reads=[PB_[par]], append=(pr > 0))
                    if pr == 0:
                        return
                    j0 = quad * 4
                    if d == 1:
                        av = accv[:, j0 * 128:j0 * 128 + 512]
                        dv = denv[:, j0 * 128:j0 * 128 + 512]
                        ob = banks[obank][:, :]
                        db = banks[dbank][:, :]
                    elif d == 4:
                        av = accv[:, quad:SGT:4]
                        dv = denv[:, quad:SGT:4]
                        ob = banks[obank][:, :]
                        db = banks[dbank][:, :]
                    else:
                        av = accv.rearrange("p (i r) -> p r i", r=16)[:, j0:j0 + 4, :]
                        dv = denv.rearrange("p (i r) -> p r i", r=16)[:, j0:j0 + 4, :]
                        ob = banks[obank][:, :].rearrange("p (a b) -> p a b", a=4)
                        db = banks[dbank][:, :].rearrange("p (a b) -> p a b", a=4)
                    if g == 0:
                        P.op("act", lambda a, av=av, ob=ob: a.activation(out=av, in_=ob, func=AF.Copy),
                             reads=[bankB[obank]], writes=[accB], append=(quad > 0))
                        P.op("dve", lambda v, dv=dv, db=db: v.tensor_copy(out=dv, in_=db),
                             reads=[bankB[dbank]], writes=[denB], append=(quad > 0))
                    else:
                        P.op("dve", lambda v, av=av, ob=ob: v.tensor_tensor(out=av, in0=ob, in1=av, op=ALU.add),
                             reads=[bankB[obank]], writes=[accB], append=(quad > 0))
                        P.op("dve", lambda v, dv=dv, db=db: v.tensor_tensor(out=dv, in0=db, in1=dv, op=ALU.add),
                             reads=[bankB[dbank]], writes=[denB], append=(quad > 0))

                for g, d in enumerate(DIL):
                    nb = 16 // d
                    for quad in range(4):
                        obank = 3 + (quad % 2)
                        dbank = 5 + (quad % 2)
                        for pr in range(2):
                            j = quad * 4 + pr * 2
                            info = []
                            for jj in (j, j + 1):
                                r, n = jj // nb, jj % nb
                                info.append((r, n, r + d * 128 * n))
                            cross = [n == 0 for (_, n, _) in info]
                            if cross[0] and cross[1]:
                                mki, ncols = 2, (512 if has_prev else 256)
                            elif cross[0]:
                                mki, ncols = 1, (512 if has_prev else 384)
                            else:
                                mki, ncols = 0, 512
                            sbank = (0, 1, 2, 7)[step % 4]
                            par = step % 4
                            step += 1
                            mms = []
                            for ii, (r, n, qs) in enumerate(info):
                                qap = QT[:, g, ss(qs, d)]
                                mms.append((banks[sbank][:, ii * 128:(ii + 1) * 128], KT[:, ss(SGT + qs, d)], qap, True, True))
                            for ii, col in ((1, 256), (0, 384)):
                                r, n, qs = info[ii]
                                if n == 0 and not has_prev:
                                    continue
                                qap = QT[:, g, ss(qs, d)]
                                ks = SGT + qs - 128 * d
                                mms.append((banks[sbank][:, col:col + 128], KT[:, ss(ks, d)], qap, True, True))
                            mm_group(sbank, mms, reads=[KTB, QTB[g]])
                            P.op("act", lambda a, sbank=sbank, par=par, ncols=ncols: a.activation(
                                out=Et[par][:, 0:ncols], in_=banks[sbank][:, 0:ncols], func=AF.Exp),
                                reads=[bankB[sbank]], writes=[EB[par]])
                            P.op("dve", lambda v, par=par, ncols=ncols, mki=mki: v.tensor_tensor(
                                out=Pm[par][:, 0:ncols], in0=Et[par][:, 0:ncols], in1=mk_b[:, mki, 0:ncols], op=ALU.mult),
                                reads=[EB[par]], writes=[PB_[par]])
                            pending.append((g, d, quad, pr, info, par, obank, dbank))
                            if len(pending) > LAG:
                                rec_pv(*pending.pop(0))
                while pending:
                    rec_pv(*pending.pop(0))
                sgB = sgCB
                for tg in range(NTG):
                    bi = (7, 0)[tg % 2]
                    proj_group(bi, slA, sBA, 384, 128, tg)
                    P.op("act", lambda a, tg=tg, bi=bi: a.activation(out=sgv[tg % 2], in_=banks[bi][:, :], func=AF.Silu),
                         reads=[bankB[bi]], writes=[sgB[tg % 2]])
                    P.op("act", lambda a, tg=tg: a.activation(out=recv, in_=denv[:, tg * TG:(tg + 1) * TG], func=AF.Ln), reads=[denB], writes=[recB])
                    P.op("act", lambda a: a.activation(out=recv, in_=recv, func=AF.Exp, scale=-1.0), reads=[recB], writes=[recB])
                    P.op("dve", lambda v, tg=tg: v.tensor_tensor(out=tv, in0=accv[:, tg * TG:(tg + 1) * TG], in1=recv, op=ALU.mult),
                         reads=[accB, recB], writes=[tB])
                    P.op("dve", lambda v, tg=tg, h=h: v.tensor_tensor(out=yT[:, 0 + h, tg * TG:(tg + 1) * TG], in0=tv, in1=sgv[tg % 2], op=ALU.mult),
                         reads=[tB, sgB[tg % 2]], writes=[ybuf(2, h, tg)] + ([sqBs[h]] if h < 2 else []))
            P.barrier()

        def phase_prev_recompute(l):
            for h in range(4):
                KTB, VTB = kv_project(l, h)
                kv_save(h, KTB, VTB)
                P.barrier()
            sl, sB = wload([win_piece(l, OFF["p_in"], 512, 0)])
            for g in range(4):
                bi = 6 + (g % 2)
                proj_group(bi, sl, sB, g * 128, 128, 3)
                P.op("act", lambda a, g=g, bi=bi: a.activation(out=ptail[:, g, :], in_=banks[bi][:, 496:512], func=AF.Copy),
                     reads=[bankB[bi]], writes=[ptailB], append=(g > 0))

        def phase_B(l, sgi, save_tail):
            pbuf = [f32v(r2(0, 4128)), f32v(r1(16384, 4128))]
            s_a = f32v(r2(4128, 4128))
            s_b = f32v(r1(0, 4128))
            dg = [r2(8256, 2048), r1(16384 + 4128, 2048)]
            sgl = [f32v(r2(10304, 1024)), f32v(r2(11328, 1024))]
            W = SGT + 16
            pbB = [Buf("pbuf0"), Buf("pbuf1")]
            saB = Buf("s_a")
            sbB = Buf("s_b")
            dgB = [Buf("dg0"), Buf("dg1")]
            sgB = [Buf("sg0"), Buf("sg1")]
            t16B = Buf("t16")
            slots_g = {}

            def front(g):
                q = g % 2
                slots_g[g] = wload([win_piece(l, OFF["p_in"] + 128 * g, 128, 0), win_piece(l, OFF["p_gate"] + 128 * g, 128, 128)])
                sl, sB = slots_g[g]
                P.op("dve", lambda v: v.tensor_copy(out=pbuf[q][:, 0:16], in_=ptail[:, g, :]), reads=[ptailB], writes=[pbB[q]])
                for tg in range(NTG):
                    bi = 6 + (tg % 2)
                    proj_group(bi, sl, sB, 0, 128, tg)
                    P.op("act", lambda a, tg=tg, bi=bi: a.activation(out=pbuf[q][:, 16 + tg * TG:16 + (tg + 1) * TG], in_=banks[bi][:, :], func=AF.Copy),
                         reads=[bankB[bi]], writes=[pbB[q]], append=True)

            def back(g):
                q = g % 2
                w = 2 ** (g + 1)
                sl, sB = slots_g[g]
                pb = pbuf[q]
                P.op("dve", lambda v: v.tensor_tensor(out=s_a[:, 1:W], in0=pb[:, 1:W], in1=pb[:, 0:W - 1], op=ALU.add),
                     reads=[pbB[q]], writes=[saB])
                S, SB_ = s_a, saB
                if g >= 1:
                    P.op("dve", lambda v: v.tensor_tensor(out=s_b[:, 3:W], in0=s_a[:, 3:W], in1=s_a[:, 1:W - 2], op=ALU.add),
                         reads=[saB], writes=[sbB])
                    S, SB_ = s_b, sbB
                if g >= 2:
                    P.op("dve", lambda v: v.tensor_tensor(out=s_a[:, 7:W], in0=s_b[:, 7:W], in1=s_b[:, 3:W - 4], op=ALU.add),
                         reads=[sbB], writes=[saB])
                    S, SB_ = s_a, saB
                if g >= 3:
                    P.op("dve", lambda v: v.tensor_tensor(out=s_b[:, 15:W], in0=s_a[:, 15:W], in1=s_a[:, 7:W - 8], op=ALU.add),
                         reads=[saB], writes=[sbB])
                    S, SB_ = s_b, sbB
                P.op("dve", lambda v, S=S: v.scalar_tensor_tensor(out=dg[q][:, :], in0=S[:, 16:W], scalar=1.0 / w, in1=pb[:, 16:W],
                                                                  op0=ALU.mult, op1=ALU.subtract), reads=[SB_, pbB[q]], writes=[dgB[q]])
                P.op("dve", lambda v, S=S: v.tensor_tensor(out=t16[:], in0=S[:, 16:32], in1=rc_t[:, sgi, g * 16:(g + 1) * 16], op=ALU.mult),
                     reads=[SB_], writes=[t16B])
                P.op("dve", lambda v: v.tensor_tensor(out=dg[q][:, 0:16], in0=t16[:], in1=pb[:, 16:32], op=ALU.subtract),
                     reads=[t16B, pbB[q]], writes=[dgB[q]], append=True)
                if save_tail:
                    P.op("dve", lambda v: v.tensor_copy(out=ptail[:, g, :], in_=pb[:, SGT:SGT + 16]), reads=[pbB[q]], writes=[ptailB])
                for tg in range(NTG):
                    bg = 2 + (tg % 2)
                    by = 4 + (tg % 2)
                    proj_group(bg, sl, sB, 128, 128, tg)
                    P.op("act", lambda a, tg=tg, bg=bg: a.activation(out=sgl[tg % 2], in_=banks[bg][:, :], func=AF.Silu),
                         reads=[bankB[bg]], writes=[sgB[tg % 2]])
                    mm_group(by, [(banks[by][:, :], poolw[:, g, :], dg[q][:, tg * TG:(tg + 1) * TG], True, True)], reads=[dgB[q], layerB])
                    P.op("dve", lambda v, tg=tg, by=by: v.scalar_tensor_tensor(
                        out=yT[:, 8 + g, tg * TG:(tg + 1) * TG], in0=banks[by][:, :], scalar=psc_col[:, l, g:g + 1], in1=sgl[tg % 2],
                        op0=ALU.mult, op1=ALU.mult), reads=[bankB[by], sgB[tg % 2]], writes=[ybuf(1, g, tg)])

            front(0)
            front(1)
            back(0)
            front(2)
            back(1)
            front(3)
            back(2)
            back(3)
            P.barrier()

        def phase_A(l):
            vg = [f32v(r2(1024 * i, 1024)) for i in range(4)]
            vt = vg
            vln = [r2(4096 + 512 * i, 512) for i in range(4)]
            gu = [f32v(r2(6144 + 1024 * i, 1024)) for i in range(4)]
            sgl = [f32v(r2(10240 + 1024 * i, 1024)) for i in range(4)]
            t1 = [f32v(r2(14336, 1024)), f32v(r2(15360, 1024))]
            slV, sBV = wload([win_piece(l, OFF["a_v"], 512, 0)])
            slU, sBU = wload([win_piece(l, OFF["a_u"], 512, 0)])
            slG, sBG = wload([win_piece(l, OFF["a_gate"], 512, 0)])
            vgB = [Buf("vg%d" % i) for i in range(4)]
            vtB = vgB
            vlnB = [Buf("vln%d" % i) for i in range(4)]
            stB = [Buf("st%d" % i) for i in range(4)]
            guB = [Buf("gu%d" % i) for i in range(4)]
            sgB = [Buf("sg%d" % i) for i in range(4)]
            t1B = [Buf("t10"), Buf("t11")]

            def rec_front(idx, tg, tt, p):
                c0 = tg * TG + tt * 128
                bv = 6 + (idx % 2)
                mm_group(bv, [(banks[bv][:, :], hT[:, k, c0:c0 + 128], slV[:, k, :], k == 0, k == 7) for k in range(8)],
                         reads=[sBV, hTB[tg]])
                P.op("act", lambda a: a.activation(out=vg[p], in_=banks[bv][:, :], func=AF.Gelu),
                     reads=[bankB[bv]], writes=[vgB[p]])
                P.op("dve", lambda v: v.bn_stats(out=bnst[:, p, :], in_=vg[p]), reads=[vgB[p]], writes=[stB[p]])
                P.op("dve", lambda v: v.bn_aggr(out=bnmv[:, p, :], in_=bnst[:, p, :]), reads=[stB[p]], writes=[stB[p]])
                P.op("act", lambda a: a.activation(out=bnrs[:, p, :], in_=bnmv[:, p, 1:2], func=AF.Sqrt, bias=epst[:], scale=1.0),
                     reads=[stB[p]], writes=[stB[p]])
                P.op("dve", lambda v: v.reciprocal(out=bnrs[:, p, :], in_=bnrs[:, p, :]), reads=[stB[p]], writes=[stB[p]])
                P.op("dve", lambda v: v.tensor_scalar(out=vt[p], in0=vg[p], scalar1=bnmv[:, p, 0:1], scalar2=bnrs[:, p, 0:1],
                                                      op0=ALU.subtract, op1=ALU.mult), reads=[vgB[p], stB[p]], writes=[vtB[p]])
                P.op("dve", lambda v: v.tensor_tensor(out=vt[p], in0=vt[p], in1=lng_b[:], op=ALU.mult),
                     reads=[vtB[p], layerB], writes=[vtB[p]])
                P.op("dve", lambda v: v.tensor_tensor(out=vln[p], in0=vt[p], in1=lnb_b[:], op=ALU.add),
                     reads=[vtB[p], layerB], writes=[vlnB[p]])

            def rec_back(idx, tg, tt, p):
                for h in range(4):
                    mm_group(h, [(banks[h][:, tt * 128:(tt + 1) * 128], vln[p][:, h * 128:(h + 1) * 128], ws_bf[:, h, :], True, True)],
                             reads=[vlnB[p], layerB], append=(tt > 0))
                if tt < 3:
                    return
                for h in range(4):
                    bu = 4 + (h % 2)
                    proj_group(bu, slU, sBU, h * 128, 128, tg)
                    P.op("act", lambda a, h=h, bu=bu: a.activation(out=gu[h], in_=banks[bu][:, :], func=AF.Gelu), reads=[bankB[bu]], writes=[guB[h]])
                for h in range(4):
                    bg = 4 + (h % 2)
                    proj_group(bg, slG, sBG, h * 128, 128, tg)
                    P.op("act", lambda a, h=h, bg=bg: a.activation(out=sgl[h], in_=banks[bg][:, :], func=AF.Silu), reads=[bankB[bg]], writes=[sgB[h]])
                for h in range(4):
                    q = h % 2
                    bsv = bs_b[:, h * 128:(h + 1) * 128].unsqueeze(1).to_broadcast([128, 4, 128])
                    P.op("dve", lambda v, q=q, h=h, bsv=bsv: v.tensor_tensor(
                        out=t1[q].rearrange("p (a b) -> p a b", a=4), in0=banks[h][:, :].rearrange("p (a b) -> p a b", a=4), in1=bsv, op=ALU.add),
                        reads=[bankB[h], layerB], writes=[t1B[q]])
                    P.op("dve", lambda v, q=q, h=h: v.tensor_tensor(out=t1[q], in0=t1[q], in1=gu[h], op=ALU.mult), reads=[t1B[q], guB[h]], writes=[t1B[q]])
                    P.op("dve", lambda v, q=q, h=h: v.tensor_tensor(out=yT[:, 4 + h, tg * TG:(tg + 1) * TG], in0=t1[q], in1=sgl[h], op=ALU.mult),
                         reads=[t1B[q], sgB[h]], writes=[ybuf(0, h, tg)])

            pend = []
            idx = 0
            for tg in range(NTG):
                for tt in range(4):
                    p = idx % 4
                    rec_front(idx, tg, tt, p)
                    pend.append((idx, tg, tt, p))
                    idx += 1
                    if len(pend) > 3:
                        rec_back(*pend.pop(0))
            while pend:
                rec_back(*pend.pop(0))
            P.barrier()

        def layer_setup(l):
            P.dma("sp", lng_b[:], ln_g[l:l + 1, :].partition_broadcast(128), writes=[layerB])
            P.dma("sp", lnb_b[:], ln_b[l:l + 1, :].partition_broadcast(128), writes=[layerB], append=True)
            P.dma("sp", bs_b[:], gm_bs[l:l + 1, :].partition_broadcast(128), writes=[layerB], append=True)
            P.dma("sp", ws_f[:], wsT[l].rearrange("h s t -> s h t"), writes=[layerB], append=True)
            P.dma("pool", poolw[:], pool_w[l].rearrange("g i o -> i g o"), writes=[layerB], append=True)
            for h in range(4):
                P.op("dve", lambda v, h=h: v.tensor_tensor(out=ws_bf[:, h, :], in0=ws_f[:, h, :], in1=mk_b[:, 0, 0:128], op=ALU.mult),
                     reads=[layerB], writes=[layerB], append=True)
            if not recompute_prev:
                P.op("dve", lambda v: v.memset(ptail[:], 0.0), writes=[ptailB])
            memfB2 = Buf("memf2")
            P.dma("sp", memf, memT.rearrange("(k p) m -> p k m", p=128), writes=[memfB2])
            for k in range(8):
                P.op("dve", lambda v, k=k: v.scalar_tensor_tensor(out=mn[:, k, :], in0=memf[:, k, :], scalar=gm_col[:, l, k:k + 1], in1=rstd_m[:],
                                                                   op0=ALU.mult, op1=ALU.mult), reads=[memfB2, memB], writes=[memB], append=True)
            slK, sBK = wload([(lambda sl: sl[:, :, :], w_kv[l, :, 0:512].rearrange("(k p) n -> p k n", p=128))])
            slVv, sBVv = wload([(lambda sl: sl[:, :, :], w_kv[l, :, 512:1024].rearrange("(k p) n -> p k n", p=128))])
            for h in range(4):
                bi = 6 + (h % 2)
                mm_group(bi, [(banks[bi][:, 0:256], slK[:, k, h * 128:(h + 1) * 128], mn[:, k, :], k == 0, k == 7) for k in range(8)],
                         reads=[sBK, memB])
                P.op("act", lambda a, h=h, bi=bi: a.activation(out=kmT[:, h, :], in_=banks[bi][:, 0:256], func=AF.Copy),
                     reads=[bankB[bi]], writes=[memB], append=True)
            for mb in range(2):
                bi = 4 + mb
                mm_group(bi, [(banks[bi][:, :], mn[:, k, mb * 128:(mb + 1) * 128], slVv[:, k, :], k == 0, k == 7) for k in range(8)],
                         reads=[sBVv, memB])
                P.op("act", lambda a, mb=mb, bi=bi: a.activation(out=vm[:, mb, :], in_=banks[bi][:, :], func=AF.Copy),
                     reads=[bankB[bi]], writes=[memB], append=True)
            P.barrier()

        def phase_M(l):
            qs = [r2(512 * i, 512) for i in range(3)]
            Ev = [r2(1536 + 1024 * i, 1024).rearrange("p (m t) -> p m t", m=2) for i in range(3)]
            rec = [f32v(r2(4608 + 1024 * i, 1024)) for i in range(2)]
            sgl = [f32v(r2(6656 + 1024 * i, 1024)) for i in range(3)]
            tt_ = [f32v(r2(9728 + 1024 * i, 1024)) for i in range(2)]
            qsB = [Buf("qs%d" % i) for i in range(3)]
            EvB = [Buf("E%d" % i) for i in range(3)]
            recB = [Buf("r0"), Buf("r1")]
            sgB = [Buf("s%d" % i) for i in range(3)]
            ttB = [Buf("t0"), Buf("t1")]
            slots_h = {}

            def stA(i, h, tg):
                p3 = i % 3
                if tg == 0:
                    slots_h[h] = wload([win_piece(l, OFF["m_q"] + 128 * h, 128, 0), win_piece(l, OFF["m_gate"] + 128 * h, 128, 128)])
                sl, sB = slots_h[h]
                bq = 4 + (i % 2)
                bg = 6 + (i % 2)
                proj_group(bq, sl, sB, 0, 128, tg)
                P.op("dve", lambda v: v.tensor_scalar_mul(out=qs[p3], in0=banks[bq][:, :], scalar1=QSCALE),
                     reads=[bankB[bq]], writes=[qsB[p3]])
                proj_group(bg, sl, sB, 128, 128, tg)
                P.op("act", lambda a: a.activation(out=sgl[p3], in_=banks[bg][:, :], func=AF.Silu), reads=[bankB[bg]], writes=[sgB[p3]])

            def stB_(i, h, tg):
                p3 = i % 3
                for mb in range(2):
                    mm_group(mb, [(banks[mb][:, :], kmT[:, h, mb * 128:(mb + 1) * 128], qs[p3], True, True)], reads=[qsB[p3], memB])
                    P.op("act", lambda a, mb=mb: a.activation(out=Ev[p3][:, mb, :], in_=banks[mb][:, :], func=AF.Exp),
                         reads=[bankB[mb]], writes=[EvB[p3]], append=(mb > 0))

            def stC(i, h, tg):
                p3 = i % 3
                p = i % 2
                mm_group(2, [(banks[2][:, :], vm[:, mb, h * 128:(h + 1) * 128], Ev[p3][:, mb, :], mb == 0, mb == 1) for mb in range(2)],
                         reads=[EvB[p3], memB])
                mm_group(3, [(banks[3][:, :], ones_bf[:], Ev[p3][:, mb, :], mb == 0, mb == 1) for mb in range(2)], reads=[EvB[p3]])
                P.op("act", lambda a: a.activation(out=rec[p], in_=banks[3][:, :], func=AF.Ln), reads=[bankB[3]], writes=[recB[p]])
                P.op("act", lambda a: a.activation(out=rec[p], in_=rec[p], func=AF.Exp, scale=-1.0), reads=[recB[p]], writes=[recB[p]])
                P.op("dve", lambda v: v.tensor_tensor(out=tt_[p], in0=banks[2][:, :], in1=rec[p], op=ALU.mult),
                     reads=[bankB[2], recB[p]], writes=[ttB[p]])
                P.op("dve", lambda v: v.tensor_tensor(out=yT[:, 12 + h, tg * TG:(tg + 1) * TG], in0=tt_[p], in1=sgl[p3], op=ALU.mult),
                     reads=[ttB[p], sgB[p3]], writes=[ybuf(3, h, tg)])

            its = [(i, i // 4, i % 4) for i in range(16)]
            for i in range(16 + 2):
                if i < 16:
                    stA(*its[i])
                if 0 <= i - 1 < 16:
                    stB_(*its[i - 1])
                if 0 <= i - 2 < 16:
                    stC(*its[i - 2])
            P.barrier()

        zT = R2[:, :].rearrange("p (c t) -> p c t", c=8)
        zB = {}

        def phase_G(l):
            sgt = [xt0[:, 0:512], xt0[:, 512:1024]]
            zacc = [xt0[:, 1024:1536], xt0[:, 1536:2048]]
            tmp = [xt1[:, 0:512], xt1[:, 512:1024]]
            sgtB = [Buf("sgt0"), Buf("sgt1")]
            zaB = [Buf("za0"), Buf("za1")]
            tmB = [Buf("tm0"), Buf("tm1")]
            it = 0
            for cc in range(8):
                slG, sBG = wload([win_piece(l, OFF["g_merge"] + b * 1024 + cc * 128, 128, b * 128) for b in range(4)])
                slW, sBW = wload([(lambda sl, b=b: sl[:, 0:4, b * 128:(b + 1) * 128],
                                   w_br[l, b, :, cc * 128:(cc + 1) * 128].rearrange("(k p) n -> p k n", p=128)) for b in range(4)])
                for tg in range(NTG):
                    zp = it % 2
                    it += 1
                    for b in range(4):
                        p = b % 2
                        gb = 0 + p
                        yb = 2 + p
                        proj_group(gb, slG, sBG, b * 128, 128, tg)
                        yi = YIDX[b]
                        mm_group(yb, [(banks[yb][:, :], slW[:, kc, b * 128:(b + 1) * 128], yT[:, yi * 4 + kc, tg * TG:(tg + 1) * TG], kc == 0, kc == 3)
                                      for kc in range(4)], reads=[sBW] + [ybuf(b, kc, tg) for kc in range(4)])
                        P.op("act", lambda a, p=p, gb=gb: a.activation(out=sgt[p], in_=banks[gb][:, :], func=AF.Sigmoid),
                             reads=[bankB[gb]], writes=[sgtB[p]])
                        if b == 0:
                            P.op("dve", lambda v, p=p, yb=yb, zp=zp: v.tensor_tensor(out=zacc[zp], in0=banks[yb][:, :], in1=sgt[p], op=ALU.mult),
                                 reads=[bankB[yb], sgtB[p]], writes=[zaB[zp]])
                        else:
                            P.op("dve", lambda v, p=p, yb=yb: v.tensor_tensor(out=tmp[p], in0=banks[yb][:, :], in1=sgt[p], op=ALU.mult),
                                 reads=[bankB[yb], sgtB[p]], writes=[tmB[p]])
                            if b < 3:
                                P.op("dve", lambda v, p=p, zp=zp: v.tensor_tensor(out=zacc[zp], in0=zacc[zp], in1=tmp[p], op=ALU.add),
                                     reads=[tmB[p], zaB[zp]], writes=[zaB[zp]])
                            else:
                                zB[(cc, tg)] = Buf(f"z{cc}_{tg}")
                                P.op("dve", lambda v, p=p, zp=zp, cc=cc, tg=tg: v.tensor_tensor(
                                    out=zT[:, cc, tg * TG:(tg + 1) * TG], in0=zacc[zp], in1=tmp[p], op=ALU.add),
                                    reads=[tmB[p], zaB[zp]], writes=[zB[(cc, tg)]])
            P.barrier()

        def phase_O(l, xsrc, xdst, t0):
            xq = [xt0[:, :].rearrange("p (k t) -> p k t", k=4), xt1[:, :].rearrange("p (k t) -> p k t", k=4)]
            slO = []
            for s_ in range(2):
                slO.append(wload([(lambda sl: sl[:, :, :], w_out[l, :, s_ * 512:(s_ + 1) * 512].rearrange("(k p) n -> p k n", p=128))]))
            steps = [(tg, half) for tg in range(NTG) for half in range(2)]

            def tile_of(i):
                tg, half = steps[i]
                return (t0 // 512 + tg) * 2 + half
            for i, (tg, half) in enumerate(steps):
                p = i % 2
                if i == 0:
                    P.dma("sp", xtf[0], xsrc[tile_of(0)], writes=[xthB[0]])
                if i + 1 < 8:
                    P.dma("sp", xtf[1 - p], xsrc[tile_of(i + 1)], writes=[xthB[1 - p]])
                sl, sB = slO[half]
                for kk in range(4):
                    bi = 2 + kk
                    mm_group(bi, [(banks[bi][:, :], sl[:, cc, kk * 128:(kk + 1) * 128], zT[:, cc, tg * TG:(tg + 1) * TG], cc == 0, cc == 7)
                                  for cc in range(8)], reads=[sB] + [zB[(cc, tg)] for cc in range(8)])
                    P.op("dve", lambda v, kk=kk, bi=bi, p=p: v.tensor_tensor(out=xq[p][:, kk, :], in0=banks[bi][:, :], in1=xq[p][:, kk, :], op=ALU.add),
                         reads=[bankB[bi], xthB[p]], writes=[xthB[p]], append=True)
                P.dma("sp", xdst[tile_of(i)], xtf[p], reads=[xthB[p]])
            P.barrier()

        def phase_F(xsrc):
            sqo = [hT[:, 0, :].rearrange("p (k t) -> p k t", k=8), hT[:, 1, :].rearrange("p (k t) -> p k t", k=8)]
            nst = NT // 256
            for s_ in range(nst):
                p = s_ % 2
                c0 = s_ * 256
                if s_ == 0:
                    x_load256(xsrc, 0, 0)
                if s_ + 1 < nst:
                    x_load256(xsrc, c0 + 256, 1 - p)
                P.op("act", lambda a, p=p: a.activation(out=sqo[p], in_=xth[p], func=AF.Square), reads=[xthB[p]], writes=[sqBs[p]])
                mm_group(p, [(banks[p][:, 0:256], ones_bf[:], sqo[p][:, k, :], k == 0, k == 7) for k in range(8)], reads=[sqBs[p]])
                P.op("act", lambda a, p=p: a.activation(out=rsv[p], in_=banks[p][:, 0:256], func=AF.Ln, bias=epst[:], scale=1.0 / D),
                     reads=[bankB[p]], writes=[rsB[p]])
                P.op("act", lambda a, p=p: a.activation(out=rsv[p], in_=rsv[p], func=AF.Exp, scale=-0.5), reads=[rsB[p]], writes=[rsB[p]])
                for k in range(8):
                    P.op("dve", lambda v, k=k, p=p: v.scalar_tensor_tensor(out=xth[p][:, k, :], in0=xth[p][:, k, :], scalar=gf_col[:, k:k + 1], in1=rsv[p],
                                                                            op0=ALU.mult, op1=ALU.mult), reads=[rsB[p], xthB[p]], writes=[xthB[p]], append=(k > 0))
                P.dma("sp", oT[s_], xtf[p], reads=[xthB[p]])
            P.barrier()

        for l in range(L):
            layer_setup(l)
            if recompute_prev:
                xsrc, xdst = xT, xdst_t
            else:
                xsrc = xT if l == 0 else xdst_t
                xdst = xdst_t
            for sgi in range(NSG):
                P.new_epoch()
                t0 = sgi * SGT
                if recompute_prev:
                    phase_P0(l, xpT, 0)
                    P.barrier()
                    phase_prev_recompute(l)
                    P.barrier()
                kv0 = kv_begin(l, 0)
                phase_P0(l, xsrc, t0, after_tg=lambda tg, kv0=kv0: kv_tg(kv0[0], kv0[1], tg))
                has_prev = recompute_prev or sgi > 0
                phase_C(l, has_prev, save_kv=(not recompute_prev and sgi < NSG - 1), kv0_done=True)
                phase_B(l, sgi, save_tail=(not recompute_prev and sgi < NSG - 1))
                phase_A(l)
                phase_M(l)
                phase_G(l)
                phase_O(l, xsrc, xdst, t0)
        phase_F(xdst_t)
        P.barrier(full=True)

        @block.tensor
        def _(pe):
            for f in P.q["pe"]:
                f(pe)

        @block.scalar
        def _(act):
            for f in P.q["act"]:
                f(act)

        @block.vector
        def _(dve):
            for f in P.q["dve"]:
                f(dve)

        @block.gpsimd
        def _(pool):
            for f in P.q["pool"]:
                f(pool)

        @block.sync
        def _(sp):
            for f in P.q["sp"]:
                f(sp)
    return nc


_PROG_CACHE = {}


def _get_prog(L, NSG, recompute_prev):
    key = (L, NSG, recompute_prev)
    if key not in _PROG_CACHE:
        _PROG_CACHE[key] = build_program(L, NSG, recompute_prev)
    return _PROG_CACHE[key]


def _masks(pv):
    k = np.arange(128)[:, None]
    q = np.arange(128)[None, :]
    cur = (k <= q).astype(np.float32)
    A = (k >= q).astype(np.float32)
    X = A * np.float32(pv)
    m = np.zeros((4, 128, 512), np.float32)
    m[0] = np.concatenate([cur, cur, A, A], axis=1)
    m[1] = np.concatenate([cur, cur, A, X], axis=1)
    m[2] = np.concatenate([cur, cur, X, X], axis=1)
    m[3, :, 0:128] = np.eye(128, dtype=np.float32)
    return m


def _rc(start):
    rc = np.zeros((128, 64), np.float32)
    for g, w in enumerate((2, 4, 8, 16)):
        for t in range(16):
            cnt = min(t + 1, w) if start else w
            rc[:, g * 16 + t] = 1.0 / cnt
    return rc


def _tile_x(a):
    T = a.shape[0]
    return np.ascontiguousarray(a.reshape(T // 256, 256, 8, 128).transpose(0, 3, 2, 1)).reshape(T // 256, 128, 2048)


def _tile_x_in(a):
    T = a.shape[0]
    return np.ascontiguousarray(a.reshape(T // 512, 512, 2, 4, 128).transpose(0, 2, 4, 3, 1)).reshape(T // 256, 128, 2048)


def _untile_x_in(a):
    n = a.shape[0] // 2
    return np.ascontiguousarray(a.reshape(n, 2, 128, 4, 512).transpose(0, 4, 1, 3, 2)).reshape(n * 512, 1024)


def _untile_x(a):
    n = a.shape[0]
    return np.ascontiguousarray(a.reshape(n, 128, 8, 256).transpose(0, 3, 2, 1)).reshape(n * 256, 1024)


def kernel(x, mem, norm_g, w_in, gm_ln_g, gm_ln_b, gm_ws, gm_bs, pool_w, pool_scale,
           mem_norm_g, w_mem_kv, w_branch, w_out, final_norm_g):
    f = lambda a: np.ascontiguousarray(np.asarray(a, dtype=np.float32))
    x = f(x); mem = f(mem)
    B, S, _ = x.shape
    wsT_all = np.ascontiguousarray(np.transpose(f(gm_ws), (0, 1, 3, 2)))
    gm_bs_f = f(gm_bs).reshape(DEPTH, 512)
    def colv(a, nk):
        a = f(a)
        return np.ascontiguousarray(a.reshape(a.shape[0], nk, 128).transpose(2, 0, 1))
    common = dict(w_in=f(w_in), w_kv=f(w_mem_kv), w_br=f(w_branch), w_out=f(w_out), pool_w=f(pool_w), wsT=wsT_all,
                  ln_g=f(gm_ln_g), ln_b=f(gm_ln_b), gm_bs=gm_bs_f)
    colc = dict(norm_g=colv(norm_g, 8), mem_g=colv(mem_norm_g, 8), pscale=colv(pool_scale, 4))
    fin = np.ascontiguousarray(f(final_norm_g).reshape(8, 128).T)
    out = np.empty((B, S, D), np.float32)
    if MODE == "V4":
        nc = _get_prog(DEPTH, 2, False)
        in_maps = []
        for b in range(B):
            m = dict(common)
            m.update(colc)
            m.update(xT=_tile_x_in(x[b]), memT=np.ascontiguousarray(mem[b].T), fin_g=fin,
                     masks=_masks(1.0), rc=np.stack([_rc(True), _rc(False)]))
            in_maps.append(m)
        res = run_bass_kernel_spmd(nc, in_maps, core_ids=list(range(B)))
        for b in range(B):
            out[b] = _untile_x(np.asarray(res.results[b]["oT"]))
        return out
    nc = _get_prog(1, 1, True)
    xcur = [_tile_x_in(x[b]) for b in range(B)]
    zeros_prev = np.zeros((8, 128, 2048), np.float32)
    for l in range(DEPTH):
        in_maps = []
        for c in range(8):
            b, hf = c // 2, c % 2
            m = {k: np.ascontiguousarray(v[l:l + 1]) for k, v in common.items()}
            m.update({k: np.ascontiguousarray(v[:, l:l + 1]) for k, v in colc.items()})
            m.update(xT=np.ascontiguousarray(xcur[b][hf * 8:(hf + 1) * 8]),
                     xpT=(np.ascontiguousarray(xcur[b][0:8]) if hf == 1 else zeros_prev),
                     memT=np.ascontiguousarray(mem[b].T), fin_g=fin,
                     masks=_masks(float(hf)), rc=_rc(hf == 0)[None])
            in_maps.append(m)
        res = run_bass_kernel_spmd(nc, in_maps, core_ids=list(range(8)))
        if l < DEPTH - 1:
            xcur = [np.concatenate([np.asarray(res.results[2 * b]["xo"]), np.asarray(res.results[2 * b + 1]["xo"])], axis=0) for b in range(B)]
        else:
            for b in range(B):
                out[b] = _untile_x(np.concatenate([np.asarray(res.results[2 * b]["oT"]), np.asarray(res.results[2 * b + 1]["oT"])], axis=0))
    return out
```

```python
import numpy as np
from contextlib import ExitStack
import concourse.bass as bass
import concourse.mybir as mybir
from concourse.bass_utils import run_bass_kernel_spmd

MODE = "V4"

F32 = mybir.dt.float32
BF16 = mybir.dt.bfloat16
AF = mybir.ActivationFunctionType
ALU = mybir.AluOpType

DEPTH = 4
D = 1024
DIN = 10752
SGT = 2048
TG = 512
NTG = 4
EPS = 1e-6
OFF = dict(a_u=0, a_v=512, a_gate=1024, p_in=1536, p_gate=2048, c_q=2560, c_k=4096, c_v=4608,
           c_gate=5120, m_q=5632, m_gate=6144, g_merge=6656)
QSCALE = 128 ** -0.5
NB = 4
NDS = 40
YIDX = {0: 1, 1: 2, 2: 0, 3: 3}


class Buf:
    __slots__ = ("name", "lw", "rd")

    def __init__(self, name):
        self.name = name
        self.lw = []
        self.rd = {}


class Prog:
    ENG = ("pe", "act", "dve", "pool", "sp")

    def __init__(self, nc, ES):
        self.nc = nc
        self.ES = ES
        self.q = {n: [] for n in self.ENG}
        self.sem = {}
        self.cnt = {}
        self.key = {}
        self.waited = {n: {} for n in self.ENG}
        self.last_tok = {n: None for n in self.ENG}
        self.nkeys = 0
        self.dma_sems = []
        for i in range(NDS):
            self.dma_sems.append((self._newkey(), ES.enter_context(nc.semaphore(f"dq{i}"))))
        self.dma_val = [0] * NDS
        self.dma_rr = 0
        self.dma_recent = []
        self.n_ep = 0
        self.new_epoch()

    def _newkey(self):
        self.nkeys += 1
        return self.nkeys

    def new_epoch(self):
        for n in ("pe", "act", "dve"):
            self.sem[n] = self.ES.enter_context(self.nc.semaphore(f"e{self.n_ep}_{n}"))
            self.cnt[n] = 0
            self.key[n] = self._newkey()
        self.n_ep += 1

    def _waits(self, qn, toks, skip_self=False):
        best = {}
        for t in toks:
            if t is None:
                continue
            k, s, v = t
            if skip_self and qn in self.key and k == self.key[qn]:
                continue
            if k not in best or best[k][1] < v:
                best[k] = (s, v)
        out = []
        wd = self.waited[qn]
        for k, (s, v) in best.items():
            if wd.get(k, 0) >= v:
                continue
            wd[k] = v
            out.append((s, v))
        return out

    def _deps(self, reads, writes, after, append):
        toks = list(after)
        for b in reads:
            toks.extend(b.lw)
        for b in writes:
            if not append:
                toks.extend(b.lw)
            toks.extend(b.rd.values())
        return toks

    def _update(self, tok, reads, writes, append):
        for b in writes:
            if append:
                b.lw.append(tok)
            else:
                b.lw = [tok]
                b.rd = {}
        for b in reads:
            k = tok[0]
            if k not in b.rd or b.rd[k][2] < tok[2]:
                b.rd[k] = tok

    def op(self, qn, fn, reads=(), writes=(), after=(), append=False):
        toks = self._deps(reads, writes, after, append)
        ws = self._waits(qn, toks, skip_self=(qn == "pe"))
        self.cnt[qn] += 1
        sem = self.sem[qn]
        tok = (self.key[qn], sem, self.cnt[qn])

        def run(eng, ws=ws, fn=fn, sem=sem):
            for (s, v) in ws:
                eng.wait_ge(s, v)
            fn(eng).then_inc(sem, 1)
        self.q[qn].append(run)
        self._update(tok, reads, writes, append)
        self.last_tok[qn] = tok
        return tok

    def dma(self, qn, out_ap, in_ap, reads=(), writes=(), after=(), append=False, track=False):
        toks = self._deps(reads, writes, after, append)
        i = self.dma_rr
        self.dma_rr = (i + 1) % NDS
        k, sem = self.dma_sems[i]
        prev = self.dma_val[i]
        self.dma_val[i] += 16
        val = self.dma_val[i]
        if prev > 0:
            toks.append((k, sem, prev))
        ws = self._waits(qn, toks)
        tok = (k, sem, val)

        def run(eng, ws=ws, sem=sem, out_ap=out_ap, in_ap=in_ap):
            for (s, v) in ws:
                eng.wait_ge(s, v)
            eng.dma_start(out=out_ap, in_=in_ap).then_inc(sem, 16)
        self.q[qn].append(run)
        self._update(tok, reads, writes, append)
        if qn != "pool" or track:
            self.dma_recent.append(tok)
        return tok

    def barrier(self, full=False):
        toks = [self.last_tok[n] for n in ("pe", "act", "dve")] + self.dma_recent
        for qn in (("pe", "act", "dve", "sp") if full else ("act", "dve", "sp")):
            ws = self._waits(qn, toks, skip_self=True)
            if not ws:
                continue

            def run(eng, ws=ws):
                for (s, v) in ws:
                    eng.wait_ge(s, v)
            self.q[qn].append(run)
        self.dma_recent = []


def build_program(L, NSG, recompute_prev):
    nc = bass.Bass("TRN2", target_bir_lowering=False)
    NT = SGT * NSG

    def dram(name, shape, dt=F32, kind="ExternalInput"):
        return nc.dram_tensor(name, shape, dt, kind=kind).ap()

    NTL = NT // 256
    xT = dram("xT", [NTL, 128, 2048])
    xpT = dram("xpT", [8, 128, 2048]) if recompute_prev else None
    memT = dram("memT", [D, 256])
    w_in = dram("w_in", [L, D, DIN])
    w_kv = dram("w_kv", [L, D, 1024])
    w_br = dram("w_br", [L, 4, 512, D])
    w_out = dram("w_out", [L, D, D])
    pool_w = dram("pool_w", [L, 4, 128, 128])
    wsT = dram("wsT", [L, 4, 128, 128])
    norm_g = dram("norm_g", [128, L, 8])
    mem_g = dram("mem_g", [128, L, 8])
    ln_g = dram("ln_g", [L, 512])
    ln_b = dram("ln_b", [L, 512])
    gm_bs = dram("gm_bs", [L, 512])
    pscale = dram("pscale", [128, L, 4])
    fin_g = dram("fin_g", [128, 8])
    masks = dram("masks", [4, 128, 512])
    rcin = dram("rc", [NSG, 128, 64])
    oT = dram("oT", [NTL, 128, 2048], kind="ExternalOutput")
    if recompute_prev:
        xdst_t = dram("xo", [NTL, 128, 2048], kind="ExternalOutput")
    else:
        xdst_t = dram("xs", [NTL, 128, 2048], kind="Internal")
    kvprev = dram("kvprev", [4, 2, 128, SGT], BF16, kind="Internal")

    ES = ExitStack()
    with ES:
        def sb(name, shape, dt):
            return ES.enter_context(nc.sbuf_tensor(name, shape, dt))

        hT = sb("hT", [128, 8, SGT], BF16)
        yT = sb("yT", [128, 16, SGT], BF16)
        R2 = sb("R2", [128, 16384], BF16)
        slots = [sb(f"ws{i}", [128, 8, 512], BF16) for i in range(NB)]
        xt0 = sb("xt0", [128, 2048], F32)
        xt1 = sb("xt1", [128, 2048], F32)
        rstd = sb("rstd", [128, 512], F32)
        ones_bf = sb("ones_bf", [128, 128], BF16)
        ident_bf = sb("ident_bf", [128, 128], BF16)
        mk_b = sb("mk_b", [128, 3, 512], BF16)
        epst = sb("epst", [128, 1], F32)
        g_col = sb("g_col", [128, L, 8], F32)
        gm_col = sb("gm_col", [128, L, 8], F32)
        gf_col = sb("gf_col", [128, 8], F32)
        psc_col = sb("psc_col", [128, L, 4], F32)
        lng_b = sb("lng_b", [128, 512], F32)
        lnb_b = sb("lnb_b", [128, 512], F32)
        bs_b = sb("bs_b", [128, 512], F32)
        ws_f = sb("ws_f", [128, 4, 128], F32)
        ws_bf = sb("ws_bf", [128, 4, 128], BF16)
        poolw = sb("poolw", [128, 4, 128], BF16)
        rc_t = sb("rc_t", [128, NSG, 64], F32)
        ptail = sb("ptail", [128, 4, 16], F32)
        rstd_m = sb("rstd_m", [128, 256], F32)
        mn = sb("mn", [128, 8, 256], BF16)
        kmT = sb("kmT", [128, 4, 256], BF16)
        vm = sb("vm", [128, 2, 512], BF16)
        bnst = sb("bnst", [128, 4, 6], F32)
        bnmv = sb("bnmv", [128, 4, 2], F32)
        bnrs = sb("bnrs", [128, 4, 1], F32)
        t16 = sb("t16", [128, 16], F32)
        banks = [ES.enter_context(nc.psum_tensor(f"pb{i}", [128, 512], F32)) for i in range(8)]

        P = Prog(nc, ES)
        block = ES.enter_context(nc.Block())

        bankB = [Buf(f"bank{i}") for i in range(8)]
        slotB = [Buf(f"slot{i}") for i in range(NB)]
        hTB = [Buf(f"hT{i}") for i in range(NTG)]
        xtB = Buf("xt")
        sqB = Buf("sq")
        rstdB = Buf("rstd")
        constB = Buf("const")
        layerB = Buf("layerconst")
        memB = Buf("memkv")
        ptailB = Buf("ptail")
        kvprevB = [Buf(f"kvprev{h}") for h in range(4)]
        yB = {}

        def ybuf(b, ch, tg):
            key = (b, ch, tg)
            if key not in yB:
                yB[key] = Buf(f"y{key}")
            return yB[key]

        R1 = yT[:, 4:16, :].rearrange("p a t -> p (a t)")

        def r1(off, n):
            return R1[:, off:off + n]

        def r2(off, n):
            return R2[:, off:off + n]

        def f32v(ap):
            return ap.bitcast(F32)

        def ss(s0, d):
            return slice(s0, s0 + 127 * d + 1, d)

        wstate = {"next": 0}

        def wload(pieces):
            s = wstate["next"]
            wstate["next"] = (s + 1) % NB
            first = True
            for outf, in_ap in pieces:
                P.dma("pool", outf(slots[s]), in_ap, writes=[slotB[s]], append=not first)
                first = False
            return slots[s], slotB[s]

        def win_piece(l, c0, n, dst0):
            return (lambda sl, dst0=dst0, n=n: sl[:, :, dst0:dst0 + n],
                    w_in[l, :, c0:c0 + n].rearrange("(k p) n -> p k n", p=128))

        def mm_group(bank_i, mms, reads, append=False, extra_writes=()):
            def fn(pe, mms=mms):
                inst = None
                for (o, a, b, st, sp) in mms:
                    inst = pe.matmul(o, lhsT=a, rhs=b, start=st, stop=sp)
                return inst
            return P.op("pe", fn, reads=reads, writes=[bankB[bank_i]] + list(extra_writes), append=append)

        def proj_group(bank_i, slot_ap, slot_buf, c0, n, tg, ncols=TG, tcol0=None):
            t0 = tg * TG if tcol0 is None else tcol0
            mms = [(banks[bank_i][0:n, 0:ncols], slot_ap[:, k, c0:c0 + n], hT[:, k, t0:t0 + ncols],
                    k == 0, k == 7) for k in range(8)]
            return mm_group(bank_i, mms, reads=[slot_buf, hTB[tg]])

        P.dma("pool", mk_b[:], masks[0:3].rearrange("m p n -> p m n"), writes=[constB], track=True)
        P.dma("pool", ident_bf[:], masks[3, :, 0:128], writes=[constB], append=True, track=True)
        P.dma("sp", g_col[:], norm_g, writes=[constB], append=True)
        P.dma("sp", gm_col[:], mem_g, writes=[constB], append=True)
        P.dma("sp", gf_col[:], fin_g, writes=[constB], append=True)
        P.dma("sp", psc_col[:], pscale, writes=[constB], append=True)
        P.dma("sp", rc_t[:], rcin.rearrange("s p n -> p s n"), writes=[constB], append=True)
        P.op("dve", lambda v: v.memset(epst[:], EPS), writes=[Buf("eps")])
        P.op("dve", lambda v: v.memset(ones_bf[:], 1.0), writes=[Buf("ones")])
        P.barrier()

        memf = f32v(r2(0, 4096)).rearrange("p (k m) -> p k m", k=8)
        memsq = r2(4096, 2048).rearrange("p (k m) -> p k m", k=8)
        memfB = Buf("memf")
        P.dma("sp", memf, memT.rearrange("(k p) m -> p k m", p=128), writes=[memfB])
        P.op("act", lambda a: a.activation(out=memsq, in_=memf, func=AF.Square), reads=[memfB], writes=[sqB])
        mm_group(0, [(banks[0][:, 0:256], ones_bf[:], memsq[:, k, :], k == 0, k == 7) for k in range(8)], reads=[sqB])
        P.op("act", lambda a: a.activation(out=rstd_m[:], in_=banks[0][:, 0:256], func=AF.Sqrt, bias=epst[:], scale=1.0 / D),
             reads=[bankB[0]], writes=[memB])
        P.op("dve", lambda v: v.reciprocal(out=rstd_m[:], in_=rstd_m[:]), reads=[memB], writes=[memB])
        P.barrier()

        xtf = [xt0[:, :], xt1[:, :]]
        xth = [xt0[:, :].rearrange("p (k t) -> p k t", k=8), xt1[:, :].rearrange("p (k t) -> p k t", k=8)]
        xthB = [Buf("xth0"), Buf("xth1")]
        rsv = [rstd[:, 0:256], rstd[:, 256:512]]
        rsB = [Buf("rs0"), Buf("rs1")]
        sqBs = [Buf("sq0"), Buf("sq1")]

        def phase_P0(l, xsrc, t0):
            sqv = [yT[:, 0, :].rearrange("p (k t) -> p k t", k=8), yT[:, 1, :].rearrange("p (k t) -> p k t", k=8)]
            for s_ in range(8):
                p = s_ % 2
                c0 = t0 + s_ * 256
                cl = s_ * 256
                tg = s_ // 2
                P.dma("sp", xtf[p], xsrc[c0 // 256], writes=[xthB[p]])
                P.op("act", lambda a, p=p: a.activation(out=sqv[p], in_=xth[p], func=AF.Square), reads=[xthB[p]], writes=[sqBs[p]])
                mm_group(p, [(banks[p][:, 0:256], ones_bf[:], sqv[p][:, k, :], k == 0, k == 7) for k in range(8)], reads=[sqBs[p]])
                P.op("act", lambda a, p=p: a.activation(out=rsv[p], in_=banks[p][:, 0:256], func=AF.Ln, bias=epst[:], scale=1.0 / D),
                     reads=[bankB[p]], writes=[rsB[p]])
                P.op("act", lambda a, p=p: a.activation(out=rsv[p], in_=rsv[p], func=AF.Exp, scale=-0.5), reads=[rsB[p]], writes=[rsB[p]])
                for k in range(8):
                    P.op("dve", lambda v, k=k, p=p, cl=cl: v.scalar_tensor_tensor(
                        out=hT[:, k, cl:cl + 256], in0=xth[p][:, k, :], scalar=g_col[:, l, k:k + 1], in1=rsv[p],
                        op0=ALU.mult, op1=ALU.mult), reads=[xthB[p], rsB[p]], writes=[hTB[tg]], append=not (s_ % 2 == 0 and k == 0))

        KT = r1(0, 4096)
        VT = r1(4096, 4096)
        VA = r1(8192, 8832).rearrange("p (b e) -> p b e", e=128)
        QT = r1(17024, 6144).rearrange("p (g t) -> p g t", g=3)
        accv = f32v(r2(0, 4096))
        denv = f32v(r2(4096, 4096))
        Et = [r2(8192 + 512 * i, 512) for i in range(4)]
        Pm = [r2(10240 + 512 * i, 512) for i in range(4)]
        recv = f32v(r2(12288, 1024))
        sgv = [f32v(r2(13312, 1024)), f32v(r2(14336, 1024))]
        tv = f32v(r2(15360, 1024))

        KTB = Buf("KT")
        VTB = Buf("VT")
        VAB = Buf("VA")
        QTB = [Buf(f"QT{g}") for g in range(3)]
        accB = Buf("acc")
        denB = Buf("den")
        EB = [Buf("E0"), Buf("E1"), Buf("E2"), Buf("E3")]
        PB_ = [Buf("P0"), Buf("P1"), Buf("P2"), Buf("P3")]
        recB = Buf("rec")
        tB = Buf("t")
        sgCB = [Buf("sg0"), Buf("sg1")]

        def kv_project(l, h):
            sl, sB = wload([win_piece(l, OFF["c_k"] + 128 * h, 128, 0), win_piece(l, OFF["c_v"] + 128 * h, 128, 128)])
            for tg in range(NTG):
                bk = (0, 1)[tg % 2]
                bv = (2, 7)[tg % 2]
                proj_group(bk, sl, sB, 0, 128, tg)
                P.op("act", lambda a, tg=tg, bk=bk: a.activation(out=KT[:, SGT + tg * TG:SGT + (tg + 1) * TG], in_=banks[bk][:, :], func=AF.Copy),
                     reads=[bankB[bk]], writes=[KTB], append=(tg > 0))
                proj_group(bv, sl, sB, 128, 128, tg)
                P.op("dve", lambda v, tg=tg, bv=bv: v.tensor_copy(out=VT[:, SGT + tg * TG:SGT + (tg + 1) * TG], in_=banks[bv][:, :]),
                     reads=[bankB[bv]], writes=[VTB], append=(tg > 0))
            return KTB, VTB

        def kv_save(h, KTB, VTB):
            P.dma("sp", kvprev[h, 0], KT[:, SGT:2 * SGT], reads=[KTB], writes=[kvprevB[h]])
            P.dma("sp", kvprev[h, 1], VT[:, SGT:2 * SGT], reads=[VTB], writes=[kvprevB[h]], append=True)

        def kv_loadprev(h, KTB, VTB):
            P.dma("sp", KT[:, 0:SGT], kvprev[h, 0], reads=[kvprevB[h]], writes=[KTB], append=True, after=list(KTB.lw))
            P.dma("sp", VT[:, 0:SGT], kvprev[h, 1], reads=[kvprevB[h]], writes=[VTB], append=True, after=list(VTB.lw))

        DIL = (1, 4, 16)

        def blocks_list(has_prev):
            lst = []
            for d in DIL:
                nb = 16 // d
                for r in range(d):
                    for n in range(-1 if has_prev else 0, nb):
                        lst.append((d, r, n))
            return lst

        def phase_C(l, has_prev, save_kv):
            for h in range(4):
                KTB, VTB = kv_project(l, h)
                if save_kv:
                    kv_save(h, KTB, VTB)
                if has_prev:
                    kv_loadprev(h, KTB, VTB)
                blist = blocks_list(has_prev)
                bidx = {b: i for i, b in enumerate(blist)}
                for g0 in range(0, len(blist), 8):
                    grp = blist[g0:g0 + 8]
                    bi = (7, 0, 1, 2)[(g0 // 8) % 4]
                    pbf = banks[bi][:, :].bitcast(BF16)

                    def fn(pe, grp=grp, pbf=pbf):
                        inst = None
                        for i, (d, r, n) in enumerate(grp):
                            st = SGT + r + d * 128 * n
                            inst = pe.transpose(out=pbf[:, i * 128:(i + 1) * 128], in_=VT[:, ss(st, d)], identity=ident_bf[:])
                        return inst
                    P.op("pe", fn, reads=[VTB], writes=[bankB[bi]])
                    ng = len(grp)
                    P.op("dve", lambda v, g0=g0, ng=ng, pbf=pbf: v.tensor_copy(
                        out=VA[:, g0:g0 + ng, :], in_=pbf[:, 0:ng * 128].rearrange("p (b e) -> p b e", e=128)),
                        reads=[bankB[bi]], writes=[VAB], append=(g0 > 0))
                slA, sBA = wload([win_piece(l, OFF["c_q"] + g * 512 + 128 * h, 128, g * 128) for g in range(3)]
                                 + [win_piece(l, OFF["c_gate"] + 128 * h, 128, 384)])
                for g in range(3):
                    for tg in range(NTG):
                        bi = (7, 0, 1, 2)[tg % 4]
                        proj_group(bi, slA, sBA, g * 128, 128, tg)
                        P.op("act", lambda a, g=g, tg=tg, bi=bi: a.activation(
                            out=QT[:, g, tg * TG:(tg + 1) * TG], in_=banks[bi][:, :], func=AF.Copy, scale=QSCALE),
                            reads=[bankB[bi]], writes=[QTB[g]], append=(tg > 0))
                LAG = 3
                pending = []
                step = 0

                def rec_pv(g, d, quad, pr, info, par, obank, dbank):
                    omms = []
                    dmms = []
                    for ii, (r, n, qs) in enumerate(info):
                        oc = (pr * 2 + ii) * 128
                        hasp = not (n == 0 and not has_prev)
                        pcol = 384 if ii == 0 else 256
                        omms.append((banks[obank][:, oc:oc + 128], VA[:, bidx[(d, r, n)], :], Pm[par][:, ii * 128:(ii + 1) * 128], True, not hasp))
                        dmms.append((banks[dbank][:, oc:oc + 128], ones_bf[:], Pm[par][:, ii * 128:(ii + 1) * 128], True, not hasp))
                        if hasp:
                            omms.append((banks[obank][:, oc:oc + 128], VA[:, bidx[(d, r, n - 1)], :], Pm[par][:, pcol:pcol + 128], False, True))
                            dmms.append((banks[dbank][:, oc:oc + 128], ones_bf[:], Pm[par][:, pcol:pcol + 128], False, True))
                    mm_group(obank, omms, reads=[PB_[par], VAB], append=(pr > 0))
                    mm_group(dbank, dmms, reads=[PB_[par]], append=(pr > 0))
                    if pr == 0:
                        return
                    j0 = quad * 4
                    if d == 1:
                        av = accv[:, j0 * 128:j0 * 128 + 512]
                        dv = denv[:, j0 * 128:j0 * 128 + 512]
                        ob = banks[obank][:, :]
                        db = banks[dbank][:, :]
                    elif d == 4:
                        av = accv[:, quad:SGT:4]
                        dv = denv[:, quad:SGT:4]
                        ob = banks[obank][:, :]
                        db = banks[dbank][:, :]
                    else:
                        av = accv.rearrange("p (i r) -> p r i", r=16)[:, j0:j0 + 4, :]
                        dv = denv.rearrange("p (i r) -> p r i", r=16)[:, j0:j0 + 4, :]
                        ob = banks[obank][:, :].rearrange("p (a b) -> p a b", a=4)
                        db = banks[dbank][:, :].rearrange("p (a b) -> p a b", a=4)
                    if g == 0:
                        P.op("act", lambda a, av=av, ob=ob: a.activation(out=av, in_=ob, func=AF.Copy),
                             reads=[bankB[obank]], writes=[accB], append=(quad > 0))
                        P.op("dve", lambda v, dv=dv, db=db: v.tensor_copy(out=dv, in_=db),
                             reads=[bankB[dbank]], writes=[denB], append=(quad > 0))
                    else:
                        P.op("dve", lambda v, av=av, ob=ob: v.tensor_tensor(out=av, in0=ob, in1=av, op=ALU.add),
                             reads=[bankB[obank]], writes=[accB], append=(quad > 0))
                        P.op("dve", lambda v, dv=dv, db=db: v.tensor_tensor(out=dv, in0=db, in1=dv, op=ALU.add),
                             reads=[bankB[dbank]], writes=[denB], append=(quad > 0))

                for g, d in enumerate(DIL):
                    nb = 16 // d
                    for quad in range(4):
                        obank = 3 + (quad % 2)
                        dbank = 5 + (quad % 2)
                        for pr in range(2):
                            j = quad * 4 + pr * 2
                            info = []
                            for jj in (j, j + 1):
                                r, n = jj // nb, jj % nb
                                info.append((r, n, r + d * 128 * n))
                            cross = [n == 0 for (_, n, _) in info]
                            if cross[0] and cross[1]:
                                mki, ncols = 2, (512 if has_prev else 256)
                            elif cross[0]:
                                mki, ncols = 1, (512 if has_prev else 384)
                            else:
                                mki, ncols = 0, 512
                            sbank = (0, 1, 2, 7)[step % 4]
                            par = step % 4
                            step += 1
                            mms = []
                            for ii, (r, n, qs) in enumerate(info):
                                qap = QT[:, g, ss(qs, d)]
                                mms.append((banks[sbank][:, ii * 128:(ii + 1) * 128], KT[:, ss(SGT + qs, d)], qap, True, True))
                            for ii, col in ((1, 256), (0, 384)):
                                r, n, qs = info[ii]
                                if n == 0 and not has_prev:
                                    continue
                                qap = QT[:, g, ss(qs, d)]
                                ks = SGT + qs - 128 * d
                                mms.append((banks[sbank][:, col:col + 128], KT[:, ss(ks, d)], qap, True, True))
                            mm_group(sbank, mms, reads=[KTB, QTB[g]])
                            P.op("act", lambda a, sbank=sbank, par=par, ncols=ncols: a.activation(
                                out=Et[par][:, 0:ncols], in_=banks[sbank][:, 0:ncols], func=AF.Exp),
                                reads=[bankB[sbank]], writes=[EB[par]])
                            P.op("dve", lambda v, par=par, ncols=ncols, mki=mki: v.tensor_tensor(
                                out=Pm[par][:, 0:ncols], in0=Et[par][:, 0:ncols], in1=mk_b[:, mki, 0:ncols], op=ALU.mult),
                                reads=[EB[par]], writes=[PB_[par]])
                            pending.append((g, d, quad, pr, info, par, obank, dbank))
                            if len(pending) > LAG:
                                rec_pv(*pending.pop(0))
                while pending:
                    rec_pv(*pending.pop(0))
                sgB = sgCB
                for tg in range(NTG):
                    bi = (7, 0)[tg % 2]
                    proj_group(bi, slA, sBA, 384, 128, tg)
                    P.op("act", lambda a, tg=tg, bi=bi: a.activation(out=sgv[tg % 2], in_=banks[bi][:, :], func=AF.Silu),
                         reads=[bankB[bi]], writes=[sgB[tg % 2]])
                    P.op("act", lambda a, tg=tg: a.activation(out=recv, in_=denv[:, tg * TG:(tg + 1) * TG], func=AF.Ln), reads=[denB], writes=[recB])
                    P.op("act", lambda a: a.activation(out=recv, in_=recv, func=AF.Exp, scale=-1.0), reads=[recB], writes=[recB])
                    P.op("dve", lambda v, tg=tg: v.tensor_tensor(out=tv, in0=accv[:, tg * TG:(tg + 1) * TG], in1=recv, op=ALU.mult),
                         reads=[accB, recB], writes=[tB])
                    P.op("dve", lambda v, tg=tg, h=h: v.tensor_tensor(out=yT[:, 0 + h, tg * TG:(tg + 1) * TG], in0=tv, in1=sgv[tg % 2], op=ALU.mult),
                         reads=[tB, sgB[tg % 2]], writes=[ybuf(2, h, tg)] + ([sqBs[h]] if h < 2 else []))
            P.barrier()

        def phase_prev_recompute(l):
            for h in range(4):
                KTB, VTB = kv_project(l, h)
                kv_save(h, KTB, VTB)
                P.barrier()
            sl, sB = wload([win_piece(l, OFF["p_in"], 512, 0)])
            for g in range(4):
                bi = 6 + (g % 2)
                proj_group(bi, sl, sB, g * 128, 128, 3)
                P.op("act", lambda a, g=g, bi=bi: a.activation(out=ptail[:, g, :], in_=banks[bi][:, 496:512], func=AF.Copy),
                     reads=[bankB[bi]], writes=[ptailB], append=(g > 0))

        def phase_B(l, sgi, save_tail):
            pbuf = [f32v(r2(0, 4128)), f32v(r1(16384, 4128))]
            s_a = f32v(r2(4128, 4128))
            s_b = f32v(r1(0, 4128))
            dg = [r2(8256, 2048), r1(16384 + 4128, 2048)]
            sgl = [f32v(r2(10304, 1024)), f32v(r2(11328, 1024))]
            W = SGT + 16
            pbB = [Buf("pbuf0"), Buf("pbuf1")]
            saB = Buf("s_a")
            sbB = Buf("s_b")
            dgB = [Buf("dg0"), Buf("dg1")]
            sgB = [Buf("sg0"), Buf("sg1")]
            t16B = Buf("t16")
            slots_g = {}

            def front(g):
                q = g % 2
                slots_g[g] = wload([win_piece(l, OFF["p_in"] + 128 * g, 128, 0), win_piece(l, OFF["p_gate"] + 128 * g, 128, 128)])
                sl, sB = slots_g[g]
                P.op("dve", lambda v: v.tensor_copy(out=pbuf[q][:, 0:16], in_=ptail[:, g, :]), reads=[ptailB], writes=[pbB[q]])
                for tg in range(NTG):
                    bi = 6 + (tg % 2)
                    proj_group(bi, sl, sB, 0, 128, tg)
                    P.op("act", lambda a, tg=tg, bi=bi: a.activation(out=pbuf[q][:, 16 + tg * TG:16 + (tg + 1) * TG], in_=banks[bi][:, :], func=AF.Copy),
                         reads=[bankB[bi]], writes=[pbB[q]], append=True)

            def back(g):
                q = g % 2
                w = 2 ** (g + 1)
                sl, sB = slots_g[g]
                pb = pbuf[q]
                P.op("dve", lambda v: v.tensor_tensor(out=s_a[:, 1:W], in0=pb[:, 1:W], in1=pb[:, 0:W - 1], op=ALU.add),
                     reads=[pbB[q]], writes=[saB])
                S, SB_ = s_a, saB
                if g >= 1:
                    P.op("dve", lambda v: v.tensor_tensor(out=s_b[:, 3:W], in0=s_a[:, 3:W], in1=s_a[:, 1:W - 2], op=ALU.add),
                         reads=[saB], writes=[sbB])
                    S, SB_ = s_b, sbB
                if g >= 2:
                    P.op("dve", lambda v: v.tensor_tensor(out=s_a[:, 7:W], in0=s_b[:, 7:W], in1=s_b[:, 3:W - 4], op=ALU.add),
                         reads=[sbB], writes=[saB])
                    S, SB_ = s_a, saB
                if g >= 3:
                    P.op("dve", lambda v: v.tensor_tensor(out=s_b[:, 15:W], in0=s_a[:, 15:W], in1=s_a[:, 7:W - 8], op=ALU.add),
                         reads=[saB], writes=[sbB])
                    S, SB_ = s_b, sbB
                P.op("dve", lambda v, S=S: v.scalar_tensor_tensor(out=dg[q][:, :], in0=S[:, 16:W], scalar=1.0 / w, in1=pb[:, 16:W],
                                                                  op0=ALU.mult, op1=ALU.subtract), reads=[SB_, pbB[q]], writes=[dgB[q]])
                P.op("dve", lambda v, S=S: v.tensor_tensor(out=t16[:], in0=S[:, 16:32], in1=rc_t[:, sgi, g * 16:(g + 1) * 16], op=ALU.mult),
                     reads=[SB_], writes=[t16B])
                P.op("dve", lambda v: v.tensor_tensor(out=dg[q][:, 0:16], in0=t16[:], in1=pb[:, 16:32], op=ALU.subtract),
                     reads=[t16B, pbB[q]], writes=[dgB[q]], append=True)
                if save_tail:
                    P.op("dve", lambda v: v.tensor_copy(out=ptail[:, g, :], in_=pb[:, SGT:SGT + 16]), reads=[pbB[q]], writes=[ptailB])
                for tg in range(NTG):
                    bg = 2 + (tg % 2)
                    by = 4 + (tg % 2)
                    proj_group(bg, sl, sB, 128, 128, tg)
                    P.op("act", lambda a, tg=tg, bg=bg: a.activation(out=sgl[tg % 2], in_=banks[bg][:, :], func=AF.Silu),
                         reads=[bankB[bg]], writes=[sgB[tg % 2]])
                    mm_group(by, [(banks[by][:, :], poolw[:, g, :], dg[q][:, tg * TG:(tg + 1) * TG], True, True)], reads=[dgB[q], layerB])
                    P.op("dve", lambda v, tg=tg, by=by: v.scalar_tensor_tensor(
                        out=yT[:, 8 + g, tg * TG:(tg + 1) * TG], in0=banks[by][:, :], scalar=psc_col[:, l, g:g + 1], in1=sgl[tg % 2],
                        op0=ALU.mult, op1=ALU.mult), reads=[bankB[by], sgB[tg % 2]], writes=[ybuf(1, g, tg)])

            front(0)
            front(1)
            back(0)
            front(2)
            back(1)
            front(3)
            back(2)
            back(3)
            P.barrier()

        def phase_A(l):
            vg = [f32v(r2(1024 * i, 1024)) for i in range(4)]
            vt = vg
            vln = [r2(4096 + 512 * i, 512) for i in range(4)]
            gu = [f32v(r2(6144 + 1024 * i, 1024)) for i in range(4)]
            sgl = [f32v(r2(10240 + 1024 * i, 1024)) for i in range(4)]
            t1 = [f32v(r2(14336, 1024)), f32v(r2(15360, 1024))]
            slV, sBV = wload([win_piece(l, OFF["a_v"], 512, 0)])
            slU, sBU = wload([win_piece(l, OFF["a_u"], 512, 0)])
            slG, sBG = wload([win_piece(l, OFF["a_gate"], 512, 0)])
            vgB = [Buf("vg%d" % i) for i in range(4)]
            vtB = vgB
            vlnB = [Buf("vln%d" % i) for i in range(4)]
            stB = [Buf("st%d" % i) for i in range(4)]
            guB = [Buf("gu%d" % i) for i in range(4)]
            sgB = [Buf("sg%d" % i) for i in range(4)]
            t1B = [Buf("t10"), Buf("t11")]

            def rec_front(idx, tg, tt, p):
                c0 = tg * TG + tt * 128
                bv = 6 + (idx % 2)
                mm_group(bv, [(banks[bv][:, :], hT[:, k, c0:c0 + 128], slV[:, k, :], k == 0, k == 7) for k in range(8)],
                         reads=[sBV, hTB[tg]])
                P.op("act", lambda a: a.activation(out=vg[p], in_=banks[bv][:, :], func=AF.Gelu),
                     reads=[bankB[bv]], writes=[vgB[p]])
                P.op("dve", lambda v: v.bn_stats(out=bnst[:, p, :], in_=vg[p]), reads=[vgB[p]], writes=[stB[p]])
                P.op("dve", lambda v: v.bn_aggr(out=bnmv[:, p, :], in_=bnst[:, p, :]), reads=[stB[p]], writes=[stB[p]])
                P.op("act", lambda a: a.activation(out=bnrs[:, p, :], in_=bnmv[:, p, 1:2], func=AF.Sqrt, bias=epst[:], scale=1.0),
                     reads=[stB[p]], writes=[stB[p]])
                P.op("dve", lambda v: v.reciprocal(out=bnrs[:, p, :], in_=bnrs[:, p, :]), reads=[stB[p]], writes=[stB[p]])
                P.op("dve", lambda v: v.tensor_scalar(out=vt[p], in0=vg[p], scalar1=bnmv[:, p, 0:1], scalar2=bnrs[:, p, 0:1],
                                                      op0=ALU.subtract, op1=ALU.mult), reads=[vgB[p], stB[p]], writes=[vtB[p]])
                P.op("dve", lambda v: v.tensor_tensor(out=vt[p], in0=vt[p], in1=lng_b[:], op=ALU.mult),
                     reads=[vtB[p], layerB], writes=[vtB[p]])
                P.op("dve", lambda v: v.tensor_tensor(out=vln[p], in0=vt[p], in1=lnb_b[:], op=ALU.add),
                     reads=[vtB[p], layerB], writes=[vlnB[p]])

            def rec_back(idx, tg, tt, p):
                for h in range(4):
                    mm_group(h, [(banks[h][:, tt * 128:(tt + 1) * 128], vln[p][:, h * 128:(h + 1) * 128], ws_bf[:, h, :], True, True)],
                             reads=[vlnB[p], layerB], append=(tt > 0))
                if tt < 3:
                    return
                for h in range(4):
                    bu = 4 + (h % 2)
                    proj_group(bu, slU, sBU, h * 128, 128, tg)
                    P.op("act", lambda a, h=h, bu=bu: a.activation(out=gu[h], in_=banks[bu][:, :], func=AF.Gelu), reads=[bankB[bu]], writes=[guB[h]])
                for h in range(4):
                    bg = 4 + (h % 2)
                    proj_group(bg, slG, sBG, h * 128, 128, tg)
                    P.op("act", lambda a, h=h, bg=bg: a.activation(out=sgl[h], in_=banks[bg][:, :], func=AF.Silu), reads=[bankB[bg]], writes=[sgB[h]])
                for h in range(4):
                    q = h % 2
                    bsv = bs_b[:, h * 128:(h + 1) * 128].unsqueeze(1).to_broadcast([128, 4, 128])
                    P.op("dve", lambda v, q=q, h=h, bsv=bsv: v.tensor_tensor(
                        out=t1[q].rearrange("p (a b) -> p a b", a=4), in0=banks[h][:, :].rearrange("p (a b) -> p a b", a=4), in1=bsv, op=ALU.add),
                        reads=[bankB[h], layerB], writes=[t1B[q]])
                    P.op("dve", lambda v, q=q, h=h: v.tensor_tensor(out=t1[q], in0=t1[q], in1=gu[h], op=ALU.mult), reads=[t1B[q], guB[h]], writes=[t1B[q]])
                    P.op("dve", lambda v, q=q, h=h: v.tensor_tensor(out=yT[:, 4 + h, tg * TG:(tg + 1) * TG], in0=t1[q], in1=sgl[h], op=ALU.mult),
                         reads=[t1B[q], sgB[h]], writes=[ybuf(0, h, tg)])

            pend = []
            idx = 0
            for tg in range(NTG):
                for tt in range(4):
                    p = idx % 4
                    rec_front(idx, tg, tt, p)
                    pend.append((idx, tg, tt, p))
                    idx += 1
                    if len(pend) > 3:
                        rec_back(*pend.pop(0))
            while pend:
                rec_back(*pend.pop(0))
            P.barrier()

        def layer_setup(l):
            P.dma("sp", lng_b[:], ln_g[l:l + 1, :].partition_broadcast(128), writes=[layerB])
            P.dma("sp", lnb_b[:], ln_b[l:l + 1, :].partition_broadcast(128), writes=[layerB], append=True)
            P.dma("sp", bs_b[:], gm_bs[l:l + 1, :].partition_broadcast(128), writes=[layerB], append=True)
            P.dma("sp", ws_f[:], wsT[l].rearrange("h s t -> s h t"), writes=[layerB], append=True)
            P.dma("pool", poolw[:], pool_w[l].rearrange("g i o -> i g o"), writes=[layerB], append=True)
            for h in range(4):
                P.op("dve", lambda v, h=h: v.tensor_tensor(out=ws_bf[:, h, :], in0=ws_f[:, h, :], in1=mk_b[:, 0, 0:128], op=ALU.mult),
                     reads=[layerB], writes=[layerB], append=True)
            if not recompute_prev:
                P.op("dve", lambda v: v.memset(ptail[:], 0.0), writes=[ptailB])
            memfB2 = Buf("memf2")
            P.dma("sp", memf, memT.rearrange("(k p) m -> p k m", p=128), writes=[memfB2])
            for k in range(8):
                P.op("dve", lambda v, k=k: v.scalar_tensor_tensor(out=mn[:, k, :], in0=memf[:, k, :], scalar=gm_col[:, l, k:k + 1], in1=rstd_m[:],
                                                                   op0=ALU.mult, op1=ALU.mult), reads=[memfB2, memB], writes=[memB], append=True)
            slK, sBK = wload([(lambda sl: sl[:, :, :], w_kv[l, :, 0:512].rearrange("(k p) n -> p k n", p=128))])
            slVv, sBVv = wload([(lambda sl: sl[:, :, :], w_kv[l, :, 512:1024].rearrange("(k p) n -> p k n", p=128))])
            for h in range(4):
                bi = 6 + (h % 2)
                mm_group(bi, [(banks[bi][:, 0:256], slK[:, k, h * 128:(h + 1) * 128], mn[:, k, :], k == 0, k == 7) for k in range(8)],
                         reads=[sBK, memB])
                P.op("act", lambda a, h=h, bi=bi: a.activation(out=kmT[:, h, :], in_=banks[bi][:, 0:256], func=AF.Copy),
                     reads=[bankB[bi]], writes=[memB], append=True)
            for mb in range(2):
                bi = 4 + mb
                mm_group(bi, [(banks[bi][:, :], mn[:, k, mb * 128:(mb + 1) * 128], slVv[:, k, :], k == 0, k == 7) for k in range(8)],
                         reads=[sBVv, memB])
                P.op("act", lambda a, mb=mb, bi=bi: a.activation(out=vm[:, mb, :], in_=banks[bi][:, :], func=AF.Copy),
                     reads=[bankB[bi]], writes=[memB], append=True)
            P.barrier()

        def phase_M(l):
            qs = [r2(512 * i, 512) for i in range(3)]
            Ev = [r2(1536 + 1024 * i, 1024).rearrange("p (m t) -> p m t", m=2) for i in range(3)]
            rec = [f32v(r2(4608 + 1024 * i, 1024)) for i in range(2)]
            sgl = [f32v(r2(6656 + 1024 * i, 1024)) for i in range(3)]
            tt_ = [f32v(r2(9728 + 1024 * i, 1024)) for i in range(2)]
            qsB = [Buf("qs%d" % i) for i in range(3)]
            EvB = [Buf("E%d" % i) for i in range(3)]
            recB = [Buf("r0"), Buf("r1")]
            sgB = [Buf("s%d" % i) for i in range(3)]
            ttB = [Buf("t0"), Buf("t1")]
            slots_h = {}

            def stA(i, h, tg):
                p3 = i % 3
                if tg == 0:
                    slots_h[h] = wload([win_piece(l, OFF["m_q"] + 128 * h, 128, 0), win_piece(l, OFF["m_gate"] + 128 * h, 128, 128)])
                sl, sB = slots_h[h]
                bq = 4 + (i % 2)
                bg = 6 + (i % 2)
                proj_group(bq, sl, sB, 0, 128, tg)
                P.op("dve", lambda v: v.tensor_scalar_mul(out=qs[p3], in0=banks[bq][:, :], scalar1=QSCALE),
                     reads=[bankB[bq]], writes=[qsB[p3]])
                proj_group(bg, sl, sB, 128, 128, tg)
                P.op("act", lambda a: a.activation(out=sgl[p3], in_=banks[bg][:, :], func=AF.Silu), reads=[bankB[bg]], writes=[sgB[p3]])

            def stB_(i, h, tg):
                p3 = i % 3
                for mb in range(2):
                    mm_group(mb, [(banks[mb][:, :], kmT[:, h, mb * 128:(mb + 1) * 128], qs[p3], True, True)], reads=[qsB[p3], memB])
                    P.op("act", lambda a, mb=mb: a.activation(out=Ev[p3][:, mb, :], in_=banks[mb][:, :], func=AF.Exp),
                         reads=[bankB[mb]], writes=[EvB[p3]], append=(mb > 0))

            def stC(i, h, tg):
                p3 = i % 3
                p = i % 2
                mm_group(2, [(banks[2][:, :], vm[:, mb, h * 128:(h + 1) * 128], Ev[p3][:, mb, :], mb == 0, mb == 1) for mb in range(2)],
                         reads=[EvB[p3], memB])
                mm_group(3, [(banks[3][:, :], ones_bf[:], Ev[p3][:, mb, :], mb == 0, mb == 1) for mb in range(2)], reads=[EvB[p3]])
                P.op("act", lambda a: a.activation(out=rec[p], in_=banks[3][:, :], func=AF.Ln), reads=[bankB[3]], writes=[recB[p]])
                P.op("act", lambda a: a.activation(out=rec[p], in_=rec[p], func=AF.Exp, scale=-1.0), reads=[recB[p]], writes=[recB[p]])
                P.op("dve", lambda v: v.tensor_tensor(out=tt_[p], in0=banks[2][:, :], in1=rec[p], op=ALU.mult),
                     reads=[bankB[2], recB[p]], writes=[ttB[p]])
                P.op("dve", lambda v: v.tensor_tensor(out=yT[:, 12 + h, tg * TG:(tg + 1) * TG], in0=tt_[p], in1=sgl[p3], op=ALU.mult),
                     reads=[ttB[p], sgB[p3]], writes=[ybuf(3, h, tg)])

            its = [(i, i // 4, i % 4) for i in range(16)]
            for i in range(16 + 2):
                if i < 16:
                    stA(*its[i])
                if 0 <= i - 1 < 16:
                    stB_(*its[i - 1])
                if 0 <= i - 2 < 16:
                    stC(*its[i - 2])
            P.barrier()

        zT = R2[:, :].rearrange("p (c t) -> p c t", c=8)
        zB = {}

        def phase_G(l):
            sgt = [xt0[:, 0:512], xt0[:, 512:1024]]
            zacc = [xt0[:, 1024:1536], xt0[:, 1536:2048]]
            tmp = [xt1[:, 0:512], xt1[:, 512:1024]]
            sgtB = [Buf("sgt0"), Buf("sgt1")]
            zaB = [Buf("za0"), Buf("za1")]
            tmB = [Buf("tm0"), Buf("tm1")]
            it = 0
            for cc in range(8):
                slG, sBG = wload([win_piece(l, OFF["g_merge"] + b * 1024 + cc * 128, 128, b * 128) for b in range(4)])
                slW, sBW = wload([(lambda sl, b=b: sl[:, 0:4, b * 128:(b + 1) * 128],
                                   w_br[l, b, :, cc * 128:(cc + 1) * 128].rearrange("(k p) n -> p k n", p=128)) for b in range(4)])
                for tg in range(NTG):
                    zp = it % 2
                    it += 1
                    for b in range(4):
                        p = b % 2
                        gb = 0 + p
                        yb = 2 + p
                        proj_group(gb, slG, sBG, b * 128, 128, tg)
                        yi = YIDX[b]
                        mm_group(yb, [(banks[yb][:, :], slW[:, kc, b * 128:(b + 1) * 128], yT[:, yi * 4 + kc, tg * TG:(tg + 1) * TG], kc == 0, kc == 3)
                                      for kc in range(4)], reads=[sBW] + [ybuf(b, kc, tg) for kc in range(4)])
                        P.op("act", lambda a, p=p, gb=gb: a.activation(out=sgt[p], in_=banks[gb][:, :], func=AF.Sigmoid),
                             reads=[bankB[gb]], writes=[sgtB[p]])
                        if b == 0:
                            P.op("dve", lambda v, p=p, yb=yb, zp=zp: v.tensor_tensor(out=zacc[zp], in0=banks[yb][:, :], in1=sgt[p], op=ALU.mult),
                                 reads=[bankB[yb], sgtB[p]], writes=[zaB[zp]])
                        else:
                            P.op("dve", lambda v, p=p, yb=yb: v.tensor_tensor(out=tmp[p], in0=banks[yb][:, :], in1=sgt[p], op=ALU.mult),
                                 reads=[bankB[yb], sgtB[p]], writes=[tmB[p]])
                            if b < 3:
                                P.op("dve", lambda v, p=p, zp=zp: v.tensor_tensor(out=zacc[zp], in0=zacc[zp], in1=tmp[p], op=ALU.add),
                                     reads=[tmB[p], zaB[zp]], writes=[zaB[zp]])
                            else:
                                zB[(cc, tg)] = Buf(f"z{cc}_{tg}")
                                P.op("dve", lambda v, p=p, zp=zp, cc=cc, tg=tg: v.tensor_tensor(
                                    out=zT[:, cc, tg * TG:(tg + 1) * TG], in0=zacc[zp], in1=tmp[p], op=ALU.add),
                                    reads=[tmB[p], zaB[zp]], writes=[zB[(cc, tg)]])
            P.barrier()

        def phase_O(l, xsrc, xdst, t0, final):
            sqo = [hT[:, 0, :].rearrange("p (k t) -> p k t", k=8), hT[:, 1, :].rearrange("p (k t) -> p k t", k=8)]
            slO = []
            for s_ in range(2):
                slO.append(wload([(lambda sl: sl[:, :, :], w_out[l, :, s_ * 512:(s_ + 1) * 512].rearrange("(k p) n -> p k n", p=128))]))
            for s_ in range(8):
                p = s_ % 2
                c0 = t0 + s_ * 256
                cl = s_ * 256
                tg = s_ // 2
                if s_ == 0:
                    P.dma("sp", xtf[0], xsrc[c0 // 256], writes=[xthB[0]])
                if s_ + 1 < 8:
                    P.dma("sp", xtf[1 - p], xsrc[c0 // 256 + 1], writes=[xthB[1 - p]])
                for cp in range(8):
                    bi = 2 + (cp % 4)
                    sl, sB = slO[cp // 4]
                    mm_group(bi, [(banks[bi][:, 0:256], sl[:, cc, (cp % 4) * 128:(cp % 4 + 1) * 128], zT[:, cc, cl:cl + 256], cc == 0, cc == 7)
                                  for cc in range(8)], reads=[sB] + [zB[(cc, tg)] for cc in range(8)])
                    P.op("dve", lambda v, cp=cp, bi=bi, p=p: v.tensor_tensor(out=xth[p][:, cp, :], in0=banks[bi][:, 0:256], in1=xth[p][:, cp, :], op=ALU.add),
                         reads=[bankB[bi], xthB[p]], writes=[xthB[p]], append=True)
                if xdst is not None:
                    P.dma("sp", xdst[c0 // 256], xtf[p], reads=[xthB[p]])
                if final:
                    P.op("act", lambda a, p=p: a.activation(out=sqo[p], in_=xth[p], func=AF.Square), reads=[xthB[p]], writes=[sqBs[p]])
                    mm_group(p, [(banks[p][:, 0:256], ones_bf[:], sqo[p][:, k, :], k == 0, k == 7) for k in range(8)], reads=[sqBs[p]])
                    P.op("act", lambda a, p=p: a.activation(out=rsv[p], in_=banks[p][:, 0:256], func=AF.Ln, bias=epst[:], scale=1.0 / D),
                         reads=[bankB[p]], writes=[rsB[p]])
                    P.op("act", lambda a, p=p: a.activation(out=rsv[p], in_=rsv[p], func=AF.Exp, scale=-0.5), reads=[rsB[p]], writes=[rsB[p]])
                    for k in range(8):
                        P.op("dve", lambda v, k=k, p=p: v.scalar_tensor_tensor(out=xth[p][:, k, :], in0=xth[p][:, k, :], scalar=gf_col[:, k:k + 1], in1=rsv[p],
                                                                                op0=ALU.mult, op1=ALU.mult), reads=[rsB[p], xthB[p]], writes=[xthB[p]], append=(k > 0))
                    P.dma("sp", oT[c0 // 256], xtf[p], reads=[xthB[p]])
            P.barrier()

        for l in range(L):
            layer_setup(l)
            if recompute_prev:
                xsrc, xdst = xT, xdst_t
            else:
                xsrc = xT if l == 0 else xdst_t
                xdst = xdst_t if l < L - 1 else None
            final = recompute_prev or (l == L - 1)
            for sgi in range(NSG):
                P.new_epoch()
                t0 = sgi * SGT
                if recompute_prev:
                    phase_P0(l, xpT, 0)
                    P.barrier()
                    phase_prev_recompute(l)
                    P.barrier()
                phase_P0(l, xsrc, t0)
                has_prev = recompute_prev or sgi > 0
                phase_C(l, has_prev, save_kv=(not recompute_prev and sgi < NSG - 1))
                phase_B(l, sgi, save_tail=(not recompute_prev and sgi < NSG - 1))
                phase_A(l)
                phase_M(l)
                phase_G(l)
                phase_O(l, xsrc, xdst, t0, final)
        P.barrier()

        @block.tensor
        def _(pe):
            for f in P.q["pe"]:
                f(pe)

        @block.scalar
        def _(act):
            for f in P.q["act"]:
                f(act)

        @block.vector
        def _(dve):
            for f in P.q["dve"]:
                f(dve)

        @block.gpsimd
        def _(pool):
            for f in P.q["pool"]:
                f(pool)

        @block.sync
        def _(sp):
            for f in P.q["sp"]:
                f(sp)
    return nc


_PROG_CACHE = {}


def _get_prog(L, NSG, recompute_prev):
    key = (L, NSG, recompute_prev)
    if key not in _PROG_CACHE:
        _PROG_CACHE[key] = build_program(L, NSG, recompute_prev)
    return _PROG_CACHE[key]


def _masks(pv):
    k = np.arange(128)[:, None]
    q = np.arange(128)[None, :]
    cur = (k <= q).astype(np.float32)
    A = (k >= q).astype(np.float32)
    X = A * np.float32(pv)
    m = np.zeros((4, 128, 512), np.float32)
    m[0] = np.concatenate([cur, cur, A, A], axis=1)
    m[1] = np.concatenate([cur, cur, A, X], axis=1)
    m[2] = np.concatenate([cur, cur, X, X], axis=1)
    m[3, :, 0:128] = np.eye(128, dtype=np.float32)
    return m


def _rc(start):
    rc = np.zeros((128, 64), np.float32)
    for g, w in enumerate((2, 4, 8, 16)):
        for t in range(16):
            cnt = min(t + 1, w) if start else w
            rc[:, g * 16 + t] = 1.0 / cnt
    return rc


def _tile_x(a):
    T = a.shape[0]
    return np.ascontiguousarray(a.reshape(T // 256, 256, 8, 128).transpose(0, 3, 2, 1)).reshape(T // 256, 128, 2048)


def _untile_x(a):
    n = a.shape[0]
    return np.ascontiguousarray(a.reshape(n, 128, 8, 256).transpose(0, 3, 2, 1)).reshape(n * 256, 1024)


def kernel(x, mem, norm_g, w_in, gm_ln_g, gm_ln_b, gm_ws, gm_bs, pool_w, pool_scale,
           mem_norm_g, w_mem_kv, w_branch, w_out, final_norm_g):
    f = lambda a: np.ascontiguousarray(np.asarray(a, dtype=np.float32))
    x = f(x); mem = f(mem)
    B, S, _ = x.shape
    wsT_all = np.ascontiguousarray(np.transpose(f(gm_ws), (0, 1, 3, 2)))
    gm_bs_f = f(gm_bs).reshape(DEPTH, 512)
    def colv(a, nk):
        a = f(a)
        return np.ascontiguousarray(a.reshape(a.shape[0], nk, 128).transpose(2, 0, 1))
    common = dict(w_in=f(w_in), w_kv=f(w_mem_kv), w_br=f(w_branch), w_out=f(w_out), pool_w=f(pool_w), wsT=wsT_all,
                  ln_g=f(gm_ln_g), ln_b=f(gm_ln_b), gm_bs=gm_bs_f)
    colc = dict(norm_g=colv(norm_g, 8), mem_g=colv(mem_norm_g, 8), pscale=colv(pool_scale, 4))
    fin = np.ascontiguousarray(f(final_norm_g).reshape(8, 128).T)
    out = np.empty((B, S, D), np.float32)
    if MODE == "V4":
        nc = _get_prog(DEPTH, 2, False)
        in_maps = []
        for b in range(B):
            m = dict(common)
            m.update(colc)
            m.update(xT=_tile_x(x[b]), memT=np.ascontiguousarray(mem[b].T), fin_g=fin,
                     masks=_masks(1.0), rc=np.stack([_rc(True), _rc(False)]))
            in_maps.append(m)
        res = run_bass_kernel_spmd(nc, in_maps, core_ids=list(range(B)))
        for b in range(B):
            out[b] = _untile_x(np.asarray(res.results[b]["oT"]))
        return out
    nc = _get_prog(1, 1, True)
    xcur = [_tile_x(x[b]) for b in range(B)]
    zeros_prev = np.zeros((8, 128, 2048), np.float32)
    for l in range(DEPTH):
        in_maps = []
        for c in range(8):
            b, hf = c // 2, c % 2
            m = {k: np.ascontiguousarray(v[l:l + 1]) for k, v in common.items()}
            m.update({k: np.ascontiguousarray(v[:, l:l + 1]) for k, v in colc.items()})
            m.update(xT=np.ascontiguousarray(xcur[b][hf * 8:(hf + 1) * 8]),
                     xpT=(np.ascontiguousarray(xcur[b][0:8]) if hf == 1 else zeros_prev),
                     memT=np.ascontiguousarray(mem[b].T), fin_g=fin,
                     masks=_masks(float(hf)), rc=_rc(hf == 0)[None])
            in_maps.append(m)
        res = run_bass_kernel_spmd(nc, in_maps, core_ids=list(range(8)))
        if l < DEPTH - 1:
            xcur = [np.concatenate([np.asarray(res.results[2 * b]["xo"]), np.asarray(res.results[2 * b + 1]["xo"])], axis=0) for b in range(B)]
        else:
            for b in range(B):
                out[b] = _untile_x(np.concatenate([np.asarray(res.results[2 * b]["oT"]), np.asarray(res.results[2 * b + 1]["oT"])], axis=0))
    return out
```
